# Optimizing a Trainium2 kernel written in Bass

```python
import math
import jax, jax.numpy as jnp
from jax import lax
import numpy as np

D_MODEL = 1024
BATCH = 8
SEQ = 2048
DEPTH = 2
DEC_BATCH = 128
DEC_SEQ = 1
PAST_LEN = 16384
PAGE_SIZE = 128

N_MIXERS = 2
N_GDN = (DEPTH + 1) // 2
N_SSD = DEPTH // 2

CONV_W = 4
CHUNK = 64

GDN_HEADS = 8
GDN_DK = 128
GDN_DV = 128
GDN_QK = GDN_HEADS * GDN_DK
GDN_VD = GDN_HEADS * GDN_DV
GDN_CONV_DIM = 2 * GDN_QK + GDN_VD
GDN_IN = GDN_CONV_DIM + GDN_VD + 2 * GDN_HEADS

SSD_INNER = 2 * D_MODEL
SSD_HEADDIM = 64
SSD_HEADS = SSD_INNER // SSD_HEADDIM
SSD_GROUPS = 4
SSD_STATE = 128
SSD_HPG = SSD_HEADS // SSD_GROUPS
SSD_CONV_DIM = SSD_INNER + 2 * SSD_GROUPS * SSD_STATE
SSD_IN = SSD_INNER + SSD_CONV_DIM + SSD_HEADS

D_FF = 2816
FFN_CONV_W = 3

DN_ALPHA = (2 * DEPTH) ** 0.25
DN_BETA = (8 * DEPTH) ** -0.25
LN_EPS = 1e-5
RMS_EPS = 1e-6

kernel_name = "hybrid_gdn_mamba2_convffn_deepnorm_step"


def _layer_norm(x, g, b):
    xf = x.astype(jnp.float32)
    mu = jnp.mean(xf, -1, keepdims=True)
    var = jnp.mean(jnp.square(xf - mu), -1, keepdims=True)
    return ((xf - mu) * lax.rsqrt(var + LN_EPS) * g + b).astype(x.dtype)


def _rmsnorm(t, w):
    return t * lax.rsqrt(jnp.mean(jnp.square(t), -1, keepdims=True) + RMS_EPS) * w


def _l2norm(t):
    return t * lax.rsqrt(jnp.sum(jnp.square(t), -1, keepdims=True) + 1e-6)


def _causal_dwconv(x, buf, w, b):
    xx = jnp.concatenate([buf.astype(x.dtype), x], axis=1)
    seq_len, width = x.shape[1], w.shape[0]
    y = b + xx[:, 0:seq_len] * w[0]
    for k in range(1, width):
        y = y + xx[:, k:k + seq_len] * w[k]
    return y, xx[:, -(width - 1):]


def _to_chunks(a, csz):
    seq_len = a.shape[1]
    n = -(-seq_len // csz)
    a = jnp.pad(a, [(0, 0), (0, n * csz - seq_len)] + [(0, 0)] * (a.ndim - 2))
    return a.reshape(a.shape[0], n, csz, *a.shape[2:])


def _gated_delta_rule(q, k, v, beta, g, s0):
    seq_len = q.shape[1]
    csz = min(CHUNK, seq_len)
    qc, kc, vc, bc, gc = (_to_chunks(t, csz) for t in (q, k, v, beta, g))
    gcum = jnp.cumsum(gc, axis=2)
    gcum_h = jnp.swapaxes(gcum, 2, 3)
    incl = jnp.tril(jnp.ones((csz, csz), dtype=bool))
    strict = jnp.tril(jnp.ones((csz, csz), dtype=bool), -1)
    gamma = jnp.exp(jnp.where(incl, gcum_h[..., :, None] - gcum_h[..., None, :], -jnp.inf))
    kb = kc * bc[..., None]
    kk = jnp.einsum('bnihd,bnjhd->bnhij', kb, kc) * gamma
    eye = jnp.eye(csz, dtype=kk.dtype)
    a_mat = eye + jnp.where(strict, kk, 0.0)
    t_mat = lax.linalg.triangular_solve(a_mat, jnp.broadcast_to(eye, a_mat.shape),
                                        left_side=True, lower=True)
    u = jnp.einsum('bnhij,bnjhd->bnihd', t_mat, vc * bc[..., None])
    w = jnp.einsum('bnhij,bnjhd->bnihd', t_mat, kb * jnp.exp(gcum)[..., None])
    qk = jnp.einsum('bnihd,bnjhd->bnhij', qc, kc) * gamma
    q_dec = qc * jnp.exp(gcum)[..., None]
    g_last = gcum[:, :, -1]
    k_dec = kc * jnp.exp(g_last[:, :, None] - gcum)[..., None]

    def step(s, inp):
        u_n, w_n, qk_n, qd_n, kd_n, gl_n = inp
        v_new = u_n - jnp.einsum('bihk,bhkv->bihv', w_n, s)
        o_n = jnp.einsum('bihk,bhkv->bihv', qd_n, s) + jnp.einsum('bhij,bjhv->bihv', qk_n, v_new)
        s = s * jnp.exp(gl_n)[..., None, None] + jnp.einsum('bihk,bihv->bhkv', kd_n, v_new)
        return s, o_n

    xs = tuple(jnp.moveaxis(t, 1, 0) for t in (u, w, qk, q_dec, k_dec, g_last))
    s_fin, o = lax.scan(step, s0, xs)
    o = jnp.moveaxis(o, 0, 1)
    o = o.reshape(o.shape[0], -1, GDN_HEADS, GDN_DV)[:, :seq_len]
    return o, s_fin


def _ssd_scan(x, dt, a, bm, cm, s0):
    bsz, seq_len = x.shape[0], x.shape[1]
    csz = min(CHUNK, seq_len)
    xc = _to_chunks(x * dt[..., None], csz)
    ac = _to_chunks(dt * a, csz)
    bc = _to_chunks(bm, csz)
    cc = _to_chunks(cm, csz)
    n = xc.shape[1]
    acum = jnp.cumsum(ac, axis=2)
    acum_h = jnp.swapaxes(acum, 2, 3)
    incl = jnp.tril(jnp.ones((csz, csz), dtype=bool))
    seg = jnp.exp(jnp.where(incl, acum_h[..., :, None] - acum_h[..., None, :], -jnp.inf))
    cb = jnp.einsum('bnigs,bnjgs->bngij', cc, bc)
    scores = seg.reshape(bsz, n, SSD_GROUPS, SSD_HPG, csz, csz) * cb[:, :, :, None]
    xg = xc.reshape(bsz, n, csz, SSD_GROUPS, SSD_HPG, SSD_HEADDIM)
    y_diag = jnp.einsum('bnghij,bnjghp->bnighp', scores, xg).reshape(bsz, n, csz, SSD_HEADS, SSD_HEADDIM)
    a_last = acum[:, :, -1]
    decay_to_end = jnp.exp(a_last[:, :, None] - acum)
    xdec = (xc * decay_to_end[..., None]).reshape(bsz, n, csz, SSD_GROUPS, SSD_HPG, SSD_HEADDIM)
    chunk_states = jnp.einsum('bnjgs,bnjghp->bnghps', bc, xdec).reshape(
        bsz, n, SSD_HEADS, SSD_HEADDIM, SSD_STATE)

    def step(s, inp):
        c_n, acum_n, cs_n, al_n = inp
        sg = s.reshape(bsz, SSD_GROUPS, SSD_HPG, SSD_HEADDIM, SSD_STATE)
        y_off = jnp.einsum('bigs,bghps->bighp', c_n, sg).reshape(
            bsz, csz, SSD_HEADS, SSD_HEADDIM) * jnp.exp(acum_n)[..., None]
        s = s * jnp.exp(al_n)[..., None, None] + cs_n
        return s, y_off

    xs = tuple(jnp.moveaxis(t, 1, 0) for t in (cc, acum, chunk_states, a_last))
    s_fin, y_off = lax.scan(step, s0, xs)
    y = y_diag + jnp.moveaxis(y_off, 0, 1)
    y = y.reshape(bsz, -1, SSD_HEADS, SSD_HEADDIM)[:, :seq_len]
    return y, s_fin


def _gdn_mixer(x, conv_buf, s0, w_in, conv_w, conv_b, a_log, dt_bias, norm_w, w_out):
    bsz, seq_len, _ = x.shape
    proj = x @ w_in
    qkv = proj[..., :GDN_CONV_DIM]
    z = proj[..., GDN_CONV_DIM:GDN_CONV_DIM + GDN_VD]
    b_raw = proj[..., GDN_CONV_DIM + GDN_VD:GDN_CONV_DIM + GDN_VD + GDN_HEADS]
    a_raw = proj[..., GDN_CONV_DIM + GDN_VD + GDN_HEADS:]
    qkv, new_buf = _causal_dwconv(qkv, conv_buf, conv_w, conv_b)
    qkv = jax.nn.silu(qkv).astype(jnp.float32)
    q = _l2norm(qkv[..., :GDN_QK].reshape(bsz, seq_len, GDN_HEADS, GDN_DK)) * (GDN_DK ** -0.5)
    k = _l2norm(qkv[..., GDN_QK:2 * GDN_QK].reshape(bsz, seq_len, GDN_HEADS, GDN_DK))
    v = qkv[..., 2 * GDN_QK:].reshape(bsz, seq_len, GDN_HEADS, GDN_DV)
    beta = jax.nn.sigmoid(b_raw.astype(jnp.float32))
    g = -jnp.exp(a_log.astype(jnp.float32)) * jax.nn.softplus(a_raw.astype(jnp.float32) + dt_bias)
    o, s_fin = _gated_delta_rule(q, k, v, beta, g, s0.astype(jnp.float32))
    o = _rmsnorm(o, norm_w) * jax.nn.silu(z.astype(jnp.float32).reshape(bsz, seq_len, GDN_HEADS, GDN_DV))
    out = o.reshape(bsz, seq_len, GDN_VD).astype(x.dtype) @ w_out
    return out, new_buf, s_fin.astype(s0.dtype)


def _ssd_mixer(x, conv_buf, s0, w_in, conv_w, conv_b, a_log, dt_bias, d_skip, norm_w, w_out):
    bsz, seq_len, _ = x.shape
    gn = SSD_GROUPS * SSD_STATE
    proj = x @ w_in
    z = proj[..., :SSD_INNER]
    xbc = proj[..., SSD_INNER:SSD_INNER + SSD_CONV_DIM]
    dt_raw = proj[..., SSD_INNER + SSD_CONV_DIM:]
    xbc, new_buf = _causal_dwconv(xbc, conv_buf, conv_w, conv_b)
    xbc = jax.nn.silu(xbc).astype(jnp.float32)
    xs = xbc[..., :SSD_INNER].reshape(bsz, seq_len, SSD_HEADS, SSD_HEADDIM)
    bm = xbc[..., SSD_INNER:SSD_INNER + gn].reshape(bsz, seq_len, SSD_GROUPS, SSD_STATE)
    cm = xbc[..., SSD_INNER + gn:].reshape(bsz, seq_len, SSD_GROUPS, SSD_STATE)
    dt = jax.nn.softplus(dt_raw.astype(jnp.float32) + dt_bias)
    a = -jnp.exp(a_log.astype(jnp.float32))
    y, s_fin = _ssd_scan(xs, dt, a, bm, cm, s0.astype(jnp.float32))
    y = y + d_skip[:, None] * xs
    y = y.reshape(bsz, seq_len, SSD_INNER) * jax.nn.silu(z.astype(jnp.float32))
    y = _rmsnorm(y.reshape(bsz, seq_len, SSD_GROUPS, SSD_INNER // SSD_GROUPS),
                 norm_w.reshape(SSD_GROUPS, SSD_INNER // SSD_GROUPS)).reshape(bsz, seq_len, SSD_INNER)
    out = y.astype(x.dtype) @ w_out
    return out, new_buf, s_fin.astype(s0.dtype)


def _conv_ffn(x, buf, w_up, conv_w, conv_b, w_down):
    gv = x @ w_up
    gate, val = gv[..., :D_FF], gv[..., D_FF:]
    gate, new_buf = _causal_dwconv(gate, buf, conv_w, conv_b)
    return (jax.nn.silu(gate) * val) @ w_down, new_buf


def _trunk(x, gdn_conv, gdn_state, ssd_conv, ssd_state, ffn_conv, weights):
    (gdn_w_in, gdn_conv_w, gdn_conv_b, gdn_a_log, gdn_dt_bias, gdn_norm_w, gdn_w_out,
     ssd_w_in, ssd_conv_w, ssd_conv_b, ssd_a_log, ssd_dt_bias, ssd_d, ssd_norm_w, ssd_w_out,
     ffn_w_up, ffn_conv_w, ffn_conv_b, ffn_w_down, ln1_g, ln1_b, ln2_g, ln2_b) = weights
    gdn_conv_out, gdn_state_out, ssd_conv_out, ssd_state_out, ffn_conv_out = [], [], [], [], []
    for i in range(DEPTH):
        j = i // N_MIXERS
        if i % N_MIXERS == 0:
            h, nb, ns = _gdn_mixer(x, gdn_conv[j], gdn_state[j], gdn_w_in[j], gdn_conv_w[j], gdn_conv_b[j],
                                   gdn_a_log[j], gdn_dt_bias[j], gdn_norm_w[j], gdn_w_out[j])
            gdn_conv_out.append(nb)
            gdn_state_out.append(ns)
        else:
            h, nb, ns = _ssd_mixer(x, ssd_conv[j], ssd_state[j], ssd_w_in[j], ssd_conv_w[j], ssd_conv_b[j],
                                   ssd_a_log[j], ssd_dt_bias[j], ssd_d[j], ssd_norm_w[j], ssd_w_out[j])
            ssd_conv_out.append(nb)
            ssd_state_out.append(ns)
        x = _layer_norm(DN_ALPHA * x + h, ln1_g[i], ln1_b[i])
        f, fb = _conv_ffn(x, ffn_conv[i], ffn_w_up[i], ffn_conv_w[i], ffn_conv_b[i], ffn_w_down[i])
        ffn_conv_out.append(fb)
        x = _layer_norm(DN_ALPHA * x + f, ln2_g[i], ln2_b[i])
    return (x, jnp.stack(gdn_conv_out), jnp.stack(gdn_state_out), jnp.stack(ssd_conv_out),
            jnp.stack(ssd_state_out), jnp.stack(ffn_conv_out))


def setup_inputs(seed: int = 0) -> dict:
    key = jax.random.key(seed)
    ks = iter(jax.random.split(key, 64))

    def nrm(shape, scale):
        return jax.random.normal(next(ks), shape, jnp.float32) * scale

    def dt_bias(shape):
        dt = jnp.exp(jax.random.uniform(next(ks), shape, jnp.float32, math.log(1e-3), math.log(1e-1)))
        return dt + jnp.log(-jnp.expm1(-dt))

    def a_log(shape):
        return jnp.log(jax.random.uniform(next(ks), shape, jnp.float32, 1.0, 16.0))

    return {
        "x_prompt": nrm((BATCH, SEQ, D_MODEL), 1.0),
        "x_sample": nrm((DEC_BATCH, DEC_SEQ, D_MODEL), 1.0),
        "cache_gdn_conv": nrm((N_GDN, DEC_BATCH, CONV_W - 1, GDN_CONV_DIM), 1.0),
        "state_gdn": nrm((N_GDN, DEC_BATCH, GDN_HEADS, GDN_DK, GDN_DV), 0.5),
        "cache_ssd_conv": nrm((N_SSD, DEC_BATCH, CONV_W - 1, SSD_CONV_DIM), 1.0),
        "state_ssd": nrm((N_SSD, DEC_BATCH, SSD_HEADS, SSD_HEADDIM, SSD_STATE), 0.1),
        "cache_ffn_conv": nrm((DEPTH, DEC_BATCH, FFN_CONV_W - 1, D_FF), 1.0),
        "gdn_w_in": nrm((N_GDN, D_MODEL, GDN_IN), D_MODEL ** -0.5),
        "gdn_conv_w": nrm((N_GDN, CONV_W, GDN_CONV_DIM), CONV_W ** -0.5),
        "gdn_conv_b": nrm((N_GDN, GDN_CONV_DIM), 0.02),
        "gdn_a_log": a_log((N_GDN, GDN_HEADS)),
        "gdn_dt_bias": dt_bias((N_GDN, GDN_HEADS)),
        "gdn_norm_w": 1.0 + nrm((N_GDN, GDN_DV), 0.02),
        "gdn_w_out": nrm((N_GDN, GDN_VD, D_MODEL), DN_BETA * GDN_VD ** -0.5),
        "ssd_w_in": nrm((N_SSD, D_MODEL, SSD_IN), D_MODEL ** -0.5),
        "ssd_conv_w": nrm((N_SSD, CONV_W, SSD_CONV_DIM), CONV_W ** -0.5),
        "ssd_conv_b": nrm((N_SSD, SSD_CONV_DIM), 0.02),
        "ssd_a_log": a_log((N_SSD, SSD_HEADS)),
        "ssd_dt_bias": dt_bias((N_SSD, SSD_HEADS)),
        "ssd_d": 1.0 + nrm((N_SSD, SSD_HEADS), 0.02),
        "ssd_norm_w": 1.0 + nrm((N_SSD, SSD_INNER), 0.02),
        "ssd_w_out": nrm((N_SSD, SSD_INNER, D_MODEL), DN_BETA * SSD_INNER ** -0.5),
        "ffn_w_up": nrm((DEPTH, D_MODEL, 2 * D_FF), D_MODEL ** -0.5),
        "ffn_conv_w": nrm((DEPTH, FFN_CONV_W, D_FF), FFN_CONV_W ** -0.5),
        "ffn_conv_b": nrm((DEPTH, D_FF), 0.02),
        "ffn_w_down": nrm((DEPTH, D_FF, D_MODEL), DN_BETA * D_FF ** -0.5),
        "ln1_g": 1.0 + nrm((DEPTH, D_MODEL), 0.02),
        "ln1_b": nrm((DEPTH, D_MODEL), 0.02),
        "ln2_g": 1.0 + nrm((DEPTH, D_MODEL), 0.02),
        "ln2_b": nrm((DEPTH, D_MODEL), 0.02),
    }


def reference(x_prompt, x_sample, cache_gdn_conv, state_gdn, cache_ssd_conv, state_ssd, cache_ffn_conv,
              gdn_w_in, gdn_conv_w, gdn_conv_b, gdn_a_log, gdn_dt_bias, gdn_norm_w, gdn_w_out,
              ssd_w_in, ssd_conv_w, ssd_conv_b, ssd_a_log, ssd_dt_bias, ssd_d, ssd_norm_w, ssd_w_out,
              ffn_w_up, ffn_conv_w, ffn_conv_b, ffn_w_down, ln1_g, ln1_b, ln2_g, ln2_b):
    weights = (gdn_w_in, gdn_conv_w, gdn_conv_b, gdn_a_log, gdn_dt_bias, gdn_norm_w, gdn_w_out,
               ssd_w_in, ssd_conv_w, ssd_conv_b, ssd_a_log, ssd_dt_bias, ssd_d, ssd_norm_w, ssd_w_out,
               ffn_w_up, ffn_conv_w, ffn_conv_b, ffn_w_down, ln1_g, ln1_b, ln2_g, ln2_b)
    bp = x_prompt.shape[0]
    z_gdn_conv = jnp.zeros((cache_gdn_conv.shape[0], bp) + cache_gdn_conv.shape[2:], cache_gdn_conv.dtype)
    z_gdn_state = jnp.zeros((state_gdn.shape[0], bp) + state_gdn.shape[2:], state_gdn.dtype)
    z_ssd_conv = jnp.zeros((cache_ssd_conv.shape[0], bp) + cache_ssd_conv.shape[2:], cache_ssd_conv.dtype)
    z_ssd_state = jnp.zeros((state_ssd.shape[0], bp) + state_ssd.shape[2:], state_ssd.dtype)
    z_ffn_conv = jnp.zeros((cache_ffn_conv.shape[0], bp) + cache_ffn_conv.shape[2:], cache_ffn_conv.dtype)
    y_p, gcp, gsp, scp, ssp, fcp = _trunk(x_prompt, z_gdn_conv, z_gdn_state, z_ssd_conv, z_ssd_state,
                                          z_ffn_conv, weights)
    y_s, gcs, gss, scs, sss, fcs = _trunk(x_sample, cache_gdn_conv, state_gdn, cache_ssd_conv, state_ssd,
                                          cache_ffn_conv, weights)
    return (y_p, y_s, gcp, gcs, gsp, gss, scp, scs, ssp, sss, fcp, fcs)
```

```python
import contextlib
import numpy as np
import concourse.bass as bass
import concourse.mybir as mybir
from concourse.bass_utils import run_bass_kernel_spmd

F32 = mybir.dt.float32
F32R = mybir.dt.float32r
BF16 = mybir.dt.bfloat16


def RR(ap):
    return ap.bitcast(F32R)
AF = mybir.ActivationFunctionType
ALU = mybir.AluOpType

NCORES = 8
ATTACH_WAIT = True
L = 2048
NT = 16
NS = 16
D = 1024
BIG = 30000.0
ALPHA = 4.0 ** 0.25
GDN_IN = 4112
SSD_IN = 5152
DFF = 2816
NFC = 22


class _Op:
    __slots__ = ("eng", "fn", "deps", "is_dma", "sem", "semval", "sig", "sigidx", "prev_sem_wait")

    def __init__(self, eng, fn, is_dma):
        self.eng = eng
        self.fn = fn
        self.is_dma = is_dma
        self.deps = []
        self.sem = None
        self.semval = 0
        self.sig = False
        self.sigidx = 0
        self.prev_sem_wait = None


class Prog:
    ENGS = ("pe", "act", "dve", "pool", "sp")

    def __init__(self, nc):
        self.nc = nc
        self.ops = []
        self.last_writer = {}
        self.readers = {}
        self.n_dma_sems = {"sp": 44, "pool": 36, "act": 8}
        self.dma_count = {q: 0 for q in self.n_dma_sems}
        self.dma_ops = {q: [] for q in self.n_dma_sems}
        self.bar_list = []

    @staticmethod
    def _key(r):
        if isinstance(r, tuple):
            return tuple(Prog._key(x) for x in r)
        if isinstance(r, (str, int)):
            return r
        return r.name

    def _track(self, op, reads, writes):
        reads = [self._key(r) for r in reads]
        writes = [self._key(r) for r in writes]
        deps = set()
        for r in reads + writes:
            w = self.last_writer.get(r)
            if w is not None:
                deps.add(w)
        for w in writes:
            rd = self.readers.get(w)
            if rd:
                deps.update(rd.values())
        deps.discard(op)
        op.deps = list(deps)
        for r in reads:
            d = self.readers.setdefault(r, {})
            d[("dma", id(op)) if op.is_dma else op.eng] = op
        for w in writes:
            self.last_writer[w] = op
            self.readers[w] = {}

    def op(self, eng, fn, reads=(), writes=()):
        o = _Op(eng, fn, False)
        self._track(o, reads, writes)
        self.ops.append(o)
        return o

    def dma(self, q, out, in_, reads=(), writes=(), **kw):
        o = _Op(q, lambda e: e.dma_start(out=out, in_=in_, **kw), True)
        n = self.dma_count[q]
        K = self.n_dma_sems[q]
        o.sem = (q, n % K)
        o.semval = 16 * (n // K + 1)
        if n >= K:
            o.prev_sem_wait = (o.sem, o.semval - 16)
        self.dma_count[q] = n + 1
        self._track(o, reads, writes)
        self.ops.append(o)
        self.dma_ops[q].append(o)
        return o

    def barrier(self):
        last = {}
        for o in self.ops:
            if o.is_dma:
                last[o.sem] = o
            else:
                last[o.eng] = o
        self.bar_list.append((len(self.ops), list(last.values())))

    def emit(self):
        nc = self.nc
        ops = self.ops
        for (pos, deps) in self.bar_list:
            seen = set()
            for o in ops[pos:]:
                if o.eng not in seen:
                    seen.add(o.eng)
                    o.deps = list(set(o.deps) | set(deps))
                if len(seen) == len(self.ENGS):
                    break
        for o in ops:
            nd = []
            for d in o.deps:
                if (not d.is_dma) and d.eng == o.eng and (not o.is_dma) and o.eng == "pe":
                    continue
                nd.append(d)
            o.deps = nd
            for d in nd:
                if not d.is_dma:
                    d.sig = True
        cnt = {e: 0 for e in self.ENGS}
        for o in ops:
            if not o.is_dma and o.sig:
                cnt[o.eng] += 1
                o.sigidx = cnt[o.eng]
        with contextlib.ExitStack() as st:
            esem = {e: st.enter_context(nc.semaphore("s_" + e)) for e in ("pe", "act", "dve", "pool")}
            dsem = {}
            for q, K in self.n_dma_sems.items():
                for i in range(K):
                    dsem[(q, i)] = st.enter_context(nc.semaphore("d_%s_%d" % (q, i)))
            block = st.enter_context(nc.Block())
            by_eng = {e: [o for o in ops if o.eng == e] for e in self.ENGS}
            final_dma = {}
            for q in self.dma_ops:
                for o in self.dma_ops[q]:
                    final_dma[o.sem] = max(final_dma.get(o.sem, 0), o.semval)

            def run(engname, e):
                seen = {}
                for o in by_eng[engname]:
                    waits = {}
                    for d in o.deps:
                        if d.is_dma:
                            kk, v = d.sem, d.semval
                        else:
                            kk, v = d.eng, d.sigidx
                        if seen.get(kk, 0) >= v:
                            continue
                        if waits.get(kk, 0) < v:
                            waits[kk] = v
                    if o.prev_sem_wait is not None:
                        kk, v = o.prev_sem_wait
                        if seen.get(kk, 0) < v and waits.get(kk, 0) < v:
                            waits[kk] = v
                    wl = list(waits.items())
                    attach = None
                    if wl and ATTACH_WAIT:
                        attach = wl.pop()
                    for kk, v in wl:
                        e.wait_ge(dsem[kk] if isinstance(kk, tuple) else esem[kk], v)
                        seen[kk] = v
                    ins = o.fn(e)
                    if attach is not None:
                        kk, v = attach
                        ins._wait_ge(dsem[kk] if isinstance(kk, tuple) else esem[kk], v)
                        seen[kk] = v
                    if o.is_dma:
                        ins.then_inc(dsem[o.sem], 16)
                    elif o.sig:
                        ins.then_inc(esem[o.eng], 1)
                if engname == "sp":
                    for kk, v in final_dma.items():
                        if seen.get(kk, 0) < v:
                            e.wait_ge(dsem[kk], v)

            block.tensor(lambda e: run("pe", e))
            block.scalar(lambda e: run("act", e))
            block.vector(lambda e: run("dve", e))
            block.gpsimd(lambda e: run("pool", e))
            block.sync(lambda e: run("sp", e))


class Arena:
    def __init__(self, nc, lo=20480, hi=229376):
        self.nc = nc
        self.p = lo
        self.hi = hi
        self.cnt = 0
        self.peak = lo

    def alloc(self, name, shape, dt=F32):
        nb = int(np.prod(shape[1:])) * (4 if dt == F32 else 2)
        nb = (nb + 63) // 64 * 64
        self.cnt += 1
        t = self.nc.alloc_sbuf_tensor_at("%s_%d" % (name, self.cnt), list(shape), dt, offset=self.p)
        self.p += nb
        self.peak = max(self.peak, self.p)
        assert self.p <= self.hi, "SBUF overflow at %s: %d" % (name, self.p)
        return t

    def mark(self):
        return self.p

    def reset(self, m):
        self.p = m


def _nm(x):
    return x.name


def PK(a):
    nm = a.name
    if not nm.startswith("ps"):
        return [nm]
    if not hasattr(a, "ap"):
        return [(nm, 0), (nm, 1)]
    dims = a.ap
    row = dims[0][0] if len(dims) > 1 else 1024
    fo = a.offset % row if row else 0
    ext = 1
    for st_, cn in dims[1:]:
        ext += abs(st_) * (cn - 1)
    return [(nm, b) for b in range(fo // 512, (fo + ext - 1) // 512 + 1)]


class KB:
    def __init__(self, nc, P):
        self.nc = nc
        self.P = P
        self.ident = None
        self.r32 = set()

    def reg32(self, *ts):
        for t in ts:
            self.r32.add(t.name)

    def _o(self, out):
        if out.name in self.r32 and out.dtype == F32:
            return out.bitcast(F32R)
        return out

    @staticmethod
    def _rw(outs, ins, r, w):
        if r is None:
            reads = [kk for a in ins if a is not None and not isinstance(a, (int, float)) for kk in PK(a)]
        else:
            reads = [kk for x in r for kk in (x if isinstance(x, list) else [x])]
        if w is None:
            writes = [kk for a in outs for kk in PK(a)]
        else:
            writes = [kk for x in w for kk in (x if isinstance(x, list) else [x])]
        def _exp(lst):
            o = []
            for x in lst:
                if isinstance(x, str) and x.startswith("ps"):
                    o += [(x, 0), (x, 1)]
                else:
                    o.append(x)
            return o
        reads, writes = _exp(reads), _exp(writes)
        for x in reads:
            if isinstance(x, tuple) and isinstance(x[0], str) and x[0].startswith("ps") and len(x) == 2 \
                    and isinstance(x[1], int) and x not in writes:
                writes.append(x)
        return reads, writes

    def mm(self, out, lhsT, rhs, start=True, stop=True, r=None, w=None):
        reads, writes = self._rw([out], [lhsT, rhs], r, w)
        if lhsT.name in self.r32 and rhs.name in self.r32 and lhsT.dtype == F32 and rhs.dtype == F32:
            lhsT, rhs = lhsT.bitcast(F32R), rhs.bitcast(F32R)
        self.P.op("pe", lambda e: e.matmul(out, lhsT, rhs, start=start, stop=stop), reads, writes)

    def tr(self, out, in_, r=None, w=None):
        n = in_.shape[0]
        idn = self.ident[0:n, 0:n]
        reads, writes = self._rw([out], [in_, idn], r, w)
        self.P.op("pe", lambda e: e.transpose(out=out, in_=in_, identity=idn), reads, writes)

    def act(self, out, in_, func, bias=None, scale=None, r=None, w=None, eng="act"):
        kw = {}
        if bias is not None:
            kw["bias"] = bias
        if scale is not None:
            kw["scale"] = scale
        reads, writes = self._rw([out], [in_, bias, scale], r, w)
        out = self._o(out)
        self.P.op("act", lambda e: e.activation(out=out, in_=in_, func=func, **kw), reads, writes)

    def tt(self, out, a, b, op, eng="dve", r=None, w=None):
        reads, writes = self._rw([out], [a, b], r, w)
        out = self._o(out)
        self.P.op(eng, lambda e: e.tensor_tensor(out=out, in0=a, in1=b, op=op), reads, writes)

    def ts(self, out, a, s1, s2, op0, op1=None, eng="dve", r=None, w=None):
        reads, writes = self._rw([out], [a, s1, s2], r, w)
        out = self._o(out)
        if op1 is None:
            self.P.op(eng, lambda e: e.tensor_scalar(out=out, in0=a, scalar1=s1, scalar2=None, op0=op0), reads, writes)
        else:
            self.P.op(eng, lambda e: e.tensor_scalar(out=out, in0=a, scalar1=s1, scalar2=s2, op0=op0, op1=op1), reads, writes)

    def stt(self, out, a, scalar, b, op0, op1, eng="dve", r=None, w=None):
        reads, writes = self._rw([out], [a, scalar, b], r, w)
        out = self._o(out)
        self.P.op(eng, lambda e: e.scalar_tensor_tensor(out=out, in0=a, scalar=scalar, in1=b, op0=op0, op1=op1), reads, writes)

    def cp(self, out, in_, eng="dve", r=None, w=None):
        reads, writes = self._rw([out], [in_], r, w)
        out = self._o(out)
        if eng == "act":
            self.P.op("act", lambda e: e.activation(out=out, in_=in_, func=AF.Copy), reads, writes)
        else:
            self.P.op(eng, lambda e: e.tensor_copy(out=out, in_=in_), reads, writes)

    def memset(self, out, val, eng="dve", w=None):
        reads, writes = self._rw([out], [], None, w)
        self.P.op(eng, lambda e: e.memset(out, val), reads, writes)

    def red(self, out, in_, op=ALU.add, eng="dve", r=None, w=None):
        reads, writes = self._rw([out], [in_], r, w)
        self.P.op(eng, lambda e: e.tensor_reduce(out=out, in_=in_, axis=mybir.AxisListType.X, op=op), reads, writes)

    def dma(self, q, out, in_, r=None, w=None, **kw):
        reads, writes = self._rw([out], [in_], r, w)
        self.P.dma(q, out, in_, reads, writes, **kw)


def bc(ap, shape):
    return ap.to_broadcast(list(shape))


class SlotRot:
    width = 512

    def __init__(self, views):
        self.t = list(views)
        self.i = 0

    def next(self):
        t = self.t[self.i % len(self.t)]
        self.i += 1
        return t


class PsumRot:
    width = 1024

    def __init__(self, nc, n=4):
        self.t = [nc.alloc_psum_tensor("ps%d" % i, [128, 1024], F32) for i in range(n)]
        self.i = 0

    def next(self):
        t = self.t[self.i % len(self.t)]
        self.i += 1
        return t


class Pool4:
    def __init__(self, A, n, name="w", width=1024):
        self.free_ = [A.alloc("%s%d" % (name, i), [128, width]) for i in range(n)]

    def get(self):
        assert self.free_, "scratch pool exhausted"
        return self.free_.pop(0)

    def free(self, *ts):
        for t in ts:
            self.free_.append(t)


def load_weight_bf16(k, dst, src, nk, ncols, colblk=2048):
    c0 = 0
    while c0 < ncols:
        c1 = min(ncols, c0 + colblk)
        for kc in range(nk):
            k.dma("pool", dst[:, kc, c0:c1], src[kc * 128:(kc + 1) * 128, c0:c1],
                  w=[(dst.name, kc, c0 // colblk)])
        c0 = c1


def wres(dst, nk, ncols, colblk=2048):
    out = []
    for kc in range(nk):
        for cb in range((ncols + colblk - 1) // colblk):
            out.append((dst.name, kc, cb))
    return out


def to_feature_major(k, PS, dst3, src_tok, n, nchunks, evac="act", wkey=None):
    slot = 128 if n > 16 else 16
    c = 0
    while c < nchunks:
        g = min(nchunks - c, PS.width // slot)
        ps = PS.next()
        for i in range(g):
            k.tr(ps[:, i * slot:i * slot + n], src_tok[0:n, (c + i) * 128:(c + i + 1) * 128])
        k.cp(dst3[:, c:c + g, :], ps[:, 0:g * slot].rearrange("p (g s) -> p g s", s=slot)[:, :, 0:n], eng=evac,
             w=(None if wkey is None else [x for cc in range(c, c + g) for x in wkey(cc)]))
        c += g


def to_token_major(k, PS, dst_tok, src3, n, nchunks, evac="act", rkey=None):
    c = 0
    while c < nchunks:
        g = min(nchunks - c, PS.width // 128)
        ps = PS.next()
        for i in range(g):
            k.tr(ps[0:n, i * 128:(i + 1) * 128], src3[:, c + i, :],
                 r=(None if rkey is None else rkey(c + i) + [k.ident.name]))
        k.cp(dst_tok[0:n, c * 128:(c + g) * 128], ps[0:n, 0:g * 128], eng=evac)
        c += g


def layer_norm_out(k, A, res, n, g_bc, b_bc, out_tile, stats, mv, rstd):
    for i in range(2):
        k.P.op("dve", lambda e, i=i: e.bn_stats(out=stats[0:n, i, :], in_=res[0:n, i * 512:(i + 1) * 512]),
               [res.name], [(stats.name, i)])
    k.P.op("dve", lambda e: e.bn_aggr(out=mv[0:n, :], in_=stats[0:n, :, :]),
           [(stats.name, 0), (stats.name, 1)], [mv.name])
    k.act(rstd[0:n, :], mv[0:n, 1:2], AF.Ln, bias=k.eps5[0:n, :], scale=1.0)
    k.act(rstd[0:n, :], rstd[0:n, :], AF.Exp, scale=-0.5)
    k.ts(res[0:n, 0:1024], res[0:n, 0:1024], mv[0:n, 0:1], rstd[0:n, 0:1], ALU.subtract, ALU.mult)
    k.tt(res[0:n, 0:1024], res[0:n, 0:1024], g_bc[0:n, :], ALU.mult)
    k.tt(out_tile[0:n, 0:1024], res[0:n, 0:1024], b_bc[0:n, :], ALU.add)


def conv_taps(k, out3, views, wb3, nch, res_out, res_in, nk):
    for c in range(nch):
        k.act(out3[:, c, :], views[0][:, c, :], AF.Identity, bias=wb3[:, c, nk:nk + 1], scale=wb3[:, c, 0:1],
              r=[(res_in, c), wb3.name], w=[(res_out, c)])
    for kk in range(1, nk):
        for c in range(nch):
            k.stt(out3[:, c, :], views[kk][:, c, :], wb3[:, c, kk:kk + 1], out3[:, c, :], ALU.mult, ALU.add,
                  r=[(res_in, c), (res_out, c), wb3.name], w=[(res_out, c)])


def load_small_params(k, PS, A, conv_w, conv_b, nk, nch, name):
    wb3 = A.alloc(name + "_wb3", [128, nch, nk + 1])
    m = A.mark()
    tok = A.alloc(name + "_tok", [nk + 1, nch * 128])
    k.dma("sp", tok[0:nk, :], conv_w)
    k.dma("sp", tok[nk:nk + 1, :], conv_b)
    to_feature_major(k, PS, wb3, tok, nk + 1, nch, evac="dve")
    k.P.barrier()
    A.reset(m)
    return wb3


def proj_feature_major(k, PS, W, wname, XT, ntok, fc_list, evac_fn, colblk=2048):
    slot = 128 if ntok > 16 else 16
    per = PS.width // slot if slot == 128 else 8
    i = 0
    while i < len(fc_list):
        grp = fc_list[i:i + per]
        ps = PS.next()
        for j, fc in enumerate(grp):
            for kc in range(8):
                k.mm(ps[:, j * slot:j * slot + ntok], W[:, kc, fc * 128:(fc + 1) * 128], XT[:, kc, 0:ntok],
                     start=(kc == 0), stop=(kc == 7),
                     r=[(wname, kc, (fc * 128) // colblk), XT.name], w=PK(ps[:, j * slot:j * slot + ntok]))
        view = ps[:, 0:len(grp) * slot].rearrange("p (g s) -> p g s", s=slot)[:, :, 0:ntok]
        evac_fn(i, grp, ps, view)
        i += per


def phase_gdn(C):
    nc, k, P, A, PS, dr = C.nc, C.k, C.P, C.A, C.PS, C.dr
    cst = C.cst
    IDENT, ONES, U, UBLK, MASKL, MASKUS, MASKUI, CH0, CH1 = (cst[n] for n in
                                                             ("IDENT", "ONES", "U", "UBLK", "MASKL", "MASKUS", "MASKUI", "CH0", "CH1"))
    m_phase = A.mark()
    WIN = A.alloc("gwin", [128, 8, GDN_IN], BF16)
    WOUT = A.alloc("gwout", [128, 8, 1024], BF16)
    NORMW = A.alloc("gnormw", [128, 1])
    k.dma("sp", NORMW[:], dr["gdn_norm_w"].rearrange("o p -> p o"))
    NWBC = A.alloc("gnwbc", [128, 128])
    k.dma("sp", NWBC[:], dr["gdn_norm_w"].partition_broadcast(128))
    NEGA = A.alloc("gnega", [128, 8])
    DTB = A.alloc("gdtb", [128, 8])
    k.dma("sp", NEGA[:], dr["gdn_a_log"].partition_broadcast(128))
    k.dma("sp", DTB[:], dr["gdn_dt_bias"].partition_broadcast(128))
    k.act(NEGA[:], NEGA[:], AF.Exp)
    k.ts(NEGA[:], NEGA[:], -1.0, None, ALU.mult)
    LNG = A.alloc("ln1g", [128, 1024])
    LNB_ = A.alloc("ln1b", [128, 1024])
    k.dma("sp", LNG[:], dr["ln1_g"][0:1, :].partition_broadcast(128))
    k.dma("sp", LNB_[:], dr["ln1_b"][0:1, :].partition_broadcast(128))
    stats = A.alloc("stats", [128, 2, 6])
    mv = A.alloc("mv", [128, 2])
    rstd = A.alloc("rstd", [128, 1])
    CWB = load_small_params(k, PS, A, dr["gdn_conv_w"], dr["gdn_conv_b"], 4, 24, "gcw")
    load_weight_bf16(k, WIN, dr["gdn_w_in"], 8, GDN_IN)
    load_weight_bf16(k, WOUT, dr["gdn_w_out"], 8, 1024)
    m_work = A.mark()

    XTOK = [A.alloc("xtok", [128, 1024]) for _ in range(2)]
    XT = A.alloc("xT", [128, 8, 128], BF16)
    PRE = A.alloc("pre", [128, 24, 131])
    QKV_ = [A.alloc("qkv", [128, 24, 128]) for _ in range(2)]
    ZT_ = [A.alloc("zT", [128, 8, 128]) for _ in range(2)]
    GT_ = [A.alloc("gt", [128, 16]) for _ in range(2)]
    SM_ = [A.alloc("sm", [128, 12, 8]) for _ in range(2)]
    GC_ = [A.alloc("gc", [128, 16]) for _ in range(2)]
    S2 = [A.alloc("S", [128, 4, 128]) for _ in range(2)]
    VN2 = [A.alloc("vnew", [128, 4, 128]) for _ in range(2)]
    OG = A.alloc("og", [128, 8, 128], BF16)
    WP = Pool4(A, 22, "gw", width=512)
    k.reg32(*WP.free_)
    k.reg32(*(QKV_ + S2 + VN2))
    k.memset(PRE[:, :, 0:3], 0.0, w=[(PRE.name, c) for c in range(24)])
    for x_ in S2 + VN2:
        k.ts(x_[:], ONES[:].unsqueeze(1).to_broadcast([128, 4, 128]), 0.0, None, ALU.mult)
    pst = C.PS.t + [C.psO]
    PSF = SlotRot([pst[0][:, 0:512], pst[0][:, 512:1024]])
    PSH = [SlotRot([pst[1][:, 0:512], pst[1][:, 512:1024]]), SlotRot([pst[2][:, 0:512], pst[2][:, 512:1024]])]
    PSO = [pst[3][:, 0:512], pst[3][:, 512:1024]]
    PS_full = PS
    PS = PSF
    PREk = [(PRE.name, c) for c in range(24)]

    def v3(t):
        return t[:].rearrange("p (h j) -> p h j", h=8) if len(t.shape) == 2 else t[:]

    def hb(ap):
        return ap.unsqueeze(2).to_broadcast([128, 8, 128])

    def mb(ap):
        return ap.unsqueeze(1).to_broadcast([128, 8, 128])

    def front(t, xtok, QKV, ZT, GT, SM, GC):
        QKVk = [(QKV.name, c) for c in range(24)]
        qk_, kk_, vk_ = QKVk[0:8], QKVk[8:16], QKVk[16:24]
        BETA, LNBETA, G8, GPL, EGC, DEC, BG, DECA, DECB, T8, E8 = (SM[:, i, :] for i in range(11))
        gcum, glast = GC[:, 0:8], GC[:, 8:16]
        k.dma("sp", xtok[:], dr["xp"][t * 128:(t + 1) * 128, :])
        to_feature_major(k, PS, XT, xtok, 128, 8)

        def evac(i, grp, ps, view):
            if grp[0] < 24:
                k.cp(PRE[:, grp[0]:grp[0] + len(grp), 3:131], view, eng="act",
                     r=PK(view), w=[(PRE.name, c) for c in grp])
            else:
                k.act(ZT[:, grp[0] - 24:grp[0] - 24 + len(grp), :], view, AF.Silu)
        yield
        for g0 in range(0, 32, 4):
            proj_feature_major(k, PS, WIN, WIN.name, XT, 128, list(range(g0, g0 + 4)), evac)
            yield
        psg = PS.next()
        for kc in range(8):
            k.mm(psg[:, 0:16], XT[:, kc, :], WIN[:, kc, 4096:4112], start=(kc == 0), stop=(kc == 7),
                 r=[(WIN.name, kc, 2), XT.name], w=[psg.name])
        k.cp(GT[:], psg[:, 0:16])
        yield
        for c in range(24):
            k.act(QKV[:, c, :], PRE[:, c, 0:128], AF.Identity, bias=CWB[:, c, 4:5], scale=CWB[:, c, 0:1],
                  r=[(PRE.name, c), CWB.name], w=[(QKV.name, c)])
            if c % 8 == 7:
                yield
        for kk in range(1, 4):
            for c in range(24):
                k.stt(QKV[:, c, :], PRE[:, c, kk:kk + 128], CWB[:, c, kk:kk + 1], QKV[:, c, :], ALU.mult, ALU.add,
                      r=[(PRE.name, c), (QKV.name, c), CWB.name], w=[(QKV.name, c)])
                if c % 6 == 5:
                    yield
        k.act(QKV[:], QKV[:], AF.Silu, r=QKVk, w=QKVk)
        if t == C.n_prompt_tiles - 1:
            for pc in range(6):
                TL = WP.get()
                to_token_major(k, PS, TL, PRE[:, pc * 4:(pc + 1) * 4, 128:131], 3, 4,
                               rkey=lambda c, pc=pc: [(PRE.name, pc * 4 + c)])
                k.dma("pool", dr["gcp"][:, pc * 512:(pc + 1) * 512], TL[0:3, :])
                WP.free(TL)
        k.cp(PRE[:, :, 0:3], PRE[:, :, 128:131], eng="dve", r=PREk, w=PREk)

        for which in range(2):
            for hf in range(2):
                c0 = which * 8 + hf * 4
                keys = QKVk[c0:c0 + 4]
                SQ = WP.get()
                sq3 = SQ[:].rearrange("p (h j) -> p h j", h=4)
                k.act(sq3, QKV[:, c0:c0 + 4, :], AF.Square, r=keys, w=[SQ.name])
                psq = PS.next()
                k.mm(psq, ONES[:], SQ[:])
                if which == 0:
                    k.act(SQ[:], psq, AF.Ln, bias=C.epsq[:], scale=128.0)
                else:
                    k.act(SQ[:], psq, AF.Ln, bias=C.eps6[:], scale=1.0)
                k.act(SQ[:], SQ[:], AF.Exp, scale=-0.5)
                k.tt(QKV[:, c0:c0 + 4, :], QKV[:, c0:c0 + 4, :], sq3, ALU.mult, r=keys + [SQ.name], w=keys)
                WP.free(SQ)
                yield

        yield
        k.act(E8, GT[:, 0:8], AF.Exp, scale=-1.0, r=[GT.name], w=[(SM.name, 10)])
        k.act(LNBETA, E8, AF.Ln, bias=C.one[:], scale=1.0, r=[(SM.name, 10), C.one.name], w=[(SM.name, 1)])
        k.ts(LNBETA, LNBETA, -1.0, None, ALU.mult, r=[(SM.name, 1)], w=[(SM.name, 1)])
        k.act(BETA, LNBETA, AF.Exp, r=[(SM.name, 1)], w=[(SM.name, 0)])
        k.tt(T8, GT[:, 8:16], DTB[:], ALU.add, r=[GT.name, DTB.name], w=[(SM.name, 9)])
        k.ts(T8, T8, 60.0, None, ALU.min, r=[(SM.name, 9)], w=[(SM.name, 9)])
        k.act(T8, T8, AF.Exp, r=[(SM.name, 9)], w=[(SM.name, 9)])
        k.act(T8, T8, AF.Ln, bias=C.one[:], scale=1.0, r=[(SM.name, 9), C.one.name], w=[(SM.name, 9)])
        k.tt(G8, T8, NEGA[:], ALU.mult, r=[(SM.name, 9), NEGA.name], w=[(SM.name, 2)])
        psc = PS.next()
        k.mm(psc[:, 0:8], U[:], G8, r=[U.name, (SM.name, 2)], w=[psc.name])
        k.mm(psc[:, 8:16], UBLK[:], G8, r=[UBLK.name, (SM.name, 2)], w=[psc.name])
        k.cp(GC[:], psc[:, 0:16])
        gcum, glast = GC[:, 0:8], GC[:, 8:16]
        k.tt(GPL, gcum, LNBETA, ALU.add, r=[GC.name, (SM.name, 1)], w=[(SM.name, 3)])
        k.act(EGC, gcum, AF.Exp, r=[GC.name], w=[(SM.name, 4)])
        k.tt(DEC, glast, gcum, ALU.subtract, r=[GC.name], w=[(SM.name, 5)])
        k.act(DEC, DEC, AF.Exp, r=[(SM.name, 5)], w=[(SM.name, 5)])
        k.tt(BG, BETA, EGC, ALU.mult, r=[(SM.name, 0), (SM.name, 4)], w=[(SM.name, 6)])
        k.ts(DECA, DEC, CH0[:, 0:1], None, ALU.mult, r=[(SM.name, 5), CH0.name], w=[(SM.name, 7)])
        k.ts(DECB, DEC, CH1[:, 0:1], None, ALU.mult, r=[(SM.name, 5), CH1.name], w=[(SM.name, 8)])


    def back_half(hf, t, xtok, QKV, ZT, GT, SM, GC):
        QKVk = [(QKV.name, c) for c in range(24)]
        qk_, kk_, vk_ = QKVk[0:8], QKVk[8:16], QKVk[16:24]
        BETA, LNBETA, G8, GPL, EGC, DEC, BG, DECA, DECB, T8, E8 = (SM[:, i, :] for i in range(11))
        gcum, glast = GC[:, 0:8], GC[:, 8:16]
        H0 = 4 * hf
        hsl = slice(H0, H0 + 4)
        PSh = PSH[hf]
        psO = PSO[hf]
        Sh, VNh = S2[hf], VN2[hf]
        qk4, kk4, vk4 = qk_[H0:H0 + 4], kk_[H0:H0 + 4], vk_[H0:H0 + 4]

        def hb4(ap):
            return ap[:, hsl].unsqueeze(2).to_broadcast([128, 4, 128])

        def mb4(ap):
            return ap.unsqueeze(1).to_broadcast([128, 4, 128])

        def v4(x):
            return (x[:] if hasattr(x, "shape") and len(x.shape) == 2 and not hasattr(x, "offset") else x).rearrange(
                "p (h j) -> p h j", h=4)

        def hsl_(h):
            return slice(h * 128, (h + 1) * 128)
        X, X2 = WP.get(), WP.get()
        k.tt(v4(X), mb4(U[:]), hb4(G8), ALU.mult, r=[U.name, (SM.name, 2)], w=[X.name])
        k.tt(v4(X2), mb4(IDENT[:]), hb4(LNBETA), ALU.mult, r=[IDENT.name, (SM.name, 1)], w=[X2.name])
        k.tt(X2[:], X2[:], X[:], ALU.add)
        psR, psR2 = PSh.next(), PSh.next()
        k.mm(psR, ONES[:], X[:])
        k.mm(psR2, ONES[:], X2[:])
        WP.free(X, X2)
        yield
        ER, GL, GUS, GUI = WP.get(), WP.get(), WP.get(), WP.get()
        k.act(ER[:], psR, AF.Exp)
        k.stt(v4(GL), v4(psR), -1.0, mb4(MASKL[:]), ALU.mult, ALU.add)
        k.tt(v4(GL), v4(GL), hb4(GPL), ALU.add, r=[GL.name, (SM.name, 3)], w=[GL.name])
        k.act(GL[:], GL[:], AF.Exp)
        k.tt(v4(GUS), v4(psR2), mb4(MASKUS[:]), ALU.add)
        k.tt(v4(GUS), v4(GUS), hb4(gcum), ALU.subtract, r=[GUS.name, GC.name], w=[GUS.name])
        k.act(GUS[:], GUS[:], AF.Exp)
        k.tt(v4(GUI), v4(psR), mb4(MASKUI[:]), ALU.add)
        k.tt(v4(GUI), v4(GUI), hb4(gcum), ALU.subtract, r=[GUI.name, GC.name], w=[GUI.name])
        k.act(GUI[:], GUI[:], AF.Exp)
        yield
        psK, psQ = PSh.next(), PSh.next()
        for h in range(4):
            k.mm(psK[:, hsl_(h)], QKV[:, 8 + H0 + h, :], QKV[:, 8 + H0 + h, :], r=[kk4[h]], w=PK(psK[:, hsl_(h)]))
        for h in range(4):
            k.mm(psQ[:, hsl_(h)], QKV[:, 8 + H0 + h, :], QKV[:, H0 + h, :], r=[kk4[h], qk4[h]], w=PK(psQ[:, hsl_(h)]))
        NN, AA, QKT = WP.get(), WP.get(), WP.get()
        k.stt(RR(NN[:]), psK, -1.0, GL[:], ALU.mult, ALU.mult)
        k.stt(RR(AA[:]), psK, -1.0, GUS[:], ALU.mult, ALU.mult)
        k.tt(QKT[:], psQ, GUI[:], ALU.mult)
        WP.free(GL, GUS, GUI)
        QQ = WP.get()
        k.tt(RR(v4(QQ)), v4(AA), mb4(IDENT[:]), ALU.add)
        yield
        for lvl in range(1, 6):
            NN2 = WP.get()
            psn = PSh.next()
            for h in range(4):
                k.mm(psn[:, hsl_(h)], RR(AA[:, hsl_(h)]), RR(NN[:, hsl_(h)]))
            k.cp(RR(NN2[:]), psn, eng="act")
            if lvl < 5:
                AA2 = WP.get()
                psa = PSh.next()
                for h in range(4):
                    k.mm(psa[:, hsl_(h)], RR(NN[:, hsl_(h)]), RR(AA[:, hsl_(h)]))
                k.cp(RR(AA2[:]), psa, eng="act")
            yield
            QQ2 = WP.get()
            psq2 = PSh.next()
            for h in range(4):
                k.mm(psq2[:, hsl_(h)], RR(NN2[:, hsl_(h)]), RR(QQ[:, hsl_(h)]))
            k.tt(RR(QQ2[:]), QQ[:], psq2, ALU.add)
            WP.free(NN, QQ)
            NN, QQ = NN2, QQ2
            if lvl < 5:
                WP.free(AA)
                AA = AA2
            yield
        WP.free(NN, AA)
        psT1, psT2 = PSh.next(), PSh.next()
        for h in range(4):
            k.tr(psT1[:, hsl_(h)], QKV[:, 8 + H0 + h, :], r=[kk4[h], IDENT.name], w=PK(psT1[:, hsl_(h)]))
        for h in range(4):
            k.tr(psT2[:, hsl_(h)], QKV[:, 16 + H0 + h, :], r=[vk4[h], IDENT.name], w=PK(psT2[:, hsl_(h)]))
        VB, KBG, KDA, KDB = WP.get(), WP.get(), WP.get(), WP.get()
        k.tt(v4(VB), v4(psT2), hb4(BETA), ALU.mult, r=PK(psT2) + [(SM.name, 0)], w=[VB.name])
        k.tt(v4(KBG), v4(psT1), hb4(BG), ALU.mult, r=PK(psT1) + [(SM.name, 6)], w=[KBG.name])
        k.tt(v4(KDA), v4(psT1), hb4(DECA), ALU.mult, r=PK(psT1) + [(SM.name, 7)], w=[KDA.name])
        k.tt(v4(KDB), v4(psT1), hb4(DECB), ALU.mult, r=PK(psT1) + [(SM.name, 8)], w=[KDB.name])
        yield
        psU, psW = PSh.next(), PSh.next()
        for h in range(4):
            k.mm(psU[:, hsl_(h)], QQ[:, hsl_(h)], VB[:, hsl_(h)])
        for h in range(4):
            k.mm(psW[:, hsl_(h)], KBG[:, hsl_(h)], QQ[:, hsl_(h)])
        UU, WT, QD = WP.get(), WP.get(), WP.get()
        k.cp(UU[:], psU, eng="act")
        k.cp(WT[:], psW, eng="act")
        WP.free(QQ, VB, KBG)
        k.tt(v4(QD), QKV[:, H0:H0 + 4, :], v4(ER), ALU.mult, r=qk4 + [ER.name], w=[QD.name])
        TMP = WP.get()
        yield
        for c in range(2):
            rows = slice(c * 64, (c + 1) * 64)
            psWS = PSh.next()
            for h in range(4):
                k.mm(psWS[:, hsl_(h)], WT[:, hsl_(h)], Sh[:, h, :])
            k.tt(VNh[rows].rearrange("p h v -> p (h v)"), UU[rows, :], psWS[rows, :], ALU.subtract)
            for h in range(4):
                cs = slice(h * 128 + c * 64, h * 128 + c * 64 + 64)
                k.mm(psO[:, cs], Sh[:, h, :], QD[:, cs], start=True, stop=False)
                k.mm(psO[:, cs], VNh[:, h, :], QKT[:, cs], start=False, stop=True)
            yield
            psD = PSh.next()
            KD = KDA if c == 0 else KDB
            for h in range(4):
                k.mm(psD[:, hsl_(h)], KD[:, hsl_(h)], VNh[:, h, :])
            last = c * 64 + 63
            k.tt(v4(TMP), Sh[:], v4(ER)[:, :, last:last + 1].to_broadcast([128, 4, 128]), ALU.mult)
            k.tt(Sh[:].rearrange("p h v -> p (h v)"), TMP[:], psD, ALU.add)
            yield
        WP.free(UU, WT, QD, QKT, KDA, KDB, ER)
        OT, OSQ = WP.get(), WP.get()
        k.cp(OT[:], psO, eng="act")
        k.act(OSQ[:], psO, AF.Square)
        psS = PSh.next()
        k.mm(psS, ONES[:], OSQ[:])
        k.act(OSQ[:], psS, AF.Ln, bias=C.eps6[:], scale=1.0 / 128.0)
        k.act(OSQ[:], OSQ[:], AF.Exp, scale=-0.5)
        k.tt(OT[:], OT[:], OSQ[:], ALU.mult)
        k.stt(OG[:, hsl, :].rearrange("p h j -> p (h j)"), OT[:], NORMW[:, 0:1],
              ZT[:, hsl, :].rearrange("p h j -> p (h j)"), ALU.mult, ALU.mult,
              r=[OT.name, NORMW.name, ZT.name], w=[(OG.name, hf)])
        WP.free(OSQ, TMP, OT)


    def join(t, xtok):
        RES = WP.get()
        RES2 = WP.get()
        for nb in range(2):
            psY = PSF.next()
            for h in range(8):
                k.mm(psY, OG[:, h, :], WOUT[:, h, nb * 512:(nb + 1) * 512],
                     start=(h == 0), stop=(h == 7), r=[(OG.name, h // 4), (WOUT.name, h, 0)], w=PK(psY))
            R_ = RES if nb == 0 else RES2
            k.stt(R_[:], xtok[:, nb * 512:(nb + 1) * 512], ALPHA, psY, ALU.mult, ALU.add)
        for i, R_ in enumerate((RES, RES2)):
            k.P.op("dve", lambda e, i=i, R_=R_: e.bn_stats(out=stats[:, i, :], in_=R_[:]), [R_.name], [(stats.name, i)])
        k.P.op("dve", lambda e: e.bn_aggr(out=mv[:], in_=stats[:]), [(stats.name, 0), (stats.name, 1)], [mv.name])
        k.act(rstd[:], mv[:, 1:2], AF.Ln, bias=C.eps5[:], scale=1.0)
        k.act(rstd[:], rstd[:], AF.Exp, scale=-0.5)
        for i, R_ in enumerate((RES, RES2)):
            cs = slice(i * 512, (i + 1) * 512)
            k.ts(R_[:], R_[:], mv[:, 0:1], rstd[:, 0:1], ALU.subtract, ALU.mult)
            k.tt(R_[:], R_[:], LNG[:, cs], ALU.mult)
            k.tt(R_[:], R_[:], LNB_[:, cs], ALU.add)
            k.dma("pool", dr["x1"][t * 128:(t + 1) * 128, cs], R_[:])
        WP.free(RES, RES2)

    def bufs(t):
        p = t % 2
        return (XTOK[p], QKV_[p], ZT_[p], GT_[p], SM_[p], GC_[p])

    def run_all(gens):
        alive = [True] * len(gens)

        def step(gi):
            if alive[gi]:
                try:
                    next(gens[gi])
                except StopIteration:
                    alive[gi] = False
        while any(alive):
            if len(gens) == 3:
                for gi in (0, 2, 1, 2):
                    step(gi)
            else:
                for gi in range(len(gens)):
                    step(gi)

    NTL = C.n_prompt_tiles
    if NTL > 0:
        run_all([front(0, *bufs(0))])
    for t in range(NTL):
        gens = [back_half(0, t, *bufs(t)), back_half(1, t, *bufs(t))]
        if t + 1 < NTL:
            gens.append(front(t + 1, *bufs(t + 1)))
        run_all(gens)
        join(t, XTOK[t % 2])

    for hf in range(2):
        k.dma("pool", dr["gsp"][4 * hf:4 * hf + 4].rearrange("h k v -> k h v"), S2[hf][:])
    C.gdn = dict(WIN=WIN, WOUT=WOUT, CWB=CWB, NORMW=NORMW, NWBC=NWBC, NEGA=NEGA, DTB=DTB, LNG=LNG, LNB=LNB_,
                 stats=stats, mv=mv, rstd=rstd, m_work=m_work, m_phase=m_phase)


def phase_gdn_sample(C):
    nc, k, P, A, PS, dr = C.nc, C.k, C.P, C.A, C.PS, C.dr
    cst = C.cst
    IDENT, ONES = cst["IDENT"], cst["ONES"]
    g = C.gdn
    WIN, WOUT, CWB, NWBC, NEGA, DTB = g["WIN"], g["WOUT"], g["CWB"], g["NWBC"], g["NEGA"], g["DTB"]
    P.barrier()
    A.reset(g["m_work"])
    n = NS
    XS = A.alloc("xs_tok", [n, 1024])
    XT = A.alloc("xsT", [128, 8, n], BF16)
    PRES = A.alloc("pres", [128, 24, 4, n])
    QKVs = A.alloc("qkvs", [128, 24, n])
    ZTs = A.alloc("zts", [128, 8, n])
    GTs = A.alloc("gts", [n, 16])
    CG = [A.alloc("cg", [n, 3072])]
    k.dma("sp", XS[:], dr["xs"])
    to_feature_major(k, PS, XT, XS, n, 8)
    for r_ in range(3):
        cg = CG[0]
        k.dma("sp", cg[:], dr["cgc"][:, r_, :])
        to_feature_major(k, PS, PRES[:, :, r_, :], cg, n, 24)

    def evac(i, grp, ps, view):
        if grp[0] < 24:
            k.cp(PRES[:, grp[0]:grp[0] + len(grp), 3, :], view, eng="act")
        else:
            k.act(ZTs[:], view, AF.Silu)
    proj_feature_major(k, PS, WIN, WIN.name, XT, n, list(range(32)), evac)
    psg = PS.next()
    for kc in range(8):
        k.mm(psg[0:n, 0:16], XT[:, kc, :], WIN[:, kc, 4096:4112], start=(kc == 0), stop=(kc == 7),
             r=[(WIN.name, kc, 2), XT.name], w=[psg.name])
    k.cp(GTs[:], psg[0:n, 0:16])
    k.dma("pool", dr["gcs"][:, 0:2, :], dr["cgc"][:, 1:3, :])
    NEWR = CG[0]
    to_token_major(k, PS, NEWR, PRES[:, :, 3, :], n, 24)
    k.dma("pool", dr["gcs"][:, 2, :], NEWR[:])
    for c in range(24):
        k.act(QKVs[:, c, :], PRES[:, c, 0, :], AF.Identity, bias=CWB[:, c, 4:5], scale=CWB[:, c, 0:1])
    for kk in range(1, 4):
        for c in range(24):
            k.stt(QKVs[:, c, :], PRES[:, c, kk, :], CWB[:, c, kk:kk + 1], QKVs[:, c, :], ALU.mult, ALU.add)
    k.act(QKVs[:], QKVs[:], AF.Silu)
    QS, KS, VS, ZS = (A.alloc(nm, [128, 128]) for nm in ("qs", "ks", "vs", "zs"))
    for dst, src in ((QS, QKVs[:, 0:8, :]), (KS, QKVs[:, 8:16, :]), (VS, QKVs[:, 16:24, :]), (ZS, ZTs[:])):
        ps = PS.next()
        k.tr(ps[:, 0:128], src.rearrange("p h b -> p (h b)"))
        k.cp(dst[:], ps[:, 0:128], eng="act")
    SMs = A.alloc("sms", [n, 4, 8])
    BETA, EG, T8, L8 = (SMs[:, i, :] for i in range(4))
    k.act(T8, GTs[:, 0:8], AF.Exp, scale=-1.0)
    k.act(L8, T8, AF.Ln, bias=C.one[0:n, :], scale=1.0)
    k.act(BETA, L8, AF.Exp, scale=-1.0)
    k.tt(T8, GTs[:, 8:16], DTB[0:n, :], ALU.add)
    k.ts(T8, T8, 60.0, None, ALU.min)
    k.act(T8, T8, AF.Exp)
    k.act(T8, T8, AF.Ln, bias=C.one[0:n, :], scale=1.0)
    k.tt(T8, T8, NEGA[0:n, :], ALU.mult)
    k.act(EG, T8, AF.Exp)
    LH = A.alloc("lh", [n, 2, 8, n])
    i16 = IDENT[0:n, 0:n].unsqueeze(1).to_broadcast([n, 8, n])
    k.tt(LH[:, 0, :, :], BETA.unsqueeze(2).to_broadcast([n, 8, n]), i16, ALU.mult)
    k.tt(LH[:, 1, :, :], EG.unsqueeze(2).to_broadcast([n, 8, n]), i16, ALU.mult)
    pss = PS.next()
    k.mm(pss[:, 0:1], LH[:, 0, :, :].rearrange("p h b -> p (h b)"), ONES[0:n, 0:1])
    k.mm(pss[:, 1:2], LH[:, 1, :, :].rearrange("p h b -> p (h b)"), ONES[0:n, 0:1])
    SC = A.alloc("sc", [128, 8])
    k.cp(SC[:, 0:2], pss[:, 0:2])
    k.tt(SC[:, 2:3], SC[:, 0:1], SC[:, 1:2], ALU.mult)
    TM = A.alloc("tm", [128, 128])
    for src, col, sc_, eps in ((QS, 3, 128.0, C.epsq), (KS, 4, 1.0, C.eps6)):
        k.tt(TM[:], src[:], src[:], ALU.mult)
        k.red(SC[:, col:col + 1], TM[:])
        k.act(SC[:, col:col + 1], SC[:, col:col + 1], AF.Ln, bias=eps[:], scale=sc_)
        k.act(SC[:, col:col + 1], SC[:, col:col + 1], AF.Exp, scale=-0.5)
        k.ts(src[:], src[:], SC[:, col:col + 1], None, ALU.mult)
    k.tt(TM[:], QS[:], KS[:], ALU.mult)
    k.red(SC[:, 5:6], TM[:])
    WV, QDv, UV = (A.alloc(nm, [128, 128]) for nm in ("wv", "qdv", "uv"))
    k.ts(WV[:], KS[:], SC[:, 2:3], None, ALU.mult)
    k.ts(QDv[:], QS[:], SC[:, 1:2], None, ALU.mult)
    k.ts(UV[:], VS[:], SC[:, 0:1], None, ALU.mult)
    SS = A.alloc("ss", [128, 128, 128])
    for h in range(8):
        k.dma("sp", SS[h * n:(h + 1) * n], dr["sg"][:, h, :, :])
    TMPB = A.alloc("tmpb", [128, 16, 128])
    WS, QSs, PART = (A.alloc(nm, [128, 128]) for nm in ("wS", "qS", "part"))
    for vec, acc in ((WV, WS), (QDv, QSs)):
        for blk in range(8):
            ks_ = slice(blk * 16, (blk + 1) * 16)
            k.tt(TMPB[:], SS[:, ks_, :], vec[:, ks_].unsqueeze(2).to_broadcast([128, 16, 128]), ALU.mult)
            if blk == 0:
                k.red(acc[:], TMPB[:].rearrange("p k v -> p v k"))
            else:
                k.red(PART[:], TMPB[:].rearrange("p k v -> p v k"))
                k.tt(acc[:], acc[:], PART[:], ALU.add)
    VN = A.alloc("vn", [128, 128])
    OS = A.alloc("os", [128, 128])
    k.tt(VN[:], UV[:], WS[:], ALU.subtract)
    k.stt(OS[:], VN[:], SC[:, 5:6], QSs[:], ALU.mult, ALU.add)
    for blk in range(8):
        ks_ = slice(blk * 16, (blk + 1) * 16)
        k.tt(TMPB[:], KS[:, ks_].unsqueeze(2).to_broadcast([128, 16, 128]),
             VN[:].unsqueeze(1).to_broadcast([128, 16, 128]), ALU.mult)
        k.stt(SS[:, ks_, :], SS[:, ks_, :], SC[:, 1:2], TMPB[:], ALU.mult, ALU.add)
    for h in range(8):
        k.dma("sp", dr["gss"][:, h, :, :], SS[h * n:(h + 1) * n])
    k.tt(TM[:], OS[:], OS[:], ALU.mult)
    k.red(SC[:, 6:7], TM[:])
    k.act(SC[:, 6:7], SC[:, 6:7], AF.Ln, bias=C.eps6[:], scale=1.0 / 128.0)
    k.act(SC[:, 6:7], SC[:, 6:7], AF.Exp, scale=-0.5)
    k.stt(OS[:], OS[:], SC[:, 6:7], NWBC[:], ALU.mult, ALU.mult)
    k.tt(OS[:], OS[:], ZS[:], ALU.mult)
    pso = PS.next()
    k.tr(pso[:, 0:128], OS[:])
    OGs = A.alloc("ogs", [128, 8, n], BF16)
    k.cp(OGs[:].rearrange("p h b -> p (h b)"), pso[:, 0:128], eng="act")
    psY = PS.next()
    for nb in range(2):
        for h in range(8):
            k.mm(psY[0:n, nb * 512:(nb + 1) * 512], OGs[:, h, :], WOUT[:, h, nb * 512:(nb + 1) * 512],
                 start=(h == 0), stop=(h == 7), r=[OGs.name, (WOUT.name, h, 0)], w=[psY.name])
    RES = CG[0]
    k.stt(RES[:, 0:1024], XS[:], ALPHA, psY[0:n, :], ALU.mult, ALU.add)
    layer_norm_out(k, A, RES, n, g["LNG"], g["LNB"], RES, g["stats"], g["mv"], g["rstd"])
    k.dma("pool", dr["x1"][L:L + n, :], RES[:, 0:1024])
    P.barrier()
    A.reset(g["m_phase"])


class _Cut(Exception):
    pass


class Ctx:
    def cut(self, n):
        import os
        return int(os.environ.get("K_CUT", "0")) == n


CONST_NAMES = ("IDENT", "ONES", "U", "UBLK", "MASKL", "MASKUS", "MASKUI")


def make_consts():
    i = np.arange(128)
    same = (i[:, None] // 64) == (i[None, :] // 64)
    c = {}
    c["IDENT"] = np.eye(128)
    c["ONES"] = np.ones((128, 128))
    c["U"] = (same & (i[:, None] <= i[None, :])).astype(np.float64)
    c["UBLK"] = same.astype(np.float64)
    c["MASKL"] = np.where(same & (i[:, None] > i[None, :]), 0.0, -BIG)
    c["MASKUS"] = np.where(same & (i[None, :] > i[:, None]), 0.0, -BIG)
    c["MASKUI"] = np.where(same & (i[None, :] >= i[:, None]), 0.0, -BIG)
    arr = np.concatenate([c[n] for n in CONST_NAMES], axis=1)
    small = np.zeros((128, 8))
    small[:, 0] = (i < 64)
    small[:, 1] = (i >= 64)
    small[:, 2] = 1.0
    small[:, 3] = 1e-6
    small[:, 4] = 128e-6
    small[:, 5] = 1e-5
    return np.ascontiguousarray(np.concatenate([arr, small], axis=1).astype(np.float32))


IN_SHAPES = {
    "xp": [L, D], "xs": [NS, D], "cgc": [NS, 3, 3072], "sg": [NS, 8, 128, 128], "csc": [NS, 3, 3072],
    "ss": [NS, 32, 64, 128], "cfc": [2, NS, 2, DFF],
    "gdn_w_in": [D, GDN_IN], "gdn_conv_w": [4, 3072], "gdn_conv_b": [1, 3072], "gdn_a_log": [1, 8],
    "gdn_dt_bias": [1, 8], "gdn_norm_w": [1, 128], "gdn_w_out": [1024, D],
    "ssd_w_in": [D, SSD_IN], "ssd_conv_w": [4, 3072], "ssd_conv_b": [1, 3072], "ssd_a_log": [1, 32],
    "ssd_dt_bias": [1, 32], "ssd_d": [1, 32], "ssd_norm_w": [1, 2048], "ssd_w_out": [2048, D],
    "ffn_w_up": [2, D, 2 * DFF], "ffn_conv_w": [2, 3, DFF], "ffn_conv_b": [2, DFF], "ffn_w_down": [2, DFF, D],
    "ln1_g": [2, D], "ln1_b": [2, D], "ln2_g": [2, D], "ln2_b": [2, D],
    "consts": [128, 7 * 128 + 8],
}
OUT_SHAPES = {
    "yp": [L, D], "ys": [NS, D], "gcp": [3, 3072], "gcs": [NS, 3, 3072], "gsp": [8, 128, 128],
    "gss": [NS, 8, 128, 128], "scp": [3, 3072], "scs": [NS, 3, 3072], "ssp": [32, 64, 128],
    "sss": [NS, 32, 64, 128], "fcp": [2, 2, DFF], "fcs": [2, NS, 2, DFF],
}


def build_program(stop=None, n_prompt_tiles=NT, dbg=False):
    nc = bass.Bass("TRN2", target_bir_lowering=False)
    C = Ctx()
    C.nc = nc
    C.P = Prog(nc)
    C.k = KB(nc, C.P)
    C.A = Arena(nc)
    C.n_prompt_tiles = n_prompt_tiles
    dr = {}
    for n_, shp in IN_SHAPES.items():
        dr[n_] = nc.dram_tensor(n_, shp, F32, kind="ExternalInput").ap()
    for n_, shp in OUT_SHAPES.items():
        dr[n_] = nc.dram_tensor(n_, shp, F32, kind="ExternalOutput").ap()
    for n_ in ("x1", "x2", "x3"):
        kind = "ExternalOutput" if dbg else "Internal"
        dr[n_] = nc.dram_tensor(n_, [L + NS, D], F32, kind=kind).ap()
    dr["ysc"] = nc.dram_tensor("ysc", [NT + 1, 128, 2048], F32, kind="Internal").ap()
    C.dr = dr
    ps = [nc.alloc_psum_tensor("ps%d" % i, [128, 1024], F32) for i in range(4)]
    C.psO = ps[3]
    C.PS = PsumRot.__new__(PsumRot)
    C.PS.t = ps[0:3]
    C.PS.i = 0
    k, A = C.k, C.A
    C.cst = {}
    for i, n_ in enumerate(CONST_NAMES):
        C.cst[n_] = A.alloc("c_" + n_, [128, 128])
    smalls = {}
    for i, n_ in enumerate(("CH0", "CH1", "one", "eps6", "epsq", "eps5")):
        smalls[n_] = A.alloc("c_" + n_, [128, 1])
    k.reg32(*[C.cst[n_] for n_ in CONST_NAMES])
    m_ct = A.mark()
    CT = A.alloc("consts", [128, 7 * 128 + 8])
    k.dma("sp", CT[:], dr["consts"])
    for i, n_ in enumerate(CONST_NAMES):
        k.cp(C.cst[n_][:], CT[:, i * 128:(i + 1) * 128], eng="pool")
    for i, n_ in enumerate(("CH0", "CH1", "one", "eps6", "epsq", "eps5")):
        k.cp(smalls[n_][:], CT[:, 896 + i:897 + i], eng="pool")
        if n_ in ("CH0", "CH1"):
            C.cst[n_] = smalls[n_]
        else:
            setattr(C, n_, smalls[n_])
    C.P.barrier()
    A.reset(m_ct)
    k.ident = C.cst["IDENT"]
    k.eps5 = C.eps5

    import os
    phase_gdn(C)
    if not os.environ.get("K_SKIP_SAMPLE"):
        phase_gdn_sample(C)
    if stop != "A":
        phase_ffn(C, 0, "x1", "x2", "x2")
        if stop != "B":
            phase_ssd(C)
            if stop != "C":
                phase_ffn(C, 1, "x3", None, None)
    C.P.emit()
    C.sbuf_peak = A.peak
    return nc, C


_PROG_CACHE = {}


def shard_inputs(inputs):
    f = lambda a: np.ascontiguousarray(np.asarray(a, dtype=np.float32))
    shared = {
        "gdn_w_in": f(inputs["gdn_w_in"][0]), "gdn_conv_w": f(inputs["gdn_conv_w"][0]),
        "gdn_conv_b": f(inputs["gdn_conv_b"]), "gdn_a_log": f(inputs["gdn_a_log"]),
        "gdn_dt_bias": f(inputs["gdn_dt_bias"]), "gdn_norm_w": f(inputs["gdn_norm_w"]),
        "gdn_w_out": f(inputs["gdn_w_out"][0]),
        "ssd_w_in": f(inputs["ssd_w_in"][0]), "ssd_conv_w": f(inputs["ssd_conv_w"][0]),
        "ssd_conv_b": f(inputs["ssd_conv_b"]), "ssd_a_log": f(inputs["ssd_a_log"]),
        "ssd_dt_bias": f(inputs["ssd_dt_bias"]), "ssd_d": f(inputs["ssd_d"]),
        "ssd_norm_w": f(inputs["ssd_norm_w"]), "ssd_w_out": f(inputs["ssd_w_out"][0]),
        "ffn_w_up": f(inputs["ffn_w_up"]), "ffn_conv_w": f(inputs["ffn_conv_w"]),
        "ffn_conv_b": f(inputs["ffn_conv_b"]), "ffn_w_down": f(inputs["ffn_w_down"]),
        "ln1_g": f(inputs["ln1_g"]), "ln1_b": f(inputs["ln1_b"]),
        "ln2_g": f(inputs["ln2_g"]), "ln2_b": f(inputs["ln2_b"]),
        "consts": make_consts(),
    }
    maps = []
    for c in range(NCORES):
        b = slice(c * NS, (c + 1) * NS)
        m = dict(shared)
        m["xp"] = f(inputs["x_prompt"][c])
        m["xs"] = f(inputs["x_sample"][b, 0])
        m["cgc"] = f(inputs["cache_gdn_conv"][0, b])
        m["sg"] = f(inputs["state_gdn"][0, b])
        m["csc"] = f(inputs["cache_ssd_conv"][0, b])
        m["ss"] = f(inputs["state_ssd"][0, b])
        m["cfc"] = f(inputs["cache_ffn_conv"][:, b])
        maps.append(m)
    return maps


def kernel(**inputs):
    if "nc" not in _PROG_CACHE:
        _PROG_CACHE["nc"] = build_program()[0]
    nc = _PROG_CACHE["nc"]
    maps = shard_inputs(inputs)
    res = run_bass_kernel_spmd(nc, maps, core_ids=list(range(NCORES)))
    R = res.results
    cat = lambda n_: np.stack([R[c][n_] for c in range(NCORES)], 0)
    y_p = cat("yp")
    y_s = np.concatenate([R[c]["ys"] for c in range(NCORES)], 0)[:, None, :]
    gcp = cat("gcp")[None]
    gcs = np.concatenate([R[c]["gcs"] for c in range(NCORES)], 0)[None]
    gsp = cat("gsp")[None]
    gss = np.concatenate([R[c]["gss"] for c in range(NCORES)], 0)[None]
    scp = cat("scp")[None]
    scs = np.concatenate([R[c]["scs"] for c in range(NCORES)], 0)[None]
    ssp = cat("ssp")[None]
    sss = np.concatenate([R[c]["sss"] for c in range(NCORES)], 0)[None]
    fcp = np.stack([R[c]["fcp"] for c in range(NCORES)], 1)
    fcs = np.concatenate([R[c]["fcs"] for c in range(NCORES)], 1)
    return (y_p, y_s, gcp, gcs, gsp, gss, scp, scs, ssp, sss, fcp, fcs)


def phase_ffn(C, layer, src, dst, _unused=None):
    nc, k, P, A, PS, dr = C.nc, C.k, C.P, C.A, C.PS, C.dr
    P.barrier()
    m_phase = A.mark()
    WUP = A.alloc("wup", [128, 8, 2 * DFF], BF16)
    WDN = A.alloc("wdn", [128, NFC, 1024], BF16)
    LNG = A.alloc("ln2g", [128, 1024])
    LNB_ = A.alloc("ln2b", [128, 1024])
    k.dma("sp", LNG[:], dr["ln2_g"][layer:layer + 1, :].partition_broadcast(128))
    k.dma("sp", LNB_[:], dr["ln2_b"][layer:layer + 1, :].partition_broadcast(128))
    stats = A.alloc("stats", [128, 2, 6])
    mv = A.alloc("mv", [128, 2])
    rstd = A.alloc("rstd", [128, 1])
    TAIL = A.alloc("ftail", [128, NFC, 2])
    CWB = load_small_params(k, PS, A, dr["ffn_conv_w"][layer], dr["ffn_conv_b"][layer:layer + 1, :], 3, NFC, "fcw")
    load_weight_bf16(k, WUP, dr["ffn_w_up"][layer], 8, 2 * DFF)
    load_weight_bf16(k, WDN, dr["ffn_w_down"][layer], NFC, 1024)
    k.memset(TAIL[:], 0.0)
    m_work = A.mark()
    TW = 256
    NW_ = 5
    XTOK = [A.alloc("fxtok", [128, 1024]) for _ in range(4)]
    XT = A.alloc("fxT", [128, 8, TW], BF16)
    H2 = [A.alloc("fh", [128, NFC, TW], BF16) for _ in range(2)]
    PREF = [A.alloc("fpre", [128, TW + 2]) for _ in range(NW_)]
    ACC = [A.alloc("facc", [128, TW]) for _ in range(NW_)]
    RES = A.alloc("fres", [128, 1024])
    pst = C.PS.t + [C.psO]
    PSF = SlotRot([pst[0][:, 0:512]])
    PSFC = SlotRot([pst[0][:, 512:1024]] + [pst[i][:, j * 512:(j + 1) * 512] for i in (2, 3) for j in (0, 1)])

    def out_rows(r0, n):
        if dst is not None:
            return dr[dst][r0:r0 + n, :]
        return dr["yp"][r0:r0 + n, :] if r0 < L else dr["ys"][r0 - L:r0 - L + n, :]

    def wk(fc, val):
        col = (DFF if val else 0) + fc * 128
        return col, col // 2048

    nmac = (C.n_prompt_tiles * 128) // TW
    cnt = [0]

    def front_gen(m):
        for s_ in range(2):
            xt_ = XTOK[(m % 2) * 2 + s_]
            k.dma("sp", xt_[:], dr[src][(2 * m + s_) * 128:(2 * m + s_ + 1) * 128, :])
            to_feature_major(k, PSF, XT[:, :, s_ * 128:(s_ + 1) * 128], xt_, 128, 8)
            yield

    def fc_gen(m, fc):
        H = H2[m % 2]
        psG = PSFC.next()
        i_ = cnt[0] % NW_
        cnt[0] += 1
        for val in (0, 1):
            col, cb = wk(fc, val)
            for kc in range(8):
                k.mm(psG[:, val * TW:(val + 1) * TW], WUP[:, kc, col:col + 128], XT[:, kc, :],
                     start=(kc == 0), stop=(kc == 7), r=[(WUP.name, kc, cb), XT.name], w=PK(psG))
        yield
        pre, acc = PREF[i_], ACC[i_]
        k.cp(pre[:, 0:2], TAIL[:, fc, :], r=[(TAIL.name, fc)], w=[pre.name])
        k.cp(pre[:, 2:TW + 2], psG[:, 0:TW], eng="act")
        yield
        k.cp(TAIL[:, fc, :], pre[:, TW:TW + 2], r=[pre.name], w=[(TAIL.name, fc)])
        k.act(acc[:], pre[:, 0:TW], AF.Identity, bias=CWB[:, fc, 3:4], scale=CWB[:, fc, 0:1])
        yield
        k.stt(acc[:], pre[:, 1:TW + 1], CWB[:, fc, 1:2], acc[:], ALU.mult, ALU.add)
        k.stt(acc[:], pre[:, 2:TW + 2], CWB[:, fc, 2:3], acc[:], ALU.mult, ALU.add)
        yield
        k.act(acc[:], acc[:], AF.Silu)
        yield
        k.tt(H[:, fc, :], acc[:], psG[:, TW:2 * TW], ALU.mult, w=[(H.name, fc)])

    def down_gen(m):
        H = H2[m % 2]
        psY = pst[1]
        for s_ in range(2):
            xt_ = XTOK[(m % 2) * 2 + s_]
            for nb in range(2):
                for fc in range(NFC):
                    k.mm(psY[:, nb * 512:(nb + 1) * 512], H[:, fc, s_ * 128:(s_ + 1) * 128],
                         WDN[:, fc, nb * 512:(nb + 1) * 512], start=(fc == 0), stop=(fc == NFC - 1),
                         r=[(H.name, fc), (WDN.name, fc, 0)], w=PK(psY[:, nb * 512:(nb + 1) * 512]))
                yield
            k.stt(RES[:], xt_[:], ALPHA, psY[:], ALU.mult, ALU.add)
            yield
            layer_norm_out(k, A, RES, 128, LNG, LNB_, RES, stats, mv, rstd)
            k.dma("pool", out_rows((2 * m + s_) * 128, 128), RES[:])
            yield

    def run_window(entries, width):
        finished = set()
        active = []
        nxt_i = 0
        n_ = len(entries)
        while active or nxt_i < n_:
            n_adm = 0
            while nxt_i < n_ and len(active) < width and all(d in finished for d in entries[nxt_i][1]):
                if len(entries[nxt_i]) > 2 and (n_adm >= 1 or sum(1 for a_ in active if len(entries[a_[0]]) > 2) >= NW_):
                    break
                if len(entries[nxt_i]) > 2:
                    n_adm += 1
                active.append((nxt_i, entries[nxt_i][0]))
                nxt_i += 1
            assert active, "window deadlock"
            keep = []
            for idx, g_ in active:
                try:
                    next(g_)
                    keep.append((idx, g_))
                except StopIteration:
                    finished.add(idx)
            active = keep

    entries = []
    idx_front, idx_fc, idx_down = {}, {}, {}
    for m in range(nmac):
        idx_front[m] = len(entries)
        entries.append((front_gen(m), []))
        for fc in range(NFC):
            deps = [idx_front[m]]
            if m >= 1:
                deps.append(idx_fc[(m - 1, fc)])
            if m >= 2:
                deps.append(idx_down[m - 2])
            idx_fc[(m, fc)] = len(entries)
            entries.append((fc_gen(m, fc), deps, 'fc'))
        idx_down[m] = len(entries)
        entries.append((down_gen(m), [idx_fc[(m, fc)] for fc in range(NFC)]))
    run_window(entries, NW_ + 1)
    for pc in range(3):
        nchp = min(8, NFC - pc * 8)
        to_token_major(k, PS, RES, TAIL[:, pc * 8:pc * 8 + nchp, :], 2, nchp,
                       rkey=lambda c, pc=pc: [(TAIL.name, pc * 8 + c)])
        k.dma("pool", dr["fcp"][layer, :, pc * 1024:pc * 1024 + nchp * 128], RES[0:2, 0:nchp * 128])
    P.barrier()
    A.reset(m_work)
    n = NS
    XS = A.alloc("fxs", [n, 1024])
    XTs = A.alloc("fxsT", [128, 8, n], BF16)
    CF = A.alloc("fcf", [n, DFF])
    CFT = A.alloc("fcft", [128, NFC, 2, n])
    NEWG = A.alloc("fnewg", [128, NFC, n])
    Hs = A.alloc("fhs", [128, NFC, n], BF16)
    ACCs = [A.alloc("faccs", [128, n]) for _ in range(2)]
    k.dma("sp", XS[:], dr[src][L:L + n, :])
    to_feature_major(k, PS, XTs, XS, n, 8)
    for r_ in range(2):
        k.dma("sp", CF[:], dr["cfc"][layer, :, r_, :])
        to_feature_major(k, PS, CFT[:, :, r_, :], CF, n, NFC)
    k.dma("pool", dr["fcs"][layer, :, 0, :], dr["cfc"][layer, :, 1, :])
    for fc in range(NFC):
        psG = PS.next()
        for val in (0, 1):
            col, cb = wk(fc, val)
            for kc in range(8):
                k.mm(psG[:, val * 512:val * 512 + n], WUP[:, kc, col:col + 128], XTs[:, kc, :],
                     start=(kc == 0), stop=(kc == 7), r=[(WUP.name, kc, cb), XTs.name], w=[psG.name])
        acc = ACCs[fc % 2]
        k.cp(NEWG[:, fc, :], psG[:, 0:n], eng="act")
        k.act(acc[:], CFT[:, fc, 0, :], AF.Identity, bias=CWB[:, fc, 3:4], scale=CWB[:, fc, 0:1])
        k.stt(acc[:], CFT[:, fc, 1, :], CWB[:, fc, 1:2], acc[:], ALU.mult, ALU.add)
        k.stt(acc[:], NEWG[:, fc, :], CWB[:, fc, 2:3], acc[:], ALU.mult, ALU.add)
        k.act(acc[:], acc[:], AF.Silu)
        k.tt(Hs[:, fc, :], acc[:], psG[:, 512:512 + n], ALU.mult)
    to_token_major(k, PS, CF, NEWG, n, NFC)
    k.dma("pool", dr["fcs"][layer, :, 1, :], CF[:])
    psY = PS.next()
    for nb in range(2):
        for fc in range(NFC):
            k.mm(psY[0:n, nb * 512:(nb + 1) * 512], Hs[:, fc, :], WDN[:, fc, nb * 512:(nb + 1) * 512],
                 start=(fc == 0), stop=(fc == NFC - 1), r=[Hs.name, (WDN.name, fc, 0)], w=[psY.name])
    RESs = A.alloc("fress", [n, 1024])
    k.stt(RESs[:], XS[:], ALPHA, psY[0:n, :], ALU.mult, ALU.add)
    layer_norm_out(k, A, RESs, n, LNG, LNB_, RESs, stats, mv, rstd)
    k.dma("pool", out_rows(L, n), RESs[:])
    P.barrier()
    A.reset(m_phase)


def phase_ssd(C):
    nc, k, P, A, PS, dr = C.nc, C.k, C.P, C.A, C.PS, C.dr
    cst = C.cst
    IDENT, ONES, U, UBLK, MASKUI, CH0, CH1 = (cst[n] for n in ("IDENT", "ONES", "U", "UBLK", "MASKUI", "CH0", "CH1"))
    P.barrier()
    m_phase = A.mark()
    NX = 3104
    WIN = A.alloc("swin", [128, 8, NX], BF16)
    NEGA = A.alloc("snega", [128, 32])
    DTB = A.alloc("sdtb", [128, 32])
    DFULL = A.alloc("sdfull", [128, 32])
    DV = A.alloc("sdv", [128, 16])
    k.dma("sp", NEGA[:], dr["ssd_a_log"].partition_broadcast(128))
    k.dma("sp", DTB[:], dr["ssd_dt_bias"].partition_broadcast(128))
    k.dma("sp", DFULL[:], dr["ssd_d"].partition_broadcast(128))
    k.act(NEGA[:], NEGA[:], AF.Exp)
    k.ts(NEGA[:], NEGA[:], -1.0, None, ALU.mult)
    for hh in range(2):
        k.cp(DV[hh * 64:(hh + 1) * 64, :], DFULL[hh * 64:(hh + 1) * 64, :].rearrange("p (c t) -> p c t", t=2)[:, :, hh])
    CWB = load_small_params(k, PS, A, dr["ssd_conv_w"], dr["ssd_conv_b"], 4, 24, "scw")
    load_weight_bf16(k, WIN, dr["ssd_w_in"][:, 2048:2048 + NX], 8, NX)
    m_work = A.mark()
    XTOK = A.alloc("sxtok", [128, 1024])
    XT = A.alloc("sxT", [128, 8, 128], BF16)
    PRE = A.alloc("spre", [128, 24, 131])
    XBC_ = [A.alloc("sxbc", [128, 24, 128]) for _ in range(2)]
    SM_ = [A.alloc("ssm", [128, 8, 32]) for _ in range(2)]
    AC_ = [A.alloc("sac", [128, 64]) for _ in range(2)]
    XPAD_ = [A.alloc("sxpad", [128, 32, 128], BF16) for _ in range(2)]
    XDA_ = [A.alloc("sxda", [128, 2048], BF16) for _ in range(2)]
    XDB_ = [A.alloc("sxdb", [128, 2048], BF16) for _ in range(2)]
    BTOK_ = [A.alloc("sbtok", [128, 4, 128], BF16) for _ in range(2)]
    AEXP_ = [A.alloc("saexp", [128, 2048]) for _ in range(2)]
    STG = [A.alloc("sst", [128, 512]) for _ in range(4)]
    YT = A.alloc("syt", [128, 16, 128])
    TMP3 = A.alloc("stmp3", [128, 16, 128])
    XA_ = [A.alloc("sxa", [128, 8, 128]) for _ in range(2)]
    SCT_ = [A.alloc("ssct", [128, 8, 128], BF16) for _ in range(2)]
    EG2 = [A.alloc("seg", [128, 4, 128]) for _ in range(2)]
    TMPS_ = [A.alloc("stmps", [128, 512]) for _ in range(2)]
    TMP2_ = [A.alloc("stmp2", [128, 512]) for _ in range(2)]
    EL_ = [A.alloc("sel", [128, 2, 32]) for _ in range(2)]
    AVC = A.alloc("savc", [128, 2, 32])
    MASK8 = A.alloc("smask8", [128, 8, 128])
    k.reg32(MASK8)
    k.cp(MASK8[:], MASKUI[:].unsqueeze(1).to_broadcast([128, 8, 128]))
    NAC_ = [A.alloc("snac", [128, 32]) for _ in range(2)]
    k.memset(PRE[:, :, 0:3], 0.0, w=[(PRE.name, c) for c in range(24)])
    k.reg32(*(XBC_ + AEXP_ + STG + XA_))
    for x_ in STG:
        k.ts(x_[:], ONES[:].unsqueeze(1).to_broadcast([128, 4, 128]).rearrange("p a b -> p (a b)") if False else
             ONES[:, 0:1].to_broadcast([128, 512]), 0.0, None, ALU.mult)
    for x_ in XPAD_:
        k.memset(x_[:], 0.0)
    PREk = [(PRE.name, c) for c in range(24)]
    pst = C.PS.t + [C.psO]
    PSF = SlotRot([pst[0][:, 0:512], pst[0][:, 512:1024]])

    def mb(ap):
        return ap.unsqueeze(1).to_broadcast([128, 8, 128])

    def front(t, XBC, SM, AC, XPAD, XDA, XDB, BTOK, AEXP):
        PS = PSF
        XBCk = [(XBC.name, c) for c in range(24)]
        DTR, DTV, AV, DEC, COEF, COEFA, COEFB, T32 = (SM[:, i, :] for i in range(8))
        k.dma("sp", XTOK[:], dr["x2"][t * 128:(t + 1) * 128, :])
        to_feature_major(k, PS, XT, XTOK, 128, 8)
        yield

        def evac(i, grp, ps, view):
            k.cp(PRE[:, grp[0]:grp[0] + len(grp), 3:131], view, eng="act", r=PK(view), w=[(PRE.name, c) for c in grp])
        for g0 in range(0, 24, 4):
            proj_feature_major(k, PS, WIN, WIN.name, XT, 128, list(range(g0, g0 + 4)), evac)
            yield
        psg = PS.next()
        for kc in range(8):
            k.mm(psg[:, 0:32], XT[:, kc, :], WIN[:, kc, 3072:3104], start=(kc == 0), stop=(kc == 7),
                 r=[(WIN.name, kc, 1), XT.name], w=PK(psg[:, 0:32]))
        k.cp(DTR, psg[:, 0:32], w=[(SM.name, 0)])
        yield
        for c in range(24):
            k.act(XBC[:, c, :], PRE[:, c, 0:128], AF.Identity, bias=CWB[:, c, 4:5], scale=CWB[:, c, 0:1],
                  r=[(PRE.name, c), CWB.name], w=[(XBC.name, c)])
            if c % 8 == 7:
                yield
        for kk in range(1, 4):
            for c in range(24):
                k.stt(XBC[:, c, :], PRE[:, c, kk:kk + 128], CWB[:, c, kk:kk + 1], XBC[:, c, :], ALU.mult, ALU.add,
                      r=[(PRE.name, c), (XBC.name, c), CWB.name], w=[(XBC.name, c)])
                if c % 6 == 5:
                    yield
        k.act(XBC[:], XBC[:], AF.Silu, r=XBCk, w=XBCk)
        if t == C.n_prompt_tiles - 1:
            for pc in range(6):
                to_token_major(k, PS, TMP3[:].rearrange("p c t -> p (c t)"), PRE[:, pc * 4:(pc + 1) * 4, 128:131], 3, 4,
                               rkey=lambda c, pc=pc: [(PRE.name, pc * 4 + c)])
                k.dma("pool", dr["scp"][:, pc * 512:(pc + 1) * 512], TMP3[0:3, 0:4, :].rearrange("p c t -> p (c t)"))
        k.cp(PRE[:, :, 0:3], PRE[:, :, 128:131], eng="dve", r=PREk, w=PREk)
        yield
        k.tt(T32, DTR, DTB[:], ALU.add, r=[(SM.name, 0), DTB.name], w=[(SM.name, 7)])
        k.ts(T32, T32, 60.0, None, ALU.min, r=[(SM.name, 7)], w=[(SM.name, 7)])
        k.act(T32, T32, AF.Exp, r=[(SM.name, 7)], w=[(SM.name, 7)])
        k.act(DTV, T32, AF.Ln, bias=C.one[:], scale=1.0, r=[(SM.name, 7), C.one.name], w=[(SM.name, 1)])
        k.tt(AV, DTV, NEGA[:], ALU.mult, r=[(SM.name, 1), NEGA.name], w=[(SM.name, 2)])
        psc = PS.next()
        k.mm(psc[:, 0:32], U[:], AV, r=[U.name, (SM.name, 2)], w=PK(psc[:, 0:32]))
        k.mm(psc[:, 32:64], UBLK[:], AV, r=[UBLK.name, (SM.name, 2)], w=PK(psc[:, 32:64]))
        k.cp(AC[:], psc[:, 0:64])
        acum, alast = AC[:, 0:32], AC[:, 32:64]
        k.ts(NAC_[t % 2][:], acum, -1.0, None, ALU.mult)
        k.ts(AVC[:, 0, :], AV, CH0[:, 0:1], None, ALU.mult, r=[(SM.name, 2), CH0.name], w=[AVC.name])
        k.ts(AVC[:, 1, :], AV, CH1[:, 0:1], None, ALU.mult, r=[(SM.name, 2), CH1.name], w=[AVC.name])
        psel = PS.next()
        k.mm(psel[:, 0:64], ONES[:], AVC[:].rearrange("p c h -> p (c h)"))
        k.act(EL_[t % 2][:].rearrange("p c h -> p (c h)"), psel[:, 0:64], AF.Exp)
        k.tt(DEC, alast, acum, ALU.subtract, r=[AC.name], w=[(SM.name, 3)])
        k.act(DEC, DEC, AF.Exp, r=[(SM.name, 3)], w=[(SM.name, 3)])
        k.tt(COEF, DEC, DTV, ALU.mult, r=[(SM.name, 3), (SM.name, 1)], w=[(SM.name, 4)])
        k.ts(COEFA, COEF, CH0[:, 0:1], None, ALU.mult, r=[(SM.name, 4), CH0.name], w=[(SM.name, 5)])
        k.ts(COEFB, COEF, CH1[:, 0:1], None, ALU.mult, r=[(SM.name, 4), CH1.name], w=[(SM.name, 6)])
        k.cp(AEXP[:].rearrange("p (h q) -> p h q", q=64), AV.unsqueeze(2).to_broadcast([128, 32, 64]),
             r=[(SM.name, 2)], w=[AEXP.name])
        yield
        XP4 = XPAD[:].rearrange("p (c t) q -> p c t q", t=2)
        for qtr in range(4):
            pt = PS.next()
            for i in range(4):
                k.tr(pt[:, i * 128:(i + 1) * 128], XBC[:, qtr * 4 + i, :], r=[XBCk[qtr * 4 + i], IDENT.name],
                     w=PK(pt[:, i * 128:(i + 1) * 128]))
            pt4 = pt.rearrange("p (c t q) -> p c t q", t=2, q=64)
            cs = slice(qtr * 4, qtr * 4 + 4)
            hs = slice(qtr * 8, qtr * 8 + 8)
            for hh in range(2):
                dt_v = DTV.rearrange("p (c t) -> p c t", t=2)[:, cs, hh].unsqueeze(2).to_broadcast([128, 4, 64])
                k.tt(XP4[:, cs, hh, hh * 64:(hh + 1) * 64], pt4[:, :, hh, :], dt_v, ALU.mult,
                     r=PK(pt) + [(SM.name, 1)], w=[XPAD.name])
            pt3 = pt.rearrange("p (h q) -> p h q", q=64)
            for XD, CO, ci in ((XDA, COEFA, 5), (XDB, COEFB, 6)):
                k.tt(XD[:].rearrange("p (h q) -> p h q", q=64)[:, hs, :], pt3,
                     CO[:, hs].unsqueeze(2).to_broadcast([128, 8, 64]), ALU.mult, r=PK(pt) + [(SM.name, ci)], w=[XD.name])
            yield
        pb = PS.next()
        for g in range(4):
            k.tr(pb[:, g * 128:(g + 1) * 128], XBC[:, 16 + g, :], r=[XBCk[16 + g], IDENT.name],
                 w=PK(pb[:, g * 128:(g + 1) * 128]))
        k.cp(BTOK[:].rearrange("p g n -> p (g n)"), pb, eng="act")

    def grp_thread(th, t, XBC, SM, AC, XPAD, XDA, XDB, BTOK, AEXP):
        XBCk = [(XBC.name, c) for c in range(24)]
        DTR, DTV, AV, DEC, COEF, COEFA, COEFB, T32 = (SM[:, i, :] for i in range(8))
        acum = AC[:, 0:32]
        XA, SCT, EG_, TMPS, TMP2 = XA_[th], SCT_[th], EG2[th], TMPS_[th], TMP2_[th]
        pab = pst[1 + th]
        pc_ = pst[3][:, th * 512:(th + 1) * 512]
        for g in (th, th + 2):
            ST = STG[g]
            hsl = slice(8 * g, 8 * g + 8)
            k.tt(XA[:], mb(U[:]), AV[:, hsl].unsqueeze(2).to_broadcast([128, 8, 128]), ALU.mult,
                 r=[U.name, (SM.name, 2)], w=[XA.name])
            psR = pab
            XAf = XA[:].rearrange("p h j -> p (h j)")
            M8f = MASK8[:].rearrange("p h j -> p (h j)")
            for nb in range(2):
                k.mm(psR[:, nb * 512:(nb + 1) * 512], ONES[:], XAf[:, nb * 512:(nb + 1) * 512], start=True, stop=False)
                k.mm(psR[:, nb * 512:(nb + 1) * 512], IDENT[:], M8f[:, nb * 512:(nb + 1) * 512], start=False, stop=True)
            yield
            psR3 = psR[:].rearrange("p (h j) -> p h j", h=8)
            SEG = XA
            k.tt(SEG[:], psR3, acum[:, hsl].unsqueeze(2).to_broadcast([128, 8, 128]), ALU.subtract,
                 r=PK(psR[:]) + [AC.name], w=[SEG.name])
            k.act(SEG[:], SEG[:], AF.Exp)
            EL = EL_[t % 2]
            psCB = pc_
            k.mm(psCB[:, 0:128], XBC[:, 16 + g, :], XBC[:, 20 + g, :], r=[XBCk[16 + g], XBCk[20 + g]],
                 w=PK(psCB[:, 0:128]))
            yield
            k.tt(SCT[:], SEG[:], mb(psCB[:, 0:128]), ALU.mult)
            psE = pab[:, 0:512]
            for cl in range(4):
                cp_ = 4 * g + cl
                k.mm(psE[:, cl * 128:(cl + 1) * 128], AEXP[:, cp_ * 128:(cp_ + 1) * 128], U[:])
            k.act(EG_[:].rearrange("p c j -> p (c j)"), psE, AF.Exp)
            yield
            psYd = pab[:, 512:1024]
            psYo = pc_
            for cl in range(4):
                h0 = 8 * g + 2 * cl
                k.mm(psYd[:, cl * 128:(cl + 1) * 128], XPAD[:, h0, :], SCT[:, 2 * cl, :], start=True, stop=False)
                k.mm(psYd[:, cl * 128:(cl + 1) * 128], XPAD[:, h0 + 1, :], SCT[:, 2 * cl + 1, :], start=False, stop=True)
            for c in range(2):
                for cl in range(4):
                    k.mm(psYo[:, cl * 128 + c * 64:cl * 128 + c * 64 + 64], ST[:, cl * 128:(cl + 1) * 128],
                         XBC[:, 20 + g, c * 64:(c + 1) * 64], r=[ST.name, XBCk[20 + g]],
                         w=PK(psYo[:, cl * 128 + c * 64:cl * 128 + c * 64 + 64]))
                psD = pab[:, 0:512]
                XD = XDA if c == 0 else XDB
                k.mm(psD, BTOK[:, g, :], XD[:, 512 * g:512 * (g + 1)])
                yield
                k.tt(TMPS[:].rearrange("p (h q) -> p h q", q=64), ST[:].rearrange("p (h q) -> p h q", q=64),
                     EL[:, c, hsl].unsqueeze(2).to_broadcast([128, 8, 64]), ALU.mult,
                     r=[ST.name, EL.name], w=[TMPS.name])
                k.tt(ST[:], TMPS[:], psD, ALU.add)
                yield
            k.tt(TMP2[:], psYo, EG_[:].rearrange("p c j -> p (c j)"), ALU.mult)
            k.tt(YT[:, 4 * g:4 * g + 4, :].rearrange("p c j -> p (c j)"), TMP2[:], psYd, ALU.add,
                 w=[(YT.name, g)])
            yield

    def tail(t, XBC):
        XBCk = [(XBC.name, c) for c in range(24)]
        k.tt(TMP3[:], XBC[:, 0:16, :], DV[:].unsqueeze(2).to_broadcast([128, 16, 128]), ALU.mult,
             r=XBCk[0:16] + [DV.name], w=[TMP3.name])
        k.tt(YT[:], YT[:], TMP3[:], ALU.add, r=[(YT.name, g) for g in range(4)] + [TMP3.name],
             w=[(YT.name, g) for g in range(4)])
        k.dma("pool", dr["ysc"][t], YT[:].rearrange("p c j -> p (c j)"), r=[(YT.name, g) for g in range(4)])

    def bufs(t):
        p = t % 2
        return (XBC_[p], SM_[p], AC_[p], XPAD_[p], XDA_[p], XDB_[p], BTOK_[p], AEXP_[p])

    def run_all(gens):
        alive = [True] * len(gens)

        def step(gi):
            if alive[gi]:
                try:
                    next(gens[gi])
                except StopIteration:
                    alive[gi] = False
        while any(alive):
            if len(gens) == 3:
                for gi in (0, 2, 1, 2):
                    step(gi)
            else:
                for gi in range(len(gens)):
                    step(gi)

    NTL = C.n_prompt_tiles
    if NTL > 0:
        run_all([front(0, *bufs(0))])
    for t in range(NTL):
        gens = [grp_thread(0, t, *bufs(t)), grp_thread(1, t, *bufs(t))]
        if t + 1 < NTL:
            gens.append(front(t + 1, *bufs(t + 1)))
        run_all(gens)
        tail(t, XBC_[t % 2])
    for g in range(4):
        pt = C.PS.next()
        for i in range(4):
            k.tr(pt[:, i * 128:(i + 1) * 128], STG[g][:, i * 128:(i + 1) * 128])
        k.cp(TMP3[:, 0:4, :].rearrange("p c t -> p (c t)"), pt[:, 0:512], eng="act")
        k.dma("pool", dr["ssp"].rearrange("h p s -> (h p) s").rearrange("(c q) s -> q c s", q=128)[:, g * 4:g * 4 + 4, :],
              TMP3[:, 0:4, :])
    ssd_sample(C, WIN, CWB, NEGA, DTB, DV, m_work)
    P.barrier()
    A.reset(m_phase)
    ssd_gate(C)


def ssd_sample(C, WIN, CWB, NEGA, DTB, DV, m_work):
    nc, k, P, A, PS, dr = C.nc, C.k, C.P, C.A, C.PS, C.dr
    IDENT, ONES = C.cst["IDENT"], C.cst["ONES"]
    P.barrier()
    A.reset(m_work)
    n = NS
    TOP = 229376 - 65536 - 1024
    C.WZ = nc.alloc_sbuf_tensor_at("swz_top", [128, 8, 2048], BF16, offset=TOP)
    C.WOUT2 = nc.alloc_sbuf_tensor_at("swout_top", [128, 16, 1024], BF16, offset=TOP + 32768)
    load_weight_bf16(k, C.WZ, dr["ssd_w_in"][:, 0:2048], 8, 2048)
    load_weight_bf16(k, C.WOUT2, dr["ssd_w_out"], 16, 1024)
    A.hi = TOP
    XS = A.alloc("qxs", [n, 1024])
    XT = A.alloc("qxT", [128, 8, n], BF16)
    PRES = A.alloc("qpres", [128, 24, 4, n])
    XBCs = A.alloc("qxbc", [128, 24, n])
    CG = A.alloc("qcg", [n, 3072])
    k.dma("sp", XS[:], dr["x2"][L:L + n, :])
    to_feature_major(k, PS, XT, XS, n, 8)
    for r_ in range(3):
        k.dma("sp", CG[:], dr["csc"][:, r_, :])
        to_feature_major(k, PS, PRES[:, :, r_, :], CG, n, 24)

    def evac(i, grp, ps, view):
        k.cp(PRES[:, grp[0]:grp[0] + len(grp), 3, :], view, eng="act")
    proj_feature_major(k, PS, WIN, WIN.name, XT, n, list(range(24)), evac)
    psg = PS.next()
    for kc in range(8):
        k.mm(psg[0:n, 0:32], XT[:, kc, :], WIN[:, kc, 3072:3104], start=(kc == 0), stop=(kc == 7),
             r=[(WIN.name, kc, 1), XT.name], w=[psg.name])
    SMs = A.alloc("qsm", [n, 4, 32])
    DTR, DTs, DAs, T32 = (SMs[:, i, :] for i in range(4))
    k.cp(DTR, psg[0:n, 0:32])
    k.dma("pool", dr["scs"][:, 0:2, :], dr["csc"][:, 1:3, :])
    to_token_major(k, PS, CG, PRES[:, :, 3, :], n, 24)
    k.dma("pool", dr["scs"][:, 2, :], CG[:])
    for c in range(24):
        k.act(XBCs[:, c, :], PRES[:, c, 0, :], AF.Identity, bias=CWB[:, c, 4:5], scale=CWB[:, c, 0:1])
    for kk in range(1, 4):
        for c in range(24):
            k.stt(XBCs[:, c, :], PRES[:, c, kk, :], CWB[:, c, kk:kk + 1], XBCs[:, c, :], ALU.mult, ALU.add)
    k.act(XBCs[:], XBCs[:], AF.Silu)
    k.tt(T32, DTR, DTB[0:n, :], ALU.add)
    k.ts(T32, T32, 60.0, None, ALU.min)
    k.act(T32, T32, AF.Exp)
    k.act(DTs, T32, AF.Ln, bias=C.one[0:n, :], scale=1.0)
    k.tt(T32, DTs, NEGA[0:n, :], ALU.mult)
    k.act(DAs, T32, AF.Exp)
    LS = A.alloc("qls", [n, 2, 128])
    k.memset(LS[:], 0.0)
    k.memset(LS[:, 0, 0:64], 1.0)
    k.memset(LS[:, 1, 64:128], 1.0)
    RH = A.alloc("qrh", [n, 4, 16, n])
    i16 = IDENT[0:n, 0:n].unsqueeze(1).to_broadcast([n, 16, n])
    for qi, src in enumerate((DTs, DAs)):
        for hh in range(2):
            k.tt(RH[:, qi * 2 + hh, :, :], src.rearrange("p (c t) -> p c t", t=2)[:, :, hh].unsqueeze(2).to_broadcast([n, 16, n]),
                 i16, ALU.mult)
    psx = PS.next()
    for qi in range(2):
        for hh in range(2):
            k.mm(psx[:, qi * 256:(qi + 1) * 256], LS[:, hh, :], RH[:, qi * 2 + hh, :, :].rearrange("p c b -> p (c b)"),
                 start=(hh == 0), stop=(hh == 1))
    FX = A.alloc("qfx", [128, 2, 16, n])
    k.cp(FX[:].rearrange("p a c b -> p (a c b)"), psx[:, 0:512])
    XDT = A.alloc("qxdt", [128, 16, n])
    k.tt(XDT[:], XBCs[:, 0:16, :], FX[:, 0, :, :], ALU.mult)
    BCT = A.alloc("qbct", [n, 1024])
    to_token_major(k, PS, BCT, XBCs[:, 16:24, :], n, 8)
    YS_ = A.alloc("qys", [128, 16, n])
    SB = [A.alloc("qsb", [128, 16, 128]) for _ in range(2)]
    TQ = A.alloc("qtq", [128, 16, 128])
    XSEL = [A.alloc("qxsel", [n, 1024]) for _ in range(2)]
    for b in range(n):
        S_ = SB[b % 2]
        xsel = XSEL[b % 2]
        k.dma("sp", S_[:], dr["ss"][b].rearrange("h p s -> (h p) s").rearrange("(c q) s -> q c s", q=128))
        k.ts(xsel[:], BCT[:], IDENT[0:n, b:b + 1], None, ALU.mult)
        psb = PS.next()
        for nb in range(2):
            k.mm(psb[:, nb * 512:(nb + 1) * 512], ONES[0:n, :], xsel[:, nb * 512:(nb + 1) * 512])
        S4 = S_[:].rearrange("p (g c) s -> p g c s", g=4)
        T4 = TQ[:].rearrange("p (g c) s -> p g c s", g=4)
        Bbc = psb[:, 0:512].rearrange("p (g s) -> p g s", g=4).unsqueeze(2).to_broadcast([128, 4, 4, 128])
        Cbc = psb[:, 512:1024].rearrange("p (g s) -> p g s", g=4).unsqueeze(2).to_broadcast([128, 4, 4, 128])
        xdt_bc = XDT[:, :, b].rearrange("p (g c) -> p g c", g=4).unsqueeze(3).to_broadcast([128, 4, 4, 128])
        da_bc = FX[:, 1, :, b].rearrange("p (g c) -> p g c", g=4).unsqueeze(3).to_broadcast([128, 4, 4, 128])
        k.tt(T4, Bbc, xdt_bc, ALU.mult, r=[psb.name, XDT.name], w=[TQ.name])
        k.tt(S4, S4, da_bc, ALU.mult, r=[S_.name, FX.name], w=[S_.name])
        k.tt(S_[:], S_[:], TQ[:], ALU.add)
        k.dma("sp", dr["sss"][b].rearrange("h p s -> (h p) s").rearrange("(c q) s -> q c s", q=128), S_[:])
        k.tt(T4, S4, Cbc, ALU.mult, r=[S_.name, psb.name], w=[TQ.name])
        k.red(YS_[:, :, b], TQ[:])
    k.tt(XDT[:], XBCs[:, 0:16, :], DV[:].unsqueeze(2).to_broadcast([128, 16, n]), ALU.mult)
    k.tt(YS_[:], YS_[:], XDT[:], ALU.add)
    k.dma("pool", dr["ysc"][NT][:, 0:16 * n], YS_[:].rearrange("p c b -> p (c b)"))


def ssd_gate(C):
    nc, k, P, A, PS, dr = C.nc, C.k, C.P, C.A, C.PS, C.dr
    ONES = C.cst["ONES"]
    P.barrier()
    m_phase = A.mark()
    WZ, WOUT = C.WZ, C.WOUT2
    LNG = A.alloc("l1g", [128, 1024])
    LNB_ = A.alloc("l1b", [128, 1024])
    k.dma("sp", LNG[:], dr["ln1_g"][1:2, :].partition_broadcast(128))
    k.dma("sp", LNB_[:], dr["ln1_b"][1:2, :].partition_broadcast(128))
    stats = A.alloc("stats", [128, 2, 6])
    mv = A.alloc("mv", [128, 2])
    rstd = A.alloc("rstd", [128, 1])
    NWT = A.alloc("snwt", [16, 128])
    NW = A.alloc("snw", [128, 1, 16])
    k.dma("sp", NWT[:], dr["ssd_norm_w"].rearrange("o (c q) -> (o c) q", q=128))
    to_feature_major(k, PS, NW, NWT, 16, 1, evac="dve")
    NSL = 3
    XTOK = [A.alloc("gxtok", [128, 1024]) for _ in range(NSL)]
    XT_ = [A.alloc("gxT", [128, 8, 128], BF16) for _ in range(NSL)]
    YT_ = [A.alloc("gyt", [128, 16, 128]) for _ in range(NSL)]
    ZT_ = [A.alloc("gzt", [128, 16, 128]) for _ in range(NSL)]
    GSQ_ = [A.alloc("ggsq", [128, 16, 128]) for _ in range(NSL)]
    k.reg32(*GSQ_)
    RS_ = [A.alloc("grs", [128, 4, 128]) for _ in range(NSL)]
    YN_ = [A.alloc("gyn", [128, 16, 128], BF16) for _ in range(NSL)]
    RES_ = [A.alloc("gres", [128, 1024]) for _ in range(NSL)]
    pst = C.PS.t + [C.psO]
    PSR = SlotRot([pst[i][:, j * 512:(j + 1) * 512] for i in (0, 1, 2) for j in (0, 1)])
    psYt = pst[3]
    tiles = [(t, 128) for t in range(C.n_prompt_tiles)] + [(NT, NS)]

    def tile_gen(ti, t, n):
        sl = ti % NSL
        xtok, yt, XT, ZT, GSQ, RS, YN, RES = XTOK[sl], YT_[sl], XT_[sl], ZT_[sl], GSQ_[sl], RS_[sl], YN_[sl], RES_[sl]
        r0 = t * 128
        k.dma("sp", xtok[0:n, :], dr["x2"][r0:r0 + n, :])
        k.dma("sp", yt[:, :, 0:n], dr["ysc"][t][:, 0:16 * n].rearrange("p (c j) -> p c j", j=n))
        yield
        to_feature_major(k, PSR, XT[:, :, 0:n], xtok, n, 8)
        yield

        def evac(i, grp, ps, view):
            k.act(ZT[:, grp[0]:grp[0] + len(grp), 0:n], view, AF.Silu)
        for g0 in range(0, 16, 4):
            proj_feature_major(k, PSR, WZ, WZ.name, XT, n, list(range(g0, g0 + 4)), evac)
            yield
        k.tt(yt[:, :, 0:n], yt[:, :, 0:n], ZT[:, :, 0:n], ALU.mult)
        k.act(GSQ[:, :, 0:n], yt[:, :, 0:n], AF.Square)
        yield
        pss = PSR.next()
        for g in range(4):
            for cl in range(4):
                k.mm(pss[:, g * 128:g * 128 + n], ONES[:], GSQ[:, 4 * g + cl, 0:n], start=(cl == 0), stop=(cl == 3))
        pv = pss.rearrange("p (g j) -> p g j", g=4)[:, :, 0:n]
        k.act(RS[:, :, 0:n], pv, AF.Ln, bias=C.eps6[:], scale=1.0 / 512.0)
        k.act(RS[:, :, 0:n], RS[:, :, 0:n], AF.Exp, scale=-0.5)
        yield
        y4 = yt[:, :, 0:n].rearrange("p (g c) j -> p g c j", g=4)
        k.tt(y4, y4, RS[:, :, 0:n].unsqueeze(2).to_broadcast([128, 4, 4, n]), ALU.mult, r=[yt.name, RS.name], w=[yt.name])
        k.tt(YN[:, :, 0:n], yt[:, :, 0:n], NW[:, 0, :].unsqueeze(2).to_broadcast([128, 16, n]), ALU.mult)
        yield
        for nb in range(2):
            for c in range(16):
                k.mm(psYt[0:n, nb * 512:(nb + 1) * 512], YN[:, c, 0:n], WOUT[:, c, nb * 512:(nb + 1) * 512],
                     start=(c == 0), stop=(c == 15), r=[YN.name, (WOUT.name, c, 0)],
                     w=PK(psYt[0:n, nb * 512:(nb + 1) * 512]))
        k.stt(RES[0:n, :], xtok[0:n, :], ALPHA, psYt[0:n, :], ALU.mult, ALU.add)
        yield
        layer_norm_out(k, A, RES, n, LNG, LNB_, RES, stats, mv, rstd)
        k.dma("pool", dr["x3"][r0:r0 + n, :], RES[0:n, :])

    active = []
    pend = [tile_gen(ti, t, n) for ti, (t, n) in enumerate(tiles)]
    while active or pend:
        if pend and len(active) < NSL:
            active.append(pend.pop(0))
        keep = []
        for g_ in active:
            try:
                next(g_)
                keep.append(g_)
            except StopIteration:
                pass
        active = keep
    P.barrier()
    A.reset(m_phase)
    A.hi = 229376
```

```python
import contextlib
import numpy as np
import concourse.bass as bass
import concourse.mybir as mybir
from concourse.bass_utils import run_bass_kernel_spmd

F32 = mybir.dt.float32
F32R = mybir.dt.float32r
BF16 = mybir.dt.bfloat16


def RR(ap):
    return ap.bitcast(F32R)
AF = mybir.ActivationFunctionType
ALU = mybir.AluOpType

NCORES = 8
ATTACH_WAIT = True
L = 2048
NT = 16
NS = 16
D = 1024
BIG = 30000.0
ALPHA = 4.0 ** 0.25
GDN_IN = 4112
SSD_IN = 5152
DFF = 2816
NFC = 22


class _Op:
    __slots__ = ("eng", "fn", "deps", "is_dma", "sem", "semval", "sig", "sigidx", "prev_sem_wait")

    def __init__(self, eng, fn, is_dma):
        self.eng = eng
        self.fn = fn
        self.is_dma = is_dma
        self.deps = []
        self.sem = None
        self.semval = 0
        self.sig = False
        self.sigidx = 0
        self.prev_sem_wait = None


class Prog:
    ENGS = ("pe", "act", "dve", "pool", "sp")

    def __init__(self, nc):
        self.nc = nc
        self.ops = []
        self.last_writer = {}
        self.readers = {}
        self.n_dma_sems = {"sp": 44, "pool": 36, "act": 8}
        self.dma_count = {q: 0 for q in self.n_dma_sems}
        self.dma_ops = {q: [] for q in self.n_dma_sems}
        self.bar_list = []

    @staticmethod
    def _key(r):
        if isinstance(r, tuple):
            return tuple(Prog._key(x) for x in r)
        if isinstance(r, (str, int)):
            return r
        return r.name

    def _track(self, op, reads, writes):
        reads = [self._key(r) for r in reads]
        writes = [self._key(r) for r in writes]
        deps = set()
        for r in reads + writes:
            w = self.last_writer.get(r)
            if w is not None:
                deps.add(w)
        for w in writes:
            rd = self.readers.get(w)
            if rd:
                deps.update(rd.values())
        deps.discard(op)
        op.deps = list(deps)
        for r in reads:
            d = self.readers.setdefault(r, {})
            d[("dma", id(op)) if op.is_dma else op.eng] = op
        for w in writes:
            self.last_writer[w] = op
            self.readers[w] = {}

    def op(self, eng, fn, reads=(), writes=()):
        o = _Op(eng, fn, False)
        self._track(o, reads, writes)
        self.ops.append(o)
        return o

    def dma(self, q, out, in_, reads=(), writes=(), **kw):
        o = _Op(q, lambda e: e.dma_start(out=out, in_=in_, **kw), True)
        n = self.dma_count[q]
        K = self.n_dma_sems[q]
        o.sem = (q, n % K)
        o.semval = 16 * (n // K + 1)
        if n >= K:
            o.prev_sem_wait = (o.sem, o.semval - 16)
        self.dma_count[q] = n + 1
        self._track(o, reads, writes)
        self.ops.append(o)
        self.dma_ops[q].append(o)
        return o

    def barrier(self):
        last = {}
        for o in self.ops:
            if o.is_dma:
                last[o.sem] = o
            else:
                last[o.eng] = o
        self.bar_list.append((len(self.ops), list(last.values())))

    def emit(self):
        nc = self.nc
        ops = self.ops
        for (pos, deps) in self.bar_list:
            seen = set()
            for o in ops[pos:]:
                if o.eng not in seen:
                    seen.add(o.eng)
                    o.deps = list(set(o.deps) | set(deps))
                if len(seen) == len(self.ENGS):
                    break
        for o in ops:
            nd = []
            for d in o.deps:
                if (not d.is_dma) and d.eng == o.eng and (not o.is_dma) and o.eng == "pe":
                    continue
                nd.append(d)
            o.deps = nd
            for d in nd:
                if not d.is_dma:
                    d.sig = True
        cnt = {e: 0 for e in self.ENGS}
        for o in ops:
            if not o.is_dma and o.sig:
                cnt[o.eng] += 1
                o.sigidx = cnt[o.eng]
        with contextlib.ExitStack() as st:
            esem = {e: st.enter_context(nc.semaphore("s_" + e)) for e in ("pe", "act", "dve", "pool")}
            dsem = {}
            for q, K in self.n_dma_sems.items():
                for i in range(K):
                    dsem[(q, i)] = st.enter_context(nc.semaphore("d_%s_%d" % (q, i)))
            block = st.enter_context(nc.Block())
            by_eng = {e: [o for o in ops if o.eng == e] for e in self.ENGS}
            final_dma = {}
            for q in self.dma_ops:
                for o in self.dma_ops[q]:
                    final_dma[o.sem] = max(final_dma.get(o.sem, 0), o.semval)

            def run(engname, e):
                seen = {}
                for o in by_eng[engname]:
                    waits = {}
                    for d in o.deps:
                        if d.is_dma:
                            kk, v = d.sem, d.semval
                        else:
                            kk, v = d.eng, d.sigidx
                        if seen.get(kk, 0) >= v:
                            continue
                        if waits.get(kk, 0) < v:
                            waits[kk] = v
                    if o.prev_sem_wait is not None:
                        kk, v = o.prev_sem_wait
                        if seen.get(kk, 0) < v and waits.get(kk, 0) < v:
                            waits[kk] = v
                    wl = list(waits.items())
                    attach = None
                    if wl and ATTACH_WAIT:
                        attach = wl.pop()
                    for kk, v in wl:
                        e.wait_ge(dsem[kk] if isinstance(kk, tuple) else esem[kk], v)
                        seen[kk] = v
                    ins = o.fn(e)
                    if attach is not None:
                        kk, v = attach
                        ins._wait_ge(dsem[kk] if isinstance(kk, tuple) else esem[kk], v)
                        seen[kk] = v
                    if o.is_dma:
                        ins.then_inc(dsem[o.sem], 16)
                    elif o.sig:
                        ins.then_inc(esem[o.eng], 1)
                if engname == "sp":
                    for kk, v in final_dma.items():
                        if seen.get(kk, 0) < v:
                            e.wait_ge(dsem[kk], v)

            block.tensor(lambda e: run("pe", e))
            block.scalar(lambda e: run("act", e))
            block.vector(lambda e: run("dve", e))
            block.gpsimd(lambda e: run("pool", e))
            block.sync(lambda e: run("sp", e))


class Arena:
    def __init__(self, nc, lo=20480, hi=229376):
        self.nc = nc
        self.p = lo
        self.hi = hi
        self.cnt = 0
        self.peak = lo

    def alloc(self, name, shape, dt=F32):
        nb = int(np.prod(shape[1:])) * (4 if dt == F32 else 2)
        nb = (nb + 63) // 64 * 64
        self.cnt += 1
        t = self.nc.alloc_sbuf_tensor_at("%s_%d" % (name, self.cnt), list(shape), dt, offset=self.p)
        self.p += nb
        self.peak = max(self.peak, self.p)
        assert self.p <= self.hi, "SBUF overflow at %s: %d" % (name, self.p)
        return t

    def mark(self):
        return self.p

    def reset(self, m):
        self.p = m


def _nm(x):
    return x.name


def PK(a):
    nm = a.name
    if not nm.startswith("ps"):
        return [nm]
    if not hasattr(a, "ap"):
        return [(nm, 0), (nm, 1)]
    dims = a.ap
    row = dims[0][0] if len(dims) > 1 else 1024
    fo = a.offset % row if row else 0
    ext = 1
    for st_, cn in dims[1:]:
        ext += abs(st_) * (cn - 1)
    return [(nm, b) for b in range(fo // 512, (fo + ext - 1) // 512 + 1)]


class KB:
    def __init__(self, nc, P):
        self.nc = nc
        self.P = P
        self.ident = None
        self.r32 = set()

    def reg32(self, *ts):
        for t in ts:
            self.r32.add(t.name)

    def _o(self, out):
        if out.name in self.r32 and out.dtype == F32:
            return out.bitcast(F32R)
        return out

    @staticmethod
    def _rw(outs, ins, r, w):
        if r is None:
            reads = [kk for a in ins if a is not None and not isinstance(a, (int, float)) for kk in PK(a)]
        else:
            reads = [kk for x in r for kk in (x if isinstance(x, list) else [x])]
        if w is None:
            writes = [kk for a in outs for kk in PK(a)]
        else:
            writes = [kk for x in w for kk in (x if isinstance(x, list) else [x])]
        def _exp(lst):
            o = []
            for x in lst:
                if isinstance(x, str) and x.startswith("ps"):
                    o += [(x, 0), (x, 1)]
                else:
                    o.append(x)
            return o
        reads, writes = _exp(reads), _exp(writes)
        for x in reads:
            if isinstance(x, tuple) and isinstance(x[0], str) and x[0].startswith("ps") and len(x) == 2 \
                    and isinstance(x[1], int) and x not in writes:
                writes.append(x)
        return reads, writes

    def mm(self, out, lhsT, rhs, start=True, stop=True, r=None, w=None):
        reads, writes = self._rw([out], [lhsT, rhs], r, w)
        if lhsT.name in self.r32 and rhs.name in self.r32 and lhsT.dtype == F32 and rhs.dtype == F32:
            lhsT, rhs = lhsT.bitcast(F32R), rhs.bitcast(F32R)
        self.P.op("pe", lambda e: e.matmul(out, lhsT, rhs, start=start, stop=stop), reads, writes)

    def tr(self, out, in_, r=None, w=None):
        n = in_.shape[0]
        idn = self.ident[0:n, 0:n]
        reads, writes = self._rw([out], [in_, idn], r, w)
        self.P.op("pe", lambda e: e.transpose(out=out, in_=in_, identity=idn), reads, writes)

    def act(self, out, in_, func, bias=None, scale=None, r=None, w=None, eng="act"):
        kw = {}
        if bias is not None:
            kw["bias"] = bias
        if scale is not None:
            kw["scale"] = scale
        reads, writes = self._rw([out], [in_, bias, scale], r, w)
        out = self._o(out)
        self.P.op("act", lambda e: e.activation(out=out, in_=in_, func=func, **kw), reads, writes)

    def tt(self, out, a, b, op, eng="dve", r=None, w=None):
        reads, writes = self._rw([out], [a, b], r, w)
        out = self._o(out)
        self.P.op(eng, lambda e: e.tensor_tensor(out=out, in0=a, in1=b, op=op), reads, writes)

    def ts(self, out, a, s1, s2, op0, op1=None, eng="dve", r=None, w=None):
        reads, writes = self._rw([out], [a, s1, s2], r, w)
        out = self._o(out)
        if op1 is None:
            self.P.op(eng, lambda e: e.tensor_scalar(out=out, in0=a, scalar1=s1, scalar2=None, op0=op0), reads, writes)
        else:
            self.P.op(eng, lambda e: e.tensor_scalar(out=out, in0=a, scalar1=s1, scalar2=s2, op0=op0, op1=op1), reads, writes)

    def stt(self, out, a, scalar, b, op0, op1, eng="dve", r=None, w=None):
        reads, writes = self._rw([out], [a, scalar, b], r, w)
        out = self._o(out)
        self.P.op(eng, lambda e: e.scalar_tensor_tensor(out=out, in0=a, scalar=scalar, in1=b, op0=op0, op1=op1), reads, writes)

    def cp(self, out, in_, eng="dve", r=None, w=None):
        reads, writes = self._rw([out], [in_], r, w)
        out = self._o(out)
        if eng == "act":
            self.P.op("act", lambda e: e.activation(out=out, in_=in_, func=AF.Copy), reads, writes)
        else:
            self.P.op(eng, lambda e: e.tensor_copy(out=out, in_=in_), reads, writes)

    def memset(self, out, val, eng="dve", w=None):
        reads, writes = self._rw([out], [], None, w)
        self.P.op(eng, lambda e: e.memset(out, val), reads, writes)

    def red(self, out, in_, op=ALU.add, eng="dve", r=None, w=None):
        reads, writes = self._rw([out], [in_], r, w)
        self.P.op(eng, lambda e: e.tensor_reduce(out=out, in_=in_, axis=mybir.AxisListType.X, op=op), reads, writes)

    def dma(self, q, out, in_, r=None, w=None, **kw):
        reads, writes = self._rw([out], [in_], r, w)
        self.P.dma(q, out, in_, reads, writes, **kw)


def bc(ap, shape):
    return ap.to_broadcast(list(shape))


class SlotRot:
    width = 512

    def __init__(self, views):
        self.t = list(views)
        self.i = 0

    def next(self):
        t = self.t[self.i % len(self.t)]
        self.i += 1
        return t


class PsumRot:
    width = 1024

    def __init__(self, nc, n=4):
        self.t = [nc.alloc_psum_tensor("ps%d" % i, [128, 1024], F32) for i in range(n)]
        self.i = 0

    def next(self):
        t = self.t[self.i % len(self.t)]
        self.i += 1
        return t


class Pool4:
    def __init__(self, A, n, name="w", width=1024):
        self.free_ = [A.alloc("%s%d" % (name, i), [128, width]) for i in range(n)]

    def get(self):
        assert self.free_, "scratch pool exhausted"
        return self.free_.pop(0)

    def free(self, *ts):
        for t in ts:
            self.free_.append(t)


def load_weight_bf16(k, dst, src, nk, ncols, colblk=2048):
    c0 = 0
    while c0 < ncols:
        c1 = min(ncols, c0 + colblk)
        for kc in range(nk):
            k.dma("pool", dst[:, kc, c0:c1], src[kc * 128:(kc + 1) * 128, c0:c1],
                  w=[(dst.name, kc, c0 // colblk)])
        c0 = c1


def wres(dst, nk, ncols, colblk=2048):
    out = []
    for kc in range(nk):
        for cb in range((ncols + colblk - 1) // colblk):
            out.append((dst.name, kc, cb))
    return out


def to_feature_major(k, PS, dst3, src_tok, n, nchunks, evac="act", wkey=None):
    slot = 128 if n > 16 else 16
    c = 0
    while c < nchunks:
        g = min(nchunks - c, PS.width // slot)
        ps = PS.next()
        for i in range(g):
            k.tr(ps[:, i * slot:i * slot + n], src_tok[0:n, (c + i) * 128:(c + i + 1) * 128])
        k.cp(dst3[:, c:c + g, :], ps[:, 0:g * slot].rearrange("p (g s) -> p g s", s=slot)[:, :, 0:n], eng=evac,
             w=(None if wkey is None else [x for cc in range(c, c + g) for x in wkey(cc)]))
        c += g


def to_token_major(k, PS, dst_tok, src3, n, nchunks, evac="act", rkey=None):
    c = 0
    while c < nchunks:
        g = min(nchunks - c, PS.width // 128)
        ps = PS.next()
        for i in range(g):
            k.tr(ps[0:n, i * 128:(i + 1) * 128], src3[:, c + i, :],
                 r=(None if rkey is None else rkey(c + i) + [k.ident.name]))
        k.cp(dst_tok[0:n, c * 128:(c + g) * 128], ps[0:n, 0:g * 128], eng=evac)
        c += g


def layer_norm_out(k, A, res, n, g_bc, b_bc, out_tile, stats, mv, rstd):
    for i in range(2):
        k.P.op("dve", lambda e, i=i: e.bn_stats(out=stats[0:n, i, :], in_=res[0:n, i * 512:(i + 1) * 512]),
               [res.name], [(stats.name, i)])
    k.P.op("dve", lambda e: e.bn_aggr(out=mv[0:n, :], in_=stats[0:n, :, :]),
           [(stats.name, 0), (stats.name, 1)], [mv.name])
    k.act(rstd[0:n, :], mv[0:n, 1:2], AF.Ln, bias=k.eps5[0:n, :], scale=1.0)
    k.act(rstd[0:n, :], rstd[0:n, :], AF.Exp, scale=-0.5)
    k.ts(res[0:n, 0:1024], res[0:n, 0:1024], mv[0:n, 0:1], rstd[0:n, 0:1], ALU.subtract, ALU.mult)
    k.tt(res[0:n, 0:1024], res[0:n, 0:1024], g_bc[0:n, :], ALU.mult)
    k.tt(out_tile[0:n, 0:1024], res[0:n, 0:1024], b_bc[0:n, :], ALU.add)


def conv_taps(k, out3, views, wb3, nch, res_out, res_in, nk):
    for c in range(nch):
        k.act(out3[:, c, :], views[0][:, c, :], AF.Identity, bias=wb3[:, c, nk:nk + 1], scale=wb3[:, c, 0:1],
              r=[(res_in, c), wb3.name], w=[(res_out, c)])
    for kk in range(1, nk):
        for c in range(nch):
            k.stt(out3[:, c, :], views[kk][:, c, :], wb3[:, c, kk:kk + 1], out3[:, c, :], ALU.mult, ALU.add,
                  r=[(res_in, c), (res_out, c), wb3.name], w=[(res_out, c)])


def load_small_params(k, PS, A, conv_w, conv_b, nk, nch, name):
    wb3 = A.alloc(name + "_wb3", [128, nch, nk + 1])
    m = A.mark()
    tok = A.alloc(name + "_tok", [nk + 1, nch * 128])
    k.dma("sp", tok[0:nk, :], conv_w)
    k.dma("sp", tok[nk:nk + 1, :], conv_b)
    to_feature_major(k, PS, wb3, tok, nk + 1, nch, evac="dve")
    k.P.barrier()
    A.reset(m)
    return wb3


def proj_feature_major(k, PS, W, wname, XT, ntok, fc_list, evac_fn, colblk=2048):
    slot = 128 if ntok > 16 else 16
    per = PS.width // slot if slot == 128 else 8
    i = 0
    while i < len(fc_list):
        grp = fc_list[i:i + per]
        ps = PS.next()
        for j, fc in enumerate(grp):
            for kc in range(8):
                k.mm(ps[:, j * slot:j * slot + ntok], W[:, kc, fc * 128:(fc + 1) * 128], XT[:, kc, 0:ntok],
                     start=(kc == 0), stop=(kc == 7),
                     r=[(wname, kc, (fc * 128) // colblk), XT.name], w=PK(ps[:, j * slot:j * slot + ntok]))
        view = ps[:, 0:len(grp) * slot].rearrange("p (g s) -> p g s", s=slot)[:, :, 0:ntok]
        evac_fn(i, grp, ps, view)
        i += per


def phase_gdn(C):
    nc, k, P, A, PS, dr = C.nc, C.k, C.P, C.A, C.PS, C.dr
    cst = C.cst
    IDENT, ONES, U, UBLK, MASKL, MASKUS, MASKUI, CH0, CH1 = (cst[n] for n in
                                                             ("IDENT", "ONES", "U", "UBLK", "MASKL", "MASKUS", "MASKUI", "CH0", "CH1"))
    m_phase = A.mark()
    WIN = A.alloc("gwin", [128, 8, GDN_IN], BF16)
    WOUT = A.alloc("gwout", [128, 8, 1024], BF16)
    NORMW = A.alloc("gnormw", [128, 1])
    k.dma("sp", NORMW[:], dr["gdn_norm_w"].rearrange("o p -> p o"))
    NWBC = A.alloc("gnwbc", [128, 128])
    k.dma("sp", NWBC[:], dr["gdn_norm_w"].partition_broadcast(128))
    NEGA = A.alloc("gnega", [128, 8])
    DTB = A.alloc("gdtb", [128, 8])
    k.dma("sp", NEGA[:], dr["gdn_a_log"].partition_broadcast(128))
    k.dma("sp", DTB[:], dr["gdn_dt_bias"].partition_broadcast(128))
    k.act(NEGA[:], NEGA[:], AF.Exp)
    k.ts(NEGA[:], NEGA[:], -1.0, None, ALU.mult)
    LNG = A.alloc("ln1g", [128, 1024])
    LNB_ = A.alloc("ln1b", [128, 1024])
    k.dma("sp", LNG[:], dr["ln1_g"][0:1, :].partition_broadcast(128))
    k.dma("sp", LNB_[:], dr["ln1_b"][0:1, :].partition_broadcast(128))
    stats = A.alloc("stats", [128, 2, 6])
    mv = A.alloc("mv", [128, 2])
    rstd = A.alloc("rstd", [128, 1])
    CWB = load_small_params(k, PS, A, dr["gdn_conv_w"], dr["gdn_conv_b"], 4, 24, "gcw")
    load_weight_bf16(k, WIN, dr["gdn_w_in"], 8, GDN_IN)
    load_weight_bf16(k, WOUT, dr["gdn_w_out"], 8, 1024)
    m_work = A.mark()

    XTOK = [A.alloc("xtok", [128, 1024]) for _ in range(2)]
    XT = A.alloc("xT", [128, 8, 128], BF16)
    PRE = A.alloc("pre", [128, 24, 131])
    QKV_ = [A.alloc("qkv", [128, 24, 128]) for _ in range(2)]
    ZT_ = [A.alloc("zT", [128, 8, 128]) for _ in range(2)]
    GT_ = [A.alloc("gt", [128, 16]) for _ in range(2)]
    SM_ = [A.alloc("sm", [128, 12, 8]) for _ in range(2)]
    GC_ = [A.alloc("gc", [128, 16]) for _ in range(2)]
    S2 = [A.alloc("S", [128, 4, 128]) for _ in range(2)]
    VN2 = [A.alloc("vnew", [128, 4, 128]) for _ in range(2)]
    OG = A.alloc("og", [128, 8, 128], BF16)
    WP = Pool4(A, 22, "gw", width=512)
    k.reg32(*WP.free_)
    k.reg32(*(QKV_ + S2 + VN2))
    k.memset(PRE[:, :, 0:3], 0.0, w=[(PRE.name, c) for c in range(24)])
    for x_ in S2 + VN2:
        k.ts(x_[:], ONES[:].unsqueeze(1).to_broadcast([128, 4, 128]), 0.0, None, ALU.mult)
    pst = C.PS.t + [C.psO]
    PSF = SlotRot([pst[0][:, 0:512], pst[0][:, 512:1024]])
    PSH = [SlotRot([pst[1][:, 0:512], pst[1][:, 512:1024]]), SlotRot([pst[2][:, 0:512], pst[2][:, 512:1024]])]
    PSO = [pst[3][:, 0:512], pst[3][:, 512:1024]]
    PS_full = PS
    PS = PSF
    PREk = [(PRE.name, c) for c in range(24)]

    def v3(t):
        return t[:].rearrange("p (h j) -> p h j", h=8) if len(t.shape) == 2 else t[:]

    def hb(ap):
        return ap.unsqueeze(2).to_broadcast([128, 8, 128])

    def mb(ap):
        return ap.unsqueeze(1).to_broadcast([128, 8, 128])

    def front(t, xtok, QKV, ZT, GT, SM, GC):
        QKVk = [(QKV.name, c) for c in range(24)]
        qk_, kk_, vk_ = QKVk[0:8], QKVk[8:16], QKVk[16:24]
        BETA, LNBETA, G8, GPL, EGC, DEC, BG, DECA, DECB, T8, E8 = (SM[:, i, :] for i in range(11))
        gcum, glast = GC[:, 0:8], GC[:, 8:16]
        k.dma("sp", xtok[:], dr["xp"][t * 128:(t + 1) * 128, :])
        to_feature_major(k, PS, XT, xtok, 128, 8)

        def evac(i, grp, ps, view):
            if grp[0] < 24:
                k.cp(PRE[:, grp[0]:grp[0] + len(grp), 3:131], view, eng="act",
                     r=PK(view), w=[(PRE.name, c) for c in grp])
            else:
                k.act(ZT[:, grp[0] - 24:grp[0] - 24 + len(grp), :], view, AF.Silu)
        yield
        for g0 in range(0, 32, 4):
            proj_feature_major(k, PS, WIN, WIN.name, XT, 128, list(range(g0, g0 + 4)), evac)
            yield
        psg = PS.next()
        for kc in range(8):
            k.mm(psg[:, 0:16], XT[:, kc, :], WIN[:, kc, 4096:4112], start=(kc == 0), stop=(kc == 7),
                 r=[(WIN.name, kc, 2), XT.name], w=[psg.name])
        k.cp(GT[:], psg[:, 0:16])
        yield
        for c in range(24):
            k.act(QKV[:, c, :], PRE[:, c, 0:128], AF.Identity, bias=CWB[:, c, 4:5], scale=CWB[:, c, 0:1],
                  r=[(PRE.name, c), CWB.name], w=[(QKV.name, c)])
            if c % 8 == 7:
                yield
        for kk in range(1, 4):
            for c in range(24):
                k.stt(QKV[:, c, :], PRE[:, c, kk:kk + 128], CWB[:, c, kk:kk + 1], QKV[:, c, :], ALU.mult, ALU.add,
                      r=[(PRE.name, c), (QKV.name, c), CWB.name], w=[(QKV.name, c)])
                if c % 6 == 5:
                    yield
        k.act(QKV[:], QKV[:], AF.Silu, r=QKVk, w=QKVk)
        if t == C.n_prompt_tiles - 1:
            for pc in range(6):
                TL = WP.get()
                to_token_major(k, PS, TL, PRE[:, pc * 4:(pc + 1) * 4, 128:131], 3, 4,
                               rkey=lambda c, pc=pc: [(PRE.name, pc * 4 + c)])
                k.dma("pool", dr["gcp"][:, pc * 512:(pc + 1) * 512], TL[0:3, :])
                WP.free(TL)
        k.cp(PRE[:, :, 0:3], PRE[:, :, 128:131], eng="dve", r=PREk, w=PREk)

        for which in range(2):
            for hf in range(2):
                c0 = which * 8 + hf * 4
                keys = QKVk[c0:c0 + 4]
                SQ = WP.get()
                sq3 = SQ[:].rearrange("p (h j) -> p h j", h=4)
                k.act(sq3, QKV[:, c0:c0 + 4, :], AF.Square, r=keys, w=[SQ.name])
                psq = PS.next()
                k.mm(psq, ONES[:], SQ[:])
                if which == 0:
                    k.act(SQ[:], psq, AF.Ln, bias=C.epsq[:], scale=128.0)
                else:
                    k.act(SQ[:], psq, AF.Ln, bias=C.eps6[:], scale=1.0)
                k.act(SQ[:], SQ[:], AF.Exp, scale=-0.5)
                k.tt(QKV[:, c0:c0 + 4, :], QKV[:, c0:c0 + 4, :], sq3, ALU.mult, r=keys + [SQ.name], w=keys)
                WP.free(SQ)
                yield

        yield
        k.act(E8, GT[:, 0:8], AF.Exp, scale=-1.0, r=[GT.name], w=[(SM.name, 10)])
        k.act(LNBETA, E8, AF.Ln, bias=C.one[:], scale=1.0, r=[(SM.name, 10), C.one.name], w=[(SM.name, 1)])
        k.ts(LNBETA, LNBETA, -1.0, None, ALU.mult, r=[(SM.name, 1)], w=[(SM.name, 1)])
        k.act(BETA, LNBETA, AF.Exp, r=[(SM.name, 1)], w=[(SM.name, 0)])
        k.tt(T8, GT[:, 8:16], DTB[:], ALU.add, r=[GT.name, DTB.name], w=[(SM.name, 9)])
        k.ts(T8, T8, 60.0, None, ALU.min, r=[(SM.name, 9)], w=[(SM.name, 9)])
        k.act(T8, T8, AF.Exp, r=[(SM.name, 9)], w=[(SM.name, 9)])
        k.act(T8, T8, AF.Ln, bias=C.one[:], scale=1.0, r=[(SM.name, 9), C.one.name], w=[(SM.name, 9)])
        k.tt(G8, T8, NEGA[:], ALU.mult, r=[(SM.name, 9), NEGA.name], w=[(SM.name, 2)])
        psc = PS.next()
        k.mm(psc[:, 0:8], U[:], G8, r=[U.name, (SM.name, 2)], w=[psc.name])
        k.mm(psc[:, 8:16], UBLK[:], G8, r=[UBLK.name, (SM.name, 2)], w=[psc.name])
        k.cp(GC[:], psc[:, 0:16])
        gcum, glast = GC[:, 0:8], GC[:, 8:16]
        k.tt(GPL, gcum, LNBETA, ALU.add, r=[GC.name, (SM.name, 1)], w=[(SM.name, 3)])
        k.act(EGC, gcum, AF.Exp, r=[GC.name], w=[(SM.name, 4)])
        k.tt(DEC, glast, gcum, ALU.subtract, r=[GC.name], w=[(SM.name, 5)])
        k.act(DEC, DEC, AF.Exp, r=[(SM.name, 5)], w=[(SM.name, 5)])
        k.tt(BG, BETA, EGC, ALU.mult, r=[(SM.name, 0), (SM.name, 4)], w=[(SM.name, 6)])
        k.ts(DECA, DEC, CH0[:, 0:1], None, ALU.mult, r=[(SM.name, 5), CH0.name], w=[(SM.name, 7)])
        k.ts(DECB, DEC, CH1[:, 0:1], None, ALU.mult, r=[(SM.name, 5), CH1.name], w=[(SM.name, 8)])


    def back_half(hf, t, xtok, QKV, ZT, GT, SM, GC):
        QKVk = [(QKV.name, c) for c in range(24)]
        qk_, kk_, vk_ = QKVk[0:8], QKVk[8:16], QKVk[16:24]
        BETA, LNBETA, G8, GPL, EGC, DEC, BG, DECA, DECB, T8, E8 = (SM[:, i, :] for i in range(11))
        gcum, glast = GC[:, 0:8], GC[:, 8:16]
        H0 = 4 * hf
        hsl = slice(H0, H0 + 4)
        PSh = PSH[hf]
        psO = PSO[hf]
        Sh, VNh = S2[hf], VN2[hf]
        qk4, kk4, vk4 = qk_[H0:H0 + 4], kk_[H0:H0 + 4], vk_[H0:H0 + 4]

        def hb4(ap):
            return ap[:, hsl].unsqueeze(2).to_broadcast([128, 4, 128])

        def mb4(ap):
            return ap.unsqueeze(1).to_broadcast([128, 4, 128])

        def v4(x):
            return (x[:] if hasattr(x, "shape") and len(x.shape) == 2 and not hasattr(x, "offset") else x).rearrange(
                "p (h j) -> p h j", h=4)

        def hsl_(h):
            return slice(h * 128, (h + 1) * 128)
        X, X2 = WP.get(), WP.get()
        k.tt(v4(X), mb4(U[:]), hb4(G8), ALU.mult, r=[U.name, (SM.name, 2)], w=[X.name])
        k.tt(v4(X2), mb4(IDENT[:]), hb4(LNBETA), ALU.mult, r=[IDENT.name, (SM.name, 1)], w=[X2.name])
        k.tt(X2[:], X2[:], X[:], ALU.add)
        psR, psR2 = PSh.next(), PSh.next()
        k.mm(psR, ONES[:], X[:])
        k.mm(psR2, ONES[:], X2[:])
        WP.free(X, X2)
        yield
        ER, GL, GUS, GUI = WP.get(), WP.get(), WP.get(), WP.get()
        k.act(ER[:], psR, AF.Exp)
        k.stt(v4(GL), v4(psR), -1.0, mb4(MASKL[:]), ALU.mult, ALU.add)
        k.tt(v4(GL), v4(GL), hb4(GPL), ALU.add, r=[GL.name, (SM.name, 3)], w=[GL.name])
        k.act(GL[:], GL[:], AF.Exp)
        k.tt(v4(GUS), v4(psR2), mb4(MASKUS[:]), ALU.add)
        k.tt(v4(GUS), v4(GUS), hb4(gcum), ALU.subtract, r=[GUS.name, GC.name], w=[GUS.name])
        k.act(GUS[:], GUS[:], AF.Exp)
        k.tt(v4(GUI), v4(psR), mb4(MASKUI[:]), ALU.add)
        k.tt(v4(GUI), v4(GUI), hb4(gcum), ALU.subtract, r=[GUI.name, GC.name], w=[GUI.name])
        k.act(GUI[:], GUI[:], AF.Exp)
        yield
        psK, psQ = PSh.next(), PSh.next()
        for h in range(4):
            k.mm(psK[:, hsl_(h)], QKV[:, 8 + H0 + h, :], QKV[:, 8 + H0 + h, :], r=[kk4[h]], w=PK(psK[:, hsl_(h)]))
        for h in range(4):
            k.mm(psQ[:, hsl_(h)], QKV[:, 8 + H0 + h, :], QKV[:, H0 + h, :], r=[kk4[h], qk4[h]], w=PK(psQ[:, hsl_(h)]))
        NN, AA, QKT = WP.get(), WP.get(), WP.get()
        k.stt(RR(NN[:]), psK, -1.0, GL[:], ALU.mult, ALU.mult)
        k.stt(RR(AA[:]), psK, -1.0, GUS[:], ALU.mult, ALU.mult)
        k.tt(QKT[:], psQ, GUI[:], ALU.mult)
        WP.free(GL, GUS, GUI)
        QQ = WP.get()
        k.tt(RR(v4(QQ)), v4(AA), mb4(IDENT[:]), ALU.add)
        yield
        for lvl in range(1, 6):
            NN2 = WP.get()
            psn = PSh.next()
            for h in range(4):
                k.mm(psn[:, hsl_(h)], RR(AA[:, hsl_(h)]), RR(NN[:, hsl_(h)]))
            k.cp(RR(NN2[:]), psn, eng="act")
            if lvl < 5:
                AA2 = WP.get()
                psa = PSh.next()
                for h in range(4):
                    k.mm(psa[:, hsl_(h)], RR(NN[:, hsl_(h)]), RR(AA[:, hsl_(h)]))
                k.cp(RR(AA2[:]), psa, eng="act")
            yield
            QQ2 = WP.get()
            psq2 = PSh.next()
            for h in range(4):
                k.mm(psq2[:, hsl_(h)], RR(NN2[:, hsl_(h)]), RR(QQ[:, hsl_(h)]))
            k.tt(RR(QQ2[:]), QQ[:], psq2, ALU.add)
            WP.free(NN, QQ)
            NN, QQ = NN2, QQ2
            if lvl < 5:
                WP.free(AA)
                AA = AA2
            yield
        WP.free(NN, AA)
        psT1, psT2 = PSh.next(), PSh.next()
        for h in range(4):
            k.tr(psT1[:, hsl_(h)], QKV[:, 8 + H0 + h, :], r=[kk4[h], IDENT.name], w=PK(psT1[:, hsl_(h)]))
        for h in range(4):
            k.tr(psT2[:, hsl_(h)], QKV[:, 16 + H0 + h, :], r=[vk4[h], IDENT.name], w=PK(psT2[:, hsl_(h)]))
        VB, KBG, KDA, KDB = WP.get(), WP.get(), WP.get(), WP.get()
        k.tt(v4(VB), v4(psT2), hb4(BETA), ALU.mult, r=PK(psT2) + [(SM.name, 0)], w=[VB.name])
        k.tt(v4(KBG), v4(psT1), hb4(BG), ALU.mult, r=PK(psT1) + [(SM.name, 6)], w=[KBG.name])
        k.tt(v4(KDA), v4(psT1), hb4(DECA), ALU.mult, r=PK(psT1) + [(SM.name, 7)], w=[KDA.name])
        k.tt(v4(KDB), v4(psT1), hb4(DECB), ALU.mult, r=PK(psT1) + [(SM.name, 8)], w=[KDB.name])
        yield
        psU, psW = PSh.next(), PSh.next()
        for h in range(4):
            k.mm(psU[:, hsl_(h)], QQ[:, hsl_(h)], VB[:, hsl_(h)])
        for h in range(4):
            k.mm(psW[:, hsl_(h)], KBG[:, hsl_(h)], QQ[:, hsl_(h)])
        UU, WT, QD = WP.get(), WP.get(), WP.get()
        k.cp(UU[:], psU, eng="act")
        k.cp(WT[:], psW, eng="act")
        WP.free(QQ, VB, KBG)
        k.tt(v4(QD), QKV[:, H0:H0 + 4, :], v4(ER), ALU.mult, r=qk4 + [ER.name], w=[QD.name])
        TMP = WP.get()
        yield
        for c in range(2):
            rows = slice(c * 64, (c + 1) * 64)
            psWS = PSh.next()
            for h in range(4):
                k.mm(psWS[:, hsl_(h)], WT[:, hsl_(h)], Sh[:, h, :])
            k.tt(VNh[rows].rearrange("p h v -> p (h v)"), UU[rows, :], psWS[rows, :], ALU.subtract)
            for h in range(4):
                cs = slice(h * 128 + c * 64, h * 128 + c * 64 + 64)
                k.mm(psO[:, cs], Sh[:, h, :], QD[:, cs], start=True, stop=False)
                k.mm(psO[:, cs], VNh[:, h, :], QKT[:, cs], start=False, stop=True)
            yield
            psD = PSh.next()
            KD = KDA if c == 0 else KDB
            for h in range(4):
                k.mm(psD[:, hsl_(h)], KD[:, hsl_(h)], VNh[:, h, :])
            last = c * 64 + 63
            k.tt(v4(TMP), Sh[:], v4(ER)[:, :, last:last + 1].to_broadcast([128, 4, 128]), ALU.mult)
            k.tt(Sh[:].rearrange("p h v -> p (h v)"), TMP[:], psD, ALU.add)
            yield
        WP.free(UU, WT, QD, QKT, KDA, KDB, ER)
        OT, OSQ = WP.get(), WP.get()
        k.cp(OT[:], psO, eng="act")
        k.act(OSQ[:], psO, AF.Square)
        psS = PSh.next()
        k.mm(psS, ONES[:], OSQ[:])
        k.act(OSQ[:], psS, AF.Ln, bias=C.eps6[:], scale=1.0 / 128.0)
        k.act(OSQ[:], OSQ[:], AF.Exp, scale=-0.5)
        k.tt(OT[:], OT[:], OSQ[:], ALU.mult)
        k.stt(OG[:, hsl, :].rearrange("p h j -> p (h j)"), OT[:], NORMW[:, 0:1],
              ZT[:, hsl, :].rearrange("p h j -> p (h j)"), ALU.mult, ALU.mult,
              r=[OT.name, NORMW.name, ZT.name], w=[(OG.name, hf)])
        WP.free(OSQ, TMP, OT)


    def join(t, xtok):
        RES = WP.get()
        RES2 = WP.get()
        for nb in range(2):
            psY = PSF.next()
            for h in range(8):
                k.mm(psY, OG[:, h, :], WOUT[:, h, nb * 512:(nb + 1) * 512],
                     start=(h == 0), stop=(h == 7), r=[(OG.name, h // 4), (WOUT.name, h, 0)], w=PK(psY))
            R_ = RES if nb == 0 else RES2
            k.stt(R_[:], xtok[:, nb * 512:(nb + 1) * 512], ALPHA, psY, ALU.mult, ALU.add)
        for i, R_ in enumerate((RES, RES2)):
            k.P.op("dve", lambda e, i=i, R_=R_: e.bn_stats(out=stats[:, i, :], in_=R_[:]), [R_.name], [(stats.name, i)])
        k.P.op("dve", lambda e: e.bn_aggr(out=mv[:], in_=stats[:]), [(stats.name, 0), (stats.name, 1)], [mv.name])
        k.act(rstd[:], mv[:, 1:2], AF.Ln, bias=C.eps5[:], scale=1.0)
        k.act(rstd[:], rstd[:], AF.Exp, scale=-0.5)
        for i, R_ in enumerate((RES, RES2)):
            cs = slice(i * 512, (i + 1) * 512)
            k.ts(R_[:], R_[:], mv[:, 0:1], rstd[:, 0:1], ALU.subtract, ALU.mult)
            k.tt(R_[:], R_[:], LNG[:, cs], ALU.mult)
            k.tt(R_[:], R_[:], LNB_[:, cs], ALU.add)
            k.dma("pool", dr["x1"][t * 128:(t + 1) * 128, cs], R_[:])
        WP.free(RES, RES2)

    def bufs(t):
        p = t % 2
        return (XTOK[p], QKV_[p], ZT_[p], GT_[p], SM_[p], GC_[p])

    def run_all(gens):
        alive = [True] * len(gens)

        def step(gi):
            if alive[gi]:
                try:
                    next(gens[gi])
                except StopIteration:
                    alive[gi] = False
        while any(alive):
            if len(gens) == 3:
                for gi in (2, 0, 2, 1):
                    step(gi)
            else:
                for gi in range(len(gens)):
                    step(gi)

    NTL = C.n_prompt_tiles
    if NTL > 0:
        run_all([front(0, *bufs(0))])
    for t in range(NTL):
        gens = [back_half(0, t, *bufs(t)), back_half(1, t, *bufs(t))]
        if t + 1 < NTL:
            gens.append(front(t + 1, *bufs(t + 1)))
        run_all(gens)
        join(t, XTOK[t % 2])

    for hf in range(2):
        k.dma("pool", dr["gsp"][4 * hf:4 * hf + 4].rearrange("h k v -> k h v"), S2[hf][:])
    C.gdn = dict(WIN=WIN, WOUT=WOUT, CWB=CWB, NORMW=NORMW, NWBC=NWBC, NEGA=NEGA, DTB=DTB, LNG=LNG, LNB=LNB_,
                 stats=stats, mv=mv, rstd=rstd, m_work=m_work, m_phase=m_phase)


def phase_gdn_sample(C):
    nc, k, P, A, PS, dr = C.nc, C.k, C.P, C.A, C.PS, C.dr
    cst = C.cst
    IDENT, ONES = cst["IDENT"], cst["ONES"]
    g = C.gdn
    WIN, WOUT, CWB, NWBC, NEGA, DTB = g["WIN"], g["WOUT"], g["CWB"], g["NWBC"], g["NEGA"], g["DTB"]
    P.barrier()
    A.reset(g["m_work"])
    n = NS
    XS = A.alloc("xs_tok", [n, 1024])
    XT = A.alloc("xsT", [128, 8, n], BF16)
    PRES = A.alloc("pres", [128, 24, 4, n])
    QKVs = A.alloc("qkvs", [128, 24, n])
    ZTs = A.alloc("zts", [128, 8, n])
    GTs = A.alloc("gts", [n, 16])
    CG = [A.alloc("cg", [n, 3072])]
    k.dma("sp", XS[:], dr["xs"])
    to_feature_major(k, PS, XT, XS, n, 8)
    for r_ in range(3):
        cg = CG[0]
        k.dma("sp", cg[:], dr["cgc"][:, r_, :])
        to_feature_major(k, PS, PRES[:, :, r_, :], cg, n, 24)

    def evac(i, grp, ps, view):
        if grp[0] < 24:
            k.cp(PRES[:, grp[0]:grp[0] + len(grp), 3, :], view, eng="act")
        else:
            k.act(ZTs[:], view, AF.Silu)
    proj_feature_major(k, PS, WIN, WIN.name, XT, n, list(range(32)), evac)
    psg = PS.next()
    for kc in range(8):
        k.mm(psg[0:n, 0:16], XT[:, kc, :], WIN[:, kc, 4096:4112], start=(kc == 0), stop=(kc == 7),
             r=[(WIN.name, kc, 2), XT.name], w=[psg.name])
    k.cp(GTs[:], psg[0:n, 0:16])
    k.dma("pool", dr["gcs"][:, 0:2, :], dr["cgc"][:, 1:3, :])
    NEWR = CG[0]
    to_token_major(k, PS, NEWR, PRES[:, :, 3, :], n, 24)
    k.dma("pool", dr["gcs"][:, 2, :], NEWR[:])
    for c in range(24):
        k.act(QKVs[:, c, :], PRES[:, c, 0, :], AF.Identity, bias=CWB[:, c, 4:5], scale=CWB[:, c, 0:1])
    for kk in range(1, 4):
        for c in range(24):
            k.stt(QKVs[:, c, :], PRES[:, c, kk, :], CWB[:, c, kk:kk + 1], QKVs[:, c, :], ALU.mult, ALU.add)
    k.act(QKVs[:], QKVs[:], AF.Silu)
    QS, KS, VS, ZS = (A.alloc(nm, [128, 128]) for nm in ("qs", "ks", "vs", "zs"))
    for dst, src in ((QS, QKVs[:, 0:8, :]), (KS, QKVs[:, 8:16, :]), (VS, QKVs[:, 16:24, :]), (ZS, ZTs[:])):
        ps = PS.next()
        k.tr(ps[:, 0:128], src.rearrange("p h b -> p (h b)"))
        k.cp(dst[:], ps[:, 0:128], eng="act")
    SMs = A.alloc("sms", [n, 4, 8])
    BETA, EG, T8, L8 = (SMs[:, i, :] for i in range(4))
    k.act(T8, GTs[:, 0:8], AF.Exp, scale=-1.0)
    k.act(L8, T8, AF.Ln, bias=C.one[0:n, :], scale=1.0)
    k.act(BETA, L8, AF.Exp, scale=-1.0)
    k.tt(T8, GTs[:, 8:16], DTB[0:n, :], ALU.add)
    k.ts(T8, T8, 60.0, None, ALU.min)
    k.act(T8, T8, AF.Exp)
    k.act(T8, T8, AF.Ln, bias=C.one[0:n, :], scale=1.0)
    k.tt(T8, T8, NEGA[0:n, :], ALU.mult)
    k.act(EG, T8, AF.Exp)
    LH = A.alloc("lh", [n, 2, 8, n])
    i16 = IDENT[0:n, 0:n].unsqueeze(1).to_broadcast([n, 8, n])
    k.tt(LH[:, 0, :, :], BETA.unsqueeze(2).to_broadcast([n, 8, n]), i16, ALU.mult)
    k.tt(LH[:, 1, :, :], EG.unsqueeze(2).to_broadcast([n, 8, n]), i16, ALU.mult)
    pss = PS.next()
    k.mm(pss[:, 0:1], LH[:, 0, :, :].rearrange("p h b -> p (h b)"), ONES[0:n, 0:1])
    k.mm(pss[:, 1:2], LH[:, 1, :, :].rearrange("p h b -> p (h b)"), ONES[0:n, 0:1])
    SC = A.alloc("sc", [128, 8])
    k.cp(SC[:, 0:2], pss[:, 0:2])
    k.tt(SC[:, 2:3], SC[:, 0:1], SC[:, 1:2], ALU.mult)
    TM = A.alloc("tm", [128, 128])
    for src, col, sc_, eps in ((QS, 3, 128.0, C.epsq), (KS, 4, 1.0, C.eps6)):
        k.tt(TM[:], src[:], src[:], ALU.mult)
        k.red(SC[:, col:col + 1], TM[:])
        k.act(SC[:, col:col + 1], SC[:, col:col + 1], AF.Ln, bias=eps[:], scale=sc_)
        k.act(SC[:, col:col + 1], SC[:, col:col + 1], AF.Exp, scale=-0.5)
        k.ts(src[:], src[:], SC[:, col:col + 1], None, ALU.mult)
    k.tt(TM[:], QS[:], KS[:], ALU.mult)
    k.red(SC[:, 5:6], TM[:])
    WV, QDv, UV = (A.alloc(nm, [128, 128]) for nm in ("wv", "qdv", "uv"))
    k.ts(WV[:], KS[:], SC[:, 2:3], None, ALU.mult)
    k.ts(QDv[:], QS[:], SC[:, 1:2], None, ALU.mult)
    k.ts(UV[:], VS[:], SC[:, 0:1], None, ALU.mult)
    SS = A.alloc("ss", [128, 128, 128])
    for h in range(8):
        k.dma("sp", SS[h * n:(h + 1) * n], dr["sg"][:, h, :, :])
    TMPB = A.alloc("tmpb", [128, 16, 128])
    WS, QSs, PART = (A.alloc(nm, [128, 128]) for nm in ("wS", "qS", "part"))
    for vec, acc in ((WV, WS), (QDv, QSs)):
        for blk in range(8):
            ks_ = slice(blk * 16, (blk + 1) * 16)
            k.tt(TMPB[:], SS[:, ks_, :], vec[:, ks_].unsqueeze(2).to_broadcast([128, 16, 128]), ALU.mult)
            if blk == 0:
                k.red(acc[:], TMPB[:].rearrange("p k v -> p v k"))
            else:
                k.red(PART[:], TMPB[:].rearrange("p k v -> p v k"))
                k.tt(acc[:], acc[:], PART[:], ALU.add)
    VN = A.alloc("vn", [128, 128])
    OS = A.alloc("os", [128, 128])
    k.tt(VN[:], UV[:], WS[:], ALU.subtract)
    k.stt(OS[:], VN[:], SC[:, 5:6], QSs[:], ALU.mult, ALU.add)
    for blk in range(8):
        ks_ = slice(blk * 16, (blk + 1) * 16)
        k.tt(TMPB[:], KS[:, ks_].unsqueeze(2).to_broadcast([128, 16, 128]),
             VN[:].unsqueeze(1).to_broadcast([128, 16, 128]), ALU.mult)
        k.stt(SS[:, ks_, :], SS[:, ks_, :], SC[:, 1:2], TMPB[:], ALU.mult, ALU.add)
    for h in range(8):
        k.dma("sp", dr["gss"][:, h, :, :], SS[h * n:(h + 1) * n])
    k.tt(TM[:], OS[:], OS[:], ALU.mult)
    k.red(SC[:, 6:7], TM[:])
    k.act(SC[:, 6:7], SC[:, 6:7], AF.Ln, bias=C.eps6[:], scale=1.0 / 128.0)
    k.act(SC[:, 6:7], SC[:, 6:7], AF.Exp, scale=-0.5)
    k.stt(OS[:], OS[:], SC[:, 6:7], NWBC[:], ALU.mult, ALU.mult)
    k.tt(OS[:], OS[:], ZS[:], ALU.mult)
    pso = PS.next()
    k.tr(pso[:, 0:128], OS[:])
    OGs = A.alloc("ogs", [128, 8, n], BF16)
    k.cp(OGs[:].rearrange("p h b -> p (h b)"), pso[:, 0:128], eng="act")
    psY = PS.next()
    for nb in range(2):
        for h in range(8):
            k.mm(psY[0:n, nb * 512:(nb + 1) * 512], OGs[:, h, :], WOUT[:, h, nb * 512:(nb + 1) * 512],
                 start=(h == 0), stop=(h == 7), r=[OGs.name, (WOUT.name, h, 0)], w=[psY.name])
    RES = CG[0]
    k.stt(RES[:, 0:1024], XS[:], ALPHA, psY[0:n, :], ALU.mult, ALU.add)
    layer_norm_out(k, A, RES, n, g["LNG"], g["LNB"], RES, g["stats"], g["mv"], g["rstd"])
    k.dma("pool", dr["x1"][L:L + n, :], RES[:, 0:1024])
    P.barrier()
    A.reset(g["m_phase"])


class _Cut(Exception):
    pass


class Ctx:
    def cut(self, n):
        import os
        return int(os.environ.get("K_CUT", "0")) == n


CONST_NAMES = ("IDENT", "ONES", "U", "UBLK", "MASKL", "MASKUS", "MASKUI")


def make_consts():
    i = np.arange(128)
    same = (i[:, None] // 64) == (i[None, :] // 64)
    c = {}
    c["IDENT"] = np.eye(128)
    c["ONES"] = np.ones((128, 128))
    c["U"] = (same & (i[:, None] <= i[None, :])).astype(np.float64)
    c["UBLK"] = same.astype(np.float64)
    c["MASKL"] = np.where(same & (i[:, None] > i[None, :]), 0.0, -BIG)
    c["MASKUS"] = np.where(same & (i[None, :] > i[:, None]), 0.0, -BIG)
    c["MASKUI"] = np.where(same & (i[None, :] >= i[:, None]), 0.0, -BIG)
    arr = np.concatenate([c[n] for n in CONST_NAMES], axis=1)
    small = np.zeros((128, 8))
    small[:, 0] = (i < 64)
    small[:, 1] = (i >= 64)
    small[:, 2] = 1.0
    small[:, 3] = 1e-6
    small[:, 4] = 128e-6
    small[:, 5] = 1e-5
    return np.ascontiguousarray(np.concatenate([arr, small], axis=1).astype(np.float32))


IN_SHAPES = {
    "xp": [L, D], "xs": [NS, D], "cgc": [NS, 3, 3072], "sg": [NS, 8, 128, 128], "csc": [NS, 3, 3072],
    "ss": [NS, 32, 64, 128], "cfc": [2, NS, 2, DFF],
    "gdn_w_in": [D, GDN_IN], "gdn_conv_w": [4, 3072], "gdn_conv_b": [1, 3072], "gdn_a_log": [1, 8],
    "gdn_dt_bias": [1, 8], "gdn_norm_w": [1, 128], "gdn_w_out": [1024, D],
    "ssd_w_in": [D, SSD_IN], "ssd_conv_w": [4, 3072], "ssd_conv_b": [1, 3072], "ssd_a_log": [1, 32],
    "ssd_dt_bias": [1, 32], "ssd_d": [1, 32], "ssd_norm_w": [1, 2048], "ssd_w_out": [2048, D],
    "ffn_w_up": [2, D, 2 * DFF], "ffn_conv_w": [2, 3, DFF], "ffn_conv_b": [2, DFF], "ffn_w_down": [2, DFF, D],
    "ln1_g": [2, D], "ln1_b": [2, D], "ln2_g": [2, D], "ln2_b": [2, D],
    "consts": [128, 7 * 128 + 8],
}
OUT_SHAPES = {
    "yp": [L, D], "ys": [NS, D], "gcp": [3, 3072], "gcs": [NS, 3, 3072], "gsp": [8, 128, 128],
    "gss": [NS, 8, 128, 128], "scp": [3, 3072], "scs": [NS, 3, 3072], "ssp": [32, 64, 128],
    "sss": [NS, 32, 64, 128], "fcp": [2, 2, DFF], "fcs": [2, NS, 2, DFF],
}


def build_program(stop=None, n_prompt_tiles=NT, dbg=False):
    nc = bass.Bass("TRN2", target_bir_lowering=False)
    C = Ctx()
    C.nc = nc
    C.P = Prog(nc)
    C.k = KB(nc, C.P)
    C.A = Arena(nc)
    C.n_prompt_tiles = n_prompt_tiles
    dr = {}
    for n_, shp in IN_SHAPES.items():
        dr[n_] = nc.dram_tensor(n_, shp, F32, kind="ExternalInput").ap()
    for n_, shp in OUT_SHAPES.items():
        dr[n_] = nc.dram_tensor(n_, shp, F32, kind="ExternalOutput").ap()
    for n_ in ("x1", "x2", "x3"):
        kind = "ExternalOutput" if dbg else "Internal"
        dr[n_] = nc.dram_tensor(n_, [L + NS, D], F32, kind=kind).ap()
    dr["ysc"] = nc.dram_tensor("ysc", [NT + 1, 128, 2048], F32, kind="Internal").ap()
    C.dr = dr
    ps = [nc.alloc_psum_tensor("ps%d" % i, [128, 1024], F32) for i in range(4)]
    C.psO = ps[3]
    C.PS = PsumRot.__new__(PsumRot)
    C.PS.t = ps[0:3]
    C.PS.i = 0
    k, A = C.k, C.A
    C.cst = {}
    for i, n_ in enumerate(CONST_NAMES):
        C.cst[n_] = A.alloc("c_" + n_, [128, 128])
    smalls = {}
    for i, n_ in enumerate(("CH0", "CH1", "one", "eps6", "epsq", "eps5")):
        smalls[n_] = A.alloc("c_" + n_, [128, 1])
    k.reg32(*[C.cst[n_] for n_ in CONST_NAMES])
    m_ct = A.mark()
    CT = A.alloc("consts", [128, 7 * 128 + 8])
    k.dma("sp", CT[:], dr["consts"])
    for i, n_ in enumerate(CONST_NAMES):
        k.cp(C.cst[n_][:], CT[:, i * 128:(i + 1) * 128], eng="pool")
    for i, n_ in enumerate(("CH0", "CH1", "one", "eps6", "epsq", "eps5")):
        k.cp(smalls[n_][:], CT[:, 896 + i:897 + i], eng="pool")
        if n_ in ("CH0", "CH1"):
            C.cst[n_] = smalls[n_]
        else:
            setattr(C, n_, smalls[n_])
    C.P.barrier()
    A.reset(m_ct)
    k.ident = C.cst["IDENT"]
    k.eps5 = C.eps5

    import os
    phase_gdn(C)
    if not os.environ.get("K_SKIP_SAMPLE"):
        phase_gdn_sample(C)
    if stop != "A":
        phase_ffn(C, 0, "x1", "x2", "x2")
        if stop != "B":
            phase_ssd(C)
            if stop != "C":
                phase_ffn(C, 1, "x3", None, None)
    C.P.emit()
    C.sbuf_peak = A.peak
    return nc, C


_PROG_CACHE = {}


def shard_inputs(inputs):
    f = lambda a: np.ascontiguousarray(np.asarray(a, dtype=np.float32))
    shared = {
        "gdn_w_in": f(inputs["gdn_w_in"][0]), "gdn_conv_w": f(inputs["gdn_conv_w"][0]),
        "gdn_conv_b": f(inputs["gdn_conv_b"]), "gdn_a_log": f(inputs["gdn_a_log"]),
        "gdn_dt_bias": f(inputs["gdn_dt_bias"]), "gdn_norm_w": f(inputs["gdn_norm_w"]),
        "gdn_w_out": f(inputs["gdn_w_out"][0]),
        "ssd_w_in": f(inputs["ssd_w_in"][0]), "ssd_conv_w": f(inputs["ssd_conv_w"][0]),
        "ssd_conv_b": f(inputs["ssd_conv_b"]), "ssd_a_log": f(inputs["ssd_a_log"]),
        "ssd_dt_bias": f(inputs["ssd_dt_bias"]), "ssd_d": f(inputs["ssd_d"]),
        "ssd_norm_w": f(inputs["ssd_norm_w"]), "ssd_w_out": f(inputs["ssd_w_out"][0]),
        "ffn_w_up": f(inputs["ffn_w_up"]), "ffn_conv_w": f(inputs["ffn_conv_w"]),
        "ffn_conv_b": f(inputs["ffn_conv_b"]), "ffn_w_down": f(inputs["ffn_w_down"]),
        "ln1_g": f(inputs["ln1_g"]), "ln1_b": f(inputs["ln1_b"]),
        "ln2_g": f(inputs["ln2_g"]), "ln2_b": f(inputs["ln2_b"]),
        "consts": make_consts(),
    }
    maps = []
    for c in range(NCORES):
        b = slice(c * NS, (c + 1) * NS)
        m = dict(shared)
        m["xp"] = f(inputs["x_prompt"][c])
        m["xs"] = f(inputs["x_sample"][b, 0])
        m["cgc"] = f(inputs["cache_gdn_conv"][0, b])
        m["sg"] = f(inputs["state_gdn"][0, b])
        m["csc"] = f(inputs["cache_ssd_conv"][0, b])
        m["ss"] = f(inputs["state_ssd"][0, b])
        m["cfc"] = f(inputs["cache_ffn_conv"][:, b])
        maps.append(m)
    return maps


def kernel(**inputs):
    if "nc" not in _PROG_CACHE:
        _PROG_CACHE["nc"] = build_program()[0]
    nc = _PROG_CACHE["nc"]
    maps = shard_inputs(inputs)
    res = run_bass_kernel_spmd(nc, maps, core_ids=list(range(NCORES)))
    R = res.results
    cat = lambda n_: np.stack([R[c][n_] for c in range(NCORES)], 0)
    y_p = cat("yp")
    y_s = np.concatenate([R[c]["ys"] for c in range(NCORES)], 0)[:, None, :]
    gcp = cat("gcp")[None]
    gcs = np.concatenate([R[c]["gcs"] for c in range(NCORES)], 0)[None]
    gsp = cat("gsp")[None]
    gss = np.concatenate([R[c]["gss"] for c in range(NCORES)], 0)[None]
    scp = cat("scp")[None]
    scs = np.concatenate([R[c]["scs"] for c in range(NCORES)], 0)[None]
    ssp = cat("ssp")[None]
    sss = np.concatenate([R[c]["sss"] for c in range(NCORES)], 0)[None]
    fcp = np.stack([R[c]["fcp"] for c in range(NCORES)], 1)
    fcs = np.concatenate([R[c]["fcs"] for c in range(NCORES)], 1)
    return (y_p, y_s, gcp, gcs, gsp, gss, scp, scs, ssp, sss, fcp, fcs)


def phase_ffn(C, layer, src, dst, _unused=None):
    nc, k, P, A, PS, dr = C.nc, C.k, C.P, C.A, C.PS, C.dr
    P.barrier()
    m_phase = A.mark()
    WUP = A.alloc("wup", [128, 8, 2 * DFF], BF16)
    WDN = A.alloc("wdn", [128, NFC, 1024], BF16)
    LNG = A.alloc("ln2g", [128, 1024])
    LNB_ = A.alloc("ln2b", [128, 1024])
    k.dma("sp", LNG[:], dr["ln2_g"][layer:layer + 1, :].partition_broadcast(128))
    k.dma("sp", LNB_[:], dr["ln2_b"][layer:layer + 1, :].partition_broadcast(128))
    stats = A.alloc("stats", [128, 2, 6])
    mv = A.alloc("mv", [128, 2])
    rstd = A.alloc("rstd", [128, 1])
    TAIL = A.alloc("ftail", [128, NFC, 2])
    CWB = load_small_params(k, PS, A, dr["ffn_conv_w"][layer], dr["ffn_conv_b"][layer:layer + 1, :], 3, NFC, "fcw")
    load_weight_bf16(k, WUP, dr["ffn_w_up"][layer], 8, 2 * DFF)
    load_weight_bf16(k, WDN, dr["ffn_w_down"][layer], NFC, 1024)
    k.memset(TAIL[:], 0.0)
    m_work = A.mark()
    TW = 256
    NW_ = 5
    XTOK = [A.alloc("fxtok", [128, 1024]) for _ in range(4)]
    XT = A.alloc("fxT", [128, 8, TW], BF16)
    H2 = [A.alloc("fh", [128, NFC, TW], BF16) for _ in range(2)]
    PREF = [A.alloc("fpre", [128, TW + 2]) for _ in range(NW_)]
    ACC = [A.alloc("facc", [128, TW]) for _ in range(NW_)]
    RES = A.alloc("fres", [128, 1024])
    pst = C.PS.t + [C.psO]
    PSF = SlotRot([pst[0][:, 0:512]])
    PSFC = SlotRot([pst[0][:, 512:1024]] + [pst[i][:, j * 512:(j + 1) * 512] for i in (2, 3) for j in (0, 1)])

    def out_rows(r0, n):
        if dst is not None:
            return dr[dst][r0:r0 + n, :]
        return dr["yp"][r0:r0 + n, :] if r0 < L else dr["ys"][r0 - L:r0 - L + n, :]

    def wk(fc, val):
        col = (DFF if val else 0) + fc * 128
        return col, col // 2048

    nmac = (C.n_prompt_tiles * 128) // TW
    cnt = [0]

    def front_gen(m):
        for s_ in range(2):
            xt_ = XTOK[(m % 2) * 2 + s_]
            k.dma("sp", xt_[:], dr[src][(2 * m + s_) * 128:(2 * m + s_ + 1) * 128, :])
            to_feature_major(k, PSF, XT[:, :, s_ * 128:(s_ + 1) * 128], xt_, 128, 8)
            yield

    def fc_gen(m, fc):
        H = H2[m % 2]
        psG = PSFC.next()
        i_ = cnt[0] % NW_
        cnt[0] += 1
        for val in (0, 1):
            col, cb = wk(fc, val)
            for kc in range(8):
                k.mm(psG[:, val * TW:(val + 1) * TW], WUP[:, kc, col:col + 128], XT[:, kc, :],
                     start=(kc == 0), stop=(kc == 7), r=[(WUP.name, kc, cb), XT.name], w=PK(psG))
        yield
        pre, acc = PREF[i_], ACC[i_]
        k.cp(pre[:, 0:2], TAIL[:, fc, :], r=[(TAIL.name, fc)], w=[pre.name])
        k.cp(pre[:, 2:TW + 2], psG[:, 0:TW], eng="act")
        yield
        k.cp(TAIL[:, fc, :], pre[:, TW:TW + 2], r=[pre.name], w=[(TAIL.name, fc)])
        k.act(acc[:], pre[:, 0:TW], AF.Identity, bias=CWB[:, fc, 3:4], scale=CWB[:, fc, 0:1])
        yield
        k.stt(acc[:], pre[:, 1:TW + 1], CWB[:, fc, 1:2], acc[:], ALU.mult, ALU.add)
        k.stt(acc[:], pre[:, 2:TW + 2], CWB[:, fc, 2:3], acc[:], ALU.mult, ALU.add)
        yield
        k.act(acc[:], acc[:], AF.Silu)
        yield
        k.tt(H[:, fc, :], acc[:], psG[:, TW:2 * TW], ALU.mult, w=[(H.name, fc)])

    def down_gen(m):
        H = H2[m % 2]
        psY = pst[1]
        for s_ in range(2):
            xt_ = XTOK[(m % 2) * 2 + s_]
            for nb in range(2):
                for fc in range(NFC):
                    k.mm(psY[:, nb * 512:(nb + 1) * 512], H[:, fc, s_ * 128:(s_ + 1) * 128],
                         WDN[:, fc, nb * 512:(nb + 1) * 512], start=(fc == 0), stop=(fc == NFC - 1),
                         r=[(H.name, fc), (WDN.name, fc, 0)], w=PK(psY[:, nb * 512:(nb + 1) * 512]))
                yield
            k.stt(RES[:], xt_[:], ALPHA, psY[:], ALU.mult, ALU.add)
            yield
            layer_norm_out(k, A, RES, 128, LNG, LNB_, RES, stats, mv, rstd)
            k.dma("pool", out_rows((2 * m + s_) * 128, 128), RES[:])
            yield

    def run_window(entries, width):
        finished = set()
        active = []
        nxt_i = 0
        n_ = len(entries)
        while active or nxt_i < n_:
            n_adm = 0
            while nxt_i < n_ and len(active) < width and all(d in finished for d in entries[nxt_i][1]):
                if len(entries[nxt_i]) > 2 and (n_adm >= 1 or sum(1 for a_ in active if len(entries[a_[0]]) > 2) >= NW_):
                    break
                if len(entries[nxt_i]) > 2:
                    n_adm += 1
                active.append((nxt_i, entries[nxt_i][0]))
                nxt_i += 1
            assert active, "window deadlock"
            keep = []
            for idx, g_ in active:
                try:
                    next(g_)
                    keep.append((idx, g_))
                except StopIteration:
                    finished.add(idx)
            active = keep

    entries = []
    idx_front, idx_fc, idx_down = {}, {}, {}
    for m in range(nmac):
        idx_front[m] = len(entries)
        entries.append((front_gen(m), []))
        for fc in range(NFC):
            deps = [idx_front[m]]
            if m >= 1:
                deps.append(idx_fc[(m - 1, fc)])
            if m >= 2:
                deps.append(idx_down[m - 2])
            idx_fc[(m, fc)] = len(entries)
            entries.append((fc_gen(m, fc), deps, 'fc'))
        idx_down[m] = len(entries)
        entries.append((down_gen(m), [idx_fc[(m, fc)] for fc in range(NFC)]))
    run_window(entries, NW_ + 1)
    for pc in range(3):
        nchp = min(8, NFC - pc * 8)
        to_token_major(k, PS, RES, TAIL[:, pc * 8:pc * 8 + nchp, :], 2, nchp,
                       rkey=lambda c, pc=pc: [(TAIL.name, pc * 8 + c)])
        k.dma("pool", dr["fcp"][layer, :, pc * 1024:pc * 1024 + nchp * 128], RES[0:2, 0:nchp * 128])
    P.barrier()
    A.reset(m_work)
    n = NS
    XS = A.alloc("fxs", [n, 1024])
    XTs = A.alloc("fxsT", [128, 8, n], BF16)
    CF = A.alloc("fcf", [n, DFF])
    CFT = A.alloc("fcft", [128, NFC, 2, n])
    NEWG = A.alloc("fnewg", [128, NFC, n])
    Hs = A.alloc("fhs", [128, NFC, n], BF16)
    ACCs = [A.alloc("faccs", [128, n]) for _ in range(2)]
    k.dma("sp", XS[:], dr[src][L:L + n, :])
    to_feature_major(k, PS, XTs, XS, n, 8)
    for r_ in range(2):
        k.dma("sp", CF[:], dr["cfc"][layer, :, r_, :])
        to_feature_major(k, PS, CFT[:, :, r_, :], CF, n, NFC)
    k.dma("pool", dr["fcs"][layer, :, 0, :], dr["cfc"][layer, :, 1, :])
    for fc in range(NFC):
        psG = PS.next()
        for val in (0, 1):
            col, cb = wk(fc, val)
            for kc in range(8):
                k.mm(psG[:, val * 512:val * 512 + n], WUP[:, kc, col:col + 128], XTs[:, kc, :],
                     start=(kc == 0), stop=(kc == 7), r=[(WUP.name, kc, cb), XTs.name], w=[psG.name])
        acc = ACCs[fc % 2]
        k.cp(NEWG[:, fc, :], psG[:, 0:n], eng="act")
        k.act(acc[:], CFT[:, fc, 0, :], AF.Identity, bias=CWB[:, fc, 3:4], scale=CWB[:, fc, 0:1])
        k.stt(acc[:], CFT[:, fc, 1, :], CWB[:, fc, 1:2], acc[:], ALU.mult, ALU.add)
        k.stt(acc[:], NEWG[:, fc, :], CWB[:, fc, 2:3], acc[:], ALU.mult, ALU.add)
        k.act(acc[:], acc[:], AF.Silu)
        k.tt(Hs[:, fc, :], acc[:], psG[:, 512:512 + n], ALU.mult)
    to_token_major(k, PS, CF, NEWG, n, NFC)
    k.dma("pool", dr["fcs"][layer, :, 1, :], CF[:])
    psY = PS.next()
    for nb in range(2):
        for fc in range(NFC):
            k.mm(psY[0:n, nb * 512:(nb + 1) * 512], Hs[:, fc, :], WDN[:, fc, nb * 512:(nb + 1) * 512],
                 start=(fc == 0), stop=(fc == NFC - 1), r=[Hs.name, (WDN.name, fc, 0)], w=[psY.name])
    RESs = A.alloc("fress", [n, 1024])
    k.stt(RESs[:], XS[:], ALPHA, psY[0:n, :], ALU.mult, ALU.add)
    layer_norm_out(k, A, RESs, n, LNG, LNB_, RESs, stats, mv, rstd)
    k.dma("pool", out_rows(L, n), RESs[:])
    P.barrier()
    A.reset(m_phase)


def phase_ssd(C):
    nc, k, P, A, PS, dr = C.nc, C.k, C.P, C.A, C.PS, C.dr
    cst = C.cst
    IDENT, ONES, U, UBLK, MASKUI, CH0, CH1 = (cst[n] for n in ("IDENT", "ONES", "U", "UBLK", "MASKUI", "CH0", "CH1"))
    P.barrier()
    m_phase = A.mark()
    NX = 3104
    WIN = A.alloc("swin", [128, 8, NX], BF16)
    NEGA = A.alloc("snega", [128, 32])
    DTB = A.alloc("sdtb", [128, 32])
    DFULL = A.alloc("sdfull", [128, 32])
    DV = A.alloc("sdv", [128, 16])
    k.dma("sp", NEGA[:], dr["ssd_a_log"].partition_broadcast(128))
    k.dma("sp", DTB[:], dr["ssd_dt_bias"].partition_broadcast(128))
    k.dma("sp", DFULL[:], dr["ssd_d"].partition_broadcast(128))
    k.act(NEGA[:], NEGA[:], AF.Exp)
    k.ts(NEGA[:], NEGA[:], -1.0, None, ALU.mult)
    for hh in range(2):
        k.cp(DV[hh * 64:(hh + 1) * 64, :], DFULL[hh * 64:(hh + 1) * 64, :].rearrange("p (c t) -> p c t", t=2)[:, :, hh])
    CWB = load_small_params(k, PS, A, dr["ssd_conv_w"], dr["ssd_conv_b"], 4, 24, "scw")
    load_weight_bf16(k, WIN, dr["ssd_w_in"][:, 2048:2048 + NX], 8, NX)
    m_work = A.mark()
    XTOK = A.alloc("sxtok", [128, 1024])
    XT = A.alloc("sxT", [128, 8, 128], BF16)
    PRE = A.alloc("spre", [128, 24, 131])
    XBC_ = [A.alloc("sxbc", [128, 24, 128]) for _ in range(2)]
    SM_ = [A.alloc("ssm", [128, 8, 32]) for _ in range(2)]
    AC_ = [A.alloc("sac", [128, 64]) for _ in range(2)]
    XPAD_ = [A.alloc("sxpad", [128, 32, 128], BF16) for _ in range(2)]
    XDA_ = [A.alloc("sxda", [128, 2048], BF16) for _ in range(2)]
    XDB_ = [A.alloc("sxdb", [128, 2048], BF16) for _ in range(2)]
    BTOK_ = [A.alloc("sbtok", [128, 4, 128], BF16) for _ in range(2)]
    AEXP_ = [A.alloc("saexp", [128, 2048]) for _ in range(2)]
    STG = [A.alloc("sst", [128, 512]) for _ in range(4)]
    YT = A.alloc("syt", [128, 16, 128])
    TMP3 = A.alloc("stmp3", [128, 16, 128])
    XA_ = [A.alloc("sxa", [128, 8, 128]) for _ in range(2)]
    SCT_ = [A.alloc("ssct", [128, 8, 128], BF16) for _ in range(2)]
    EG2 = [A.alloc("seg", [128, 4, 128]) for _ in range(2)]
    TMPS_ = [A.alloc("stmps", [128, 512]) for _ in range(2)]
    TMP2_ = [A.alloc("stmp2", [128, 512]) for _ in range(2)]
    EL_ = [A.alloc("sel", [128, 2, 32]) for _ in range(2)]
    AVC = A.alloc("savc", [128, 2, 32])
    MASK8 = A.alloc("smask8", [128, 8, 128])
    k.reg32(MASK8)
    k.cp(MASK8[:], MASKUI[:].unsqueeze(1).to_broadcast([128, 8, 128]))
    NAC_ = [A.alloc("snac", [128, 32]) for _ in range(2)]
    k.memset(PRE[:, :, 0:3], 0.0, w=[(PRE.name, c) for c in range(24)])
    k.reg32(*(XBC_ + AEXP_ + STG + XA_))
    for x_ in STG:
        k.ts(x_[:], ONES[:].unsqueeze(1).to_broadcast([128, 4, 128]).rearrange("p a b -> p (a b)") if False else
             ONES[:, 0:1].to_broadcast([128, 512]), 0.0, None, ALU.mult)
    for x_ in XPAD_:
        k.memset(x_[:], 0.0)
    PREk = [(PRE.name, c) for c in range(24)]
    pst = C.PS.t + [C.psO]
    PSF = SlotRot([pst[0][:, 0:512], pst[0][:, 512:1024]])

    def mb(ap):
        return ap.unsqueeze(1).to_broadcast([128, 8, 128])

    def front(t, XBC, SM, AC, XPAD, XDA, XDB, BTOK, AEXP):
        PS = PSF
        XBCk = [(XBC.name, c) for c in range(24)]
        DTR, DTV, AV, DEC, COEF, COEFA, COEFB, T32 = (SM[:, i, :] for i in range(8))
        k.dma("sp", XTOK[:], dr["x2"][t * 128:(t + 1) * 128, :])
        to_feature_major(k, PS, XT, XTOK, 128, 8)
        yield

        def evac(i, grp, ps, view):
            k.cp(PRE[:, grp[0]:grp[0] + len(grp), 3:131], view, eng="act", r=PK(view), w=[(PRE.name, c) for c in grp])
        for g0 in range(0, 24, 4):
            proj_feature_major(k, PS, WIN, WIN.name, XT, 128, list(range(g0, g0 + 4)), evac)
            yield
        psg = PS.next()
        for kc in range(8):
            k.mm(psg[:, 0:32], XT[:, kc, :], WIN[:, kc, 3072:3104], start=(kc == 0), stop=(kc == 7),
                 r=[(WIN.name, kc, 1), XT.name], w=PK(psg[:, 0:32]))
        k.cp(DTR, psg[:, 0:32], w=[(SM.name, 0)])
        yield
        for c in range(24):
            k.act(XBC[:, c, :], PRE[:, c, 0:128], AF.Identity, bias=CWB[:, c, 4:5], scale=CWB[:, c, 0:1],
                  r=[(PRE.name, c), CWB.name], w=[(XBC.name, c)])
            if c % 8 == 7:
                yield
        for kk in range(1, 4):
            for c in range(24):
                k.stt(XBC[:, c, :], PRE[:, c, kk:kk + 128], CWB[:, c, kk:kk + 1], XBC[:, c, :], ALU.mult, ALU.add,
                      r=[(PRE.name, c), (XBC.name, c), CWB.name], w=[(XBC.name, c)])
                if c % 6 == 5:
                    yield
        k.act(XBC[:], XBC[:], AF.Silu, r=XBCk, w=XBCk)
        if t == C.n_prompt_tiles - 1:
            for pc in range(6):
                to_token_major(k, PS, TMP3[:].rearrange("p c t -> p (c t)"), PRE[:, pc * 4:(pc + 1) * 4, 128:131], 3, 4,
                               rkey=lambda c, pc=pc: [(PRE.name, pc * 4 + c)])
                k.dma("pool", dr["scp"][:, pc * 512:(pc + 1) * 512], TMP3[0:3, 0:4, :].rearrange("p c t -> p (c t)"))
        k.cp(PRE[:, :, 0:3], PRE[:, :, 128:131], eng="dve", r=PREk, w=PREk)
        yield
        k.tt(T32, DTR, DTB[:], ALU.add, r=[(SM.name, 0), DTB.name], w=[(SM.name, 7)])
        k.ts(T32, T32, 60.0, None, ALU.min, r=[(SM.name, 7)], w=[(SM.name, 7)])
        k.act(T32, T32, AF.Exp, r=[(SM.name, 7)], w=[(SM.name, 7)])
        k.act(DTV, T32, AF.Ln, bias=C.one[:], scale=1.0, r=[(SM.name, 7), C.one.name], w=[(SM.name, 1)])
        k.tt(AV, DTV, NEGA[:], ALU.mult, r=[(SM.name, 1), NEGA.name], w=[(SM.name, 2)])
        psc = PS.next()
        k.mm(psc[:, 0:32], U[:], AV, r=[U.name, (SM.name, 2)], w=PK(psc[:, 0:32]))
        k.mm(psc[:, 32:64], UBLK[:], AV, r=[UBLK.name, (SM.name, 2)], w=PK(psc[:, 32:64]))
        k.cp(AC[:], psc[:, 0:64])
        acum, alast = AC[:, 0:32], AC[:, 32:64]
        k.ts(NAC_[t % 2][:], acum, -1.0, None, ALU.mult)
        k.ts(AVC[:, 0, :], AV, CH0[:, 0:1], None, ALU.mult, r=[(SM.name, 2), CH0.name], w=[AVC.name])
        k.ts(AVC[:, 1, :], AV, CH1[:, 0:1], None, ALU.mult, r=[(SM.name, 2), CH1.name], w=[AVC.name])
        psel = PS.next()
        k.mm(psel[:, 0:64], ONES[:], AVC[:].rearrange("p c h -> p (c h)"))
        k.act(EL_[t % 2][:].rearrange("p c h -> p (c h)"), psel[:, 0:64], AF.Exp)
        k.tt(DEC, alast, acum, ALU.subtract, r=[AC.name], w=[(SM.name, 3)])
        k.act(DEC, DEC, AF.Exp, r=[(SM.name, 3)], w=[(SM.name, 3)])
        k.tt(COEF, DEC, DTV, ALU.mult, r=[(SM.name, 3), (SM.name, 1)], w=[(SM.name, 4)])
        k.ts(COEFA, COEF, CH0[:, 0:1], None, ALU.mult, r=[(SM.name, 4), CH0.name], w=[(SM.name, 5)])
        k.ts(COEFB, COEF, CH1[:, 0:1], None, ALU.mult, r=[(SM.name, 4), CH1.name], w=[(SM.name, 6)])
        k.cp(AEXP[:].rearrange("p (h q) -> p h q", q=64), AV.unsqueeze(2).to_broadcast([128, 32, 64]),
             r=[(SM.name, 2)], w=[AEXP.name])
        yield
        XP4 = XPAD[:].rearrange("p (c t) q -> p c t q", t=2)
        for qtr in range(4):
            pt = PS.next()
            for i in range(4):
                k.tr(pt[:, i * 128:(i + 1) * 128], XBC[:, qtr * 4 + i, :], r=[XBCk[qtr * 4 + i], IDENT.name],
                     w=PK(pt[:, i * 128:(i + 1) * 128]))
            pt4 = pt.rearrange("p (c t q) -> p c t q", t=2, q=64)
            cs = slice(qtr * 4, qtr * 4 + 4)
            hs = slice(qtr * 8, qtr * 8 + 8)
            for hh in range(2):
                dt_v = DTV.rearrange("p (c t) -> p c t", t=2)[:, cs, hh].unsqueeze(2).to_broadcast([128, 4, 64])
                k.tt(XP4[:, cs, hh, hh * 64:(hh + 1) * 64], pt4[:, :, hh, :], dt_v, ALU.mult,
                     r=PK(pt) + [(SM.name, 1)], w=[XPAD.name])
            pt3 = pt.rearrange("p (h q) -> p h q", q=64)
            for XD, CO, ci in ((XDA, COEFA, 5), (XDB, COEFB, 6)):
                k.tt(XD[:].rearrange("p (h q) -> p h q", q=64)[:, hs, :], pt3,
                     CO[:, hs].unsqueeze(2).to_broadcast([128, 8, 64]), ALU.mult, r=PK(pt) + [(SM.name, ci)], w=[XD.name])
            yield
        pb = PS.next()
        for g in range(4):
            k.tr(pb[:, g * 128:(g + 1) * 128], XBC[:, 16 + g, :], r=[XBCk[16 + g], IDENT.name],
                 w=PK(pb[:, g * 128:(g + 1) * 128]))
        k.cp(BTOK[:].rearrange("p g n -> p (g n)"), pb, eng="act")

    def grp_thread(th, t, XBC, SM, AC, XPAD, XDA, XDB, BTOK, AEXP):
        XBCk = [(XBC.name, c) for c in range(24)]
        DTR, DTV, AV, DEC, COEF, COEFA, COEFB, T32 = (SM[:, i, :] for i in range(8))
        acum = AC[:, 0:32]
        XA, SCT, EG_, TMPS, TMP2 = XA_[th], SCT_[th], EG2[th], TMPS_[th], TMP2_[th]
        pab = pst[1 + th]
        pc_ = pst[3][:, th * 512:(th + 1) * 512]
        for g in (th, th + 2):
            ST = STG[g]
            hsl = slice(8 * g, 8 * g + 8)
            k.tt(XA[:], mb(U[:]), AV[:, hsl].unsqueeze(2).to_broadcast([128, 8, 128]), ALU.mult,
                 r=[U.name, (SM.name, 2)], w=[XA.name])
            psR = pab
            XAf = XA[:].rearrange("p h j -> p (h j)")
            M8f = MASK8[:].rearrange("p h j -> p (h j)")
            for nb in range(2):
                k.mm(psR[:, nb * 512:(nb + 1) * 512], ONES[:], XAf[:, nb * 512:(nb + 1) * 512], start=True, stop=False)
                k.mm(psR[:, nb * 512:(nb + 1) * 512], IDENT[:], M8f[:, nb * 512:(nb + 1) * 512], start=False, stop=True)
            yield
            psR3 = psR[:].rearrange("p (h j) -> p h j", h=8)
            SEG = XA
            k.tt(SEG[:], psR3, acum[:, hsl].unsqueeze(2).to_broadcast([128, 8, 128]), ALU.subtract,
                 r=PK(psR[:]) + [AC.name], w=[SEG.name])
            k.act(SEG[:], SEG[:], AF.Exp)
            EL = EL_[t % 2]
            psCB = pc_
            k.mm(psCB[:, 0:128], XBC[:, 16 + g, :], XBC[:, 20 + g, :], r=[XBCk[16 + g], XBCk[20 + g]],
                 w=PK(psCB[:, 0:128]))
            yield
            k.tt(SCT[:], SEG[:], mb(psCB[:, 0:128]), ALU.mult)
            psE = pab[:, 0:512]
            for cl in range(4):
                cp_ = 4 * g + cl
                k.mm(psE[:, cl * 128:(cl + 1) * 128], AEXP[:, cp_ * 128:(cp_ + 1) * 128], U[:])
            k.act(EG_[:].rearrange("p c j -> p (c j)"), psE, AF.Exp)
            yield
            psYd = pab[:, 512:1024]
            psYo = pc_
            for cl in range(4):
                h0 = 8 * g + 2 * cl
                k.mm(psYd[:, cl * 128:(cl + 1) * 128], XPAD[:, h0, :], SCT[:, 2 * cl, :], start=True, stop=False)
                k.mm(psYd[:, cl * 128:(cl + 1) * 128], XPAD[:, h0 + 1, :], SCT[:, 2 * cl + 1, :], start=False, stop=True)
            for c in range(2):
                for cl in range(4):
                    k.mm(psYo[:, cl * 128 + c * 64:cl * 128 + c * 64 + 64], ST[:, cl * 128:(cl + 1) * 128],
                         XBC[:, 20 + g, c * 64:(c + 1) * 64], r=[ST.name, XBCk[20 + g]],
                         w=PK(psYo[:, cl * 128 + c * 64:cl * 128 + c * 64 + 64]))
                psD = pab[:, 0:512]
                XD = XDA if c == 0 else XDB
                k.mm(psD, BTOK[:, g, :], XD[:, 512 * g:512 * (g + 1)])
                yield
                k.tt(TMPS[:].rearrange("p (h q) -> p h q", q=64), ST[:].rearrange("p (h q) -> p h q", q=64),
                     EL[:, c, hsl].unsqueeze(2).to_broadcast([128, 8, 64]), ALU.mult,
                     r=[ST.name, EL.name], w=[TMPS.name])
                k.tt(ST[:], TMPS[:], psD, ALU.add)
                yield
            k.tt(TMP2[:], psYo, EG_[:].rearrange("p c j -> p (c j)"), ALU.mult)
            k.tt(YT[:, 4 * g:4 * g + 4, :].rearrange("p c j -> p (c j)"), TMP2[:], psYd, ALU.add,
                 w=[(YT.name, g)])
            yield

    def tail(t, XBC):
        XBCk = [(XBC.name, c) for c in range(24)]
        k.tt(TMP3[:], XBC[:, 0:16, :], DV[:].unsqueeze(2).to_broadcast([128, 16, 128]), ALU.mult,
             r=XBCk[0:16] + [DV.name], w=[TMP3.name])
        k.tt(YT[:], YT[:], TMP3[:], ALU.add, r=[(YT.name, g) for g in range(4)] + [TMP3.name],
             w=[(YT.name, g) for g in range(4)])
        k.dma("pool", dr["ysc"][t], YT[:].rearrange("p c j -> p (c j)"), r=[(YT.name, g) for g in range(4)])

    def bufs(t):
        p = t % 2
        return (XBC_[p], SM_[p], AC_[p], XPAD_[p], XDA_[p], XDB_[p], BTOK_[p], AEXP_[p])

    def run_all(gens):
        alive = [True] * len(gens)

        def step(gi):
            if alive[gi]:
                try:
                    next(gens[gi])
                except StopIteration:
                    alive[gi] = False
        while any(alive):
            if len(gens) == 3:
                for gi in (2, 0, 2, 1):
                    step(gi)
            else:
                for gi in range(len(gens)):
                    step(gi)

    NTL = C.n_prompt_tiles
    if NTL > 0:
        run_all([front(0, *bufs(0))])
    for t in range(NTL):
        gens = [grp_thread(0, t, *bufs(t)), grp_thread(1, t, *bufs(t))]
        if t + 1 < NTL:
            gens.append(front(t + 1, *bufs(t + 1)))
        run_all(gens)
        tail(t, XBC_[t % 2])
    for g in range(4):
        pt = C.PS.next()
        for i in range(4):
            k.tr(pt[:, i * 128:(i + 1) * 128], STG[g][:, i * 128:(i + 1) * 128])
        k.cp(TMP3[:, 0:4, :].rearrange("p c t -> p (c t)"), pt[:, 0:512], eng="act")
        k.dma("pool", dr["ssp"].rearrange("h p s -> (h p) s").rearrange("(c q) s -> q c s", q=128)[:, g * 4:g * 4 + 4, :],
              TMP3[:, 0:4, :])
    ssd_sample(C, WIN, CWB, NEGA, DTB, DV, m_work)
    P.barrier()
    A.reset(m_phase)
    ssd_gate(C)


def ssd_sample(C, WIN, CWB, NEGA, DTB, DV, m_work):
    nc, k, P, A, PS, dr = C.nc, C.k, C.P, C.A, C.PS, C.dr
    IDENT, ONES = C.cst["IDENT"], C.cst["ONES"]
    P.barrier()
    A.reset(m_work)
    n = NS
    XS = A.alloc("qxs", [n, 1024])
    XT = A.alloc("qxT", [128, 8, n], BF16)
    PRES = A.alloc("qpres", [128, 24, 4, n])
    XBCs = A.alloc("qxbc", [128, 24, n])
    CG = A.alloc("qcg", [n, 3072])
    k.dma("sp", XS[:], dr["x2"][L:L + n, :])
    to_feature_major(k, PS, XT, XS, n, 8)
    for r_ in range(3):
        k.dma("sp", CG[:], dr["csc"][:, r_, :])
        to_feature_major(k, PS, PRES[:, :, r_, :], CG, n, 24)

    def evac(i, grp, ps, view):
        k.cp(PRES[:, grp[0]:grp[0] + len(grp), 3, :], view, eng="act")
    proj_feature_major(k, PS, WIN, WIN.name, XT, n, list(range(24)), evac)
    psg = PS.next()
    for kc in range(8):
        k.mm(psg[0:n, 0:32], XT[:, kc, :], WIN[:, kc, 3072:3104], start=(kc == 0), stop=(kc == 7),
             r=[(WIN.name, kc, 1), XT.name], w=[psg.name])
    SMs = A.alloc("qsm", [n, 4, 32])
    DTR, DTs, DAs, T32 = (SMs[:, i, :] for i in range(4))
    k.cp(DTR, psg[0:n, 0:32])
    k.dma("pool", dr["scs"][:, 0:2, :], dr["csc"][:, 1:3, :])
    to_token_major(k, PS, CG, PRES[:, :, 3, :], n, 24)
    k.dma("pool", dr["scs"][:, 2, :], CG[:])
    for c in range(24):
        k.act(XBCs[:, c, :], PRES[:, c, 0, :], AF.Identity, bias=CWB[:, c, 4:5], scale=CWB[:, c, 0:1])
    for kk in range(1, 4):
        for c in range(24):
            k.stt(XBCs[:, c, :], PRES[:, c, kk, :], CWB[:, c, kk:kk + 1], XBCs[:, c, :], ALU.mult, ALU.add)
    k.act(XBCs[:], XBCs[:], AF.Silu)
    k.tt(T32, DTR, DTB[0:n, :], ALU.add)
    k.ts(T32, T32, 60.0, None, ALU.min)
    k.act(T32, T32, AF.Exp)
    k.act(DTs, T32, AF.Ln, bias=C.one[0:n, :], scale=1.0)
    k.tt(T32, DTs, NEGA[0:n, :], ALU.mult)
    k.act(DAs, T32, AF.Exp)
    LS = A.alloc("qls", [n, 2, 128])
    k.memset(LS[:], 0.0)
    k.memset(LS[:, 0, 0:64], 1.0)
    k.memset(LS[:, 1, 64:128], 1.0)
    RH = A.alloc("qrh", [n, 4, 16, n])
    i16 = IDENT[0:n, 0:n].unsqueeze(1).to_broadcast([n, 16, n])
    for qi, src in enumerate((DTs, DAs)):
        for hh in range(2):
            k.tt(RH[:, qi * 2 + hh, :, :], src.rearrange("p (c t) -> p c t", t=2)[:, :, hh].unsqueeze(2).to_broadcast([n, 16, n]),
                 i16, ALU.mult)
    psx = PS.next()
    for qi in range(2):
        for hh in range(2):
            k.mm(psx[:, qi * 256:(qi + 1) * 256], LS[:, hh, :], RH[:, qi * 2 + hh, :, :].rearrange("p c b -> p (c b)"),
                 start=(hh == 0), stop=(hh == 1))
    FX = A.alloc("qfx", [128, 2, 16, n])
    k.cp(FX[:].rearrange("p a c b -> p (a c b)"), psx[:, 0:512])
    XDT = A.alloc("qxdt", [128, 16, n])
    k.tt(XDT[:], XBCs[:, 0:16, :], FX[:, 0, :, :], ALU.mult)
    BCT = A.alloc("qbct", [n, 1024])
    to_token_major(k, PS, BCT, XBCs[:, 16:24, :], n, 8)
    YS_ = A.alloc("qys", [128, 16, n])
    SB = [A.alloc("qsb", [128, 16, 128]) for _ in range(2)]
    TQ = A.alloc("qtq", [128, 16, 128])
    XSEL = [A.alloc("qxsel", [n, 1024]) for _ in range(2)]
    for b in range(n):
        S_ = SB[b % 2]
        xsel = XSEL[b % 2]
        k.dma("sp", S_[:], dr["ss"][b].rearrange("h p s -> (h p) s").rearrange("(c q) s -> q c s", q=128))
        k.ts(xsel[:], BCT[:], IDENT[0:n, b:b + 1], None, ALU.mult)
        psb = PS.next()
        for nb in range(2):
            k.mm(psb[:, nb * 512:(nb + 1) * 512], ONES[0:n, :], xsel[:, nb * 512:(nb + 1) * 512])
        S4 = S_[:].rearrange("p (g c) s -> p g c s", g=4)
        T4 = TQ[:].rearrange("p (g c) s -> p g c s", g=4)
        Bbc = psb[:, 0:512].rearrange("p (g s) -> p g s", g=4).unsqueeze(2).to_broadcast([128, 4, 4, 128])
        Cbc = psb[:, 512:1024].rearrange("p (g s) -> p g s", g=4).unsqueeze(2).to_broadcast([128, 4, 4, 128])
        xdt_bc = XDT[:, :, b].rearrange("p (g c) -> p g c", g=4).unsqueeze(3).to_broadcast([128, 4, 4, 128])
        da_bc = FX[:, 1, :, b].rearrange("p (g c) -> p g c", g=4).unsqueeze(3).to_broadcast([128, 4, 4, 128])
        k.tt(T4, Bbc, xdt_bc, ALU.mult, r=[psb.name, XDT.name], w=[TQ.name])
        k.tt(S4, S4, da_bc, ALU.mult, r=[S_.name, FX.name], w=[S_.name])
        k.tt(S_[:], S_[:], TQ[:], ALU.add)
        k.dma("sp", dr["sss"][b].rearrange("h p s -> (h p) s").rearrange("(c q) s -> q c s", q=128), S_[:])
        k.tt(T4, S4, Cbc, ALU.mult, r=[S_.name, psb.name], w=[TQ.name])
        k.red(YS_[:, :, b], TQ[:])
    k.tt(XDT[:], XBCs[:, 0:16, :], DV[:].unsqueeze(2).to_broadcast([128, 16, n]), ALU.mult)
    k.tt(YS_[:], YS_[:], XDT[:], ALU.add)
    k.dma("pool", dr["ysc"][NT][:, 0:16 * n], YS_[:].rearrange("p c b -> p (c b)"))


def ssd_gate(C):
    nc, k, P, A, PS, dr = C.nc, C.k, C.P, C.A, C.PS, C.dr
    ONES = C.cst["ONES"]
    P.barrier()
    m_phase = A.mark()
    WZ = A.alloc("swz", [128, 8, 2048], BF16)
    WOUT = A.alloc("swout", [128, 16, 1024], BF16)
    load_weight_bf16(k, WZ, dr["ssd_w_in"][:, 0:2048], 8, 2048)
    load_weight_bf16(k, WOUT, dr["ssd_w_out"], 16, 1024)
    LNG = A.alloc("l1g", [128, 1024])
    LNB_ = A.alloc("l1b", [128, 1024])
    k.dma("sp", LNG[:], dr["ln1_g"][1:2, :].partition_broadcast(128))
    k.dma("sp", LNB_[:], dr["ln1_b"][1:2, :].partition_broadcast(128))
    stats = A.alloc("stats", [128, 2, 6])
    mv = A.alloc("mv", [128, 2])
    rstd = A.alloc("rstd", [128, 1])
    NWT = A.alloc("snwt", [16, 128])
    NW = A.alloc("snw", [128, 1, 16])
    k.dma("sp", NWT[:], dr["ssd_norm_w"].rearrange("o (c q) -> (o c) q", q=128))
    to_feature_major(k, PS, NW, NWT, 16, 1, evac="dve")
    NSL = 3
    XTOK = [A.alloc("gxtok", [128, 1024]) for _ in range(NSL)]
    XT_ = [A.alloc("gxT", [128, 8, 128], BF16) for _ in range(NSL)]
    YT_ = [A.alloc("gyt", [128, 16, 128]) for _ in range(NSL)]
    ZT_ = [A.alloc("gzt", [128, 16, 128]) for _ in range(NSL)]
    GSQ_ = [A.alloc("ggsq", [128, 16, 128]) for _ in range(NSL)]
    k.reg32(*GSQ_)
    RS_ = [A.alloc("grs", [128, 4, 128]) for _ in range(NSL)]
    YN_ = [A.alloc("gyn", [128, 16, 128], BF16) for _ in range(NSL)]
    RES_ = [A.alloc("gres", [128, 1024]) for _ in range(NSL)]
    pst = C.PS.t + [C.psO]
    PSR = SlotRot([pst[i][:, j * 512:(j + 1) * 512] for i in (0, 1, 2) for j in (0, 1)])
    psYt = pst[3]
    tiles = [(t, 128) for t in range(C.n_prompt_tiles)] + [(NT, NS)]

    def tile_gen(ti, t, n):
        sl = ti % NSL
        xtok, yt, XT, ZT, GSQ, RS, YN, RES = XTOK[sl], YT_[sl], XT_[sl], ZT_[sl], GSQ_[sl], RS_[sl], YN_[sl], RES_[sl]
        r0 = t * 128
        k.dma("sp", xtok[0:n, :], dr["x2"][r0:r0 + n, :])
        k.dma("sp", yt[:, :, 0:n], dr["ysc"][t][:, 0:16 * n].rearrange("p (c j) -> p c j", j=n))
        yield
        to_feature_major(k, PSR, XT[:, :, 0:n], xtok, n, 8)
        yield

        def evac(i, grp, ps, view):
            k.act(ZT[:, grp[0]:grp[0] + len(grp), 0:n], view, AF.Silu)
        for g0 in range(0, 16, 4):
            proj_feature_major(k, PSR, WZ, WZ.name, XT, n, list(range(g0, g0 + 4)), evac)
            yield
        k.tt(yt[:, :, 0:n], yt[:, :, 0:n], ZT[:, :, 0:n], ALU.mult)
        k.act(GSQ[:, :, 0:n], yt[:, :, 0:n], AF.Square)
        yield
        pss = PSR.next()
        for g in range(4):
            for cl in range(4):
                k.mm(pss[:, g * 128:g * 128 + n], ONES[:], GSQ[:, 4 * g + cl, 0:n], start=(cl == 0), stop=(cl == 3))
        pv = pss.rearrange("p (g j) -> p g j", g=4)[:, :, 0:n]
        k.act(RS[:, :, 0:n], pv, AF.Ln, bias=C.eps6[:], scale=1.0 / 512.0)
        k.act(RS[:, :, 0:n], RS[:, :, 0:n], AF.Exp, scale=-0.5)
        yield
        y4 = yt[:, :, 0:n].rearrange("p (g c) j -> p g c j", g=4)
        k.tt(y4, y4, RS[:, :, 0:n].unsqueeze(2).to_broadcast([128, 4, 4, n]), ALU.mult, r=[yt.name, RS.name], w=[yt.name])
        k.tt(YN[:, :, 0:n], yt[:, :, 0:n], NW[:, 0, :].unsqueeze(2).to_broadcast([128, 16, n]), ALU.mult)
        yield
        for nb in range(2):
            for c in range(16):
                k.mm(psYt[0:n, nb * 512:(nb + 1) * 512], YN[:, c, 0:n], WOUT[:, c, nb * 512:(nb + 1) * 512],
                     start=(c == 0), stop=(c == 15), r=[YN.name, (WOUT.name, c, 0)],
                     w=PK(psYt[0:n, nb * 512:(nb + 1) * 512]))
        k.stt(RES[0:n, :], xtok[0:n, :], ALPHA, psYt[0:n, :], ALU.mult, ALU.add)
        yield
        layer_norm_out(k, A, RES, n, LNG, LNB_, RES, stats, mv, rstd)
        k.dma("pool", dr["x3"][r0:r0 + n, :], RES[0:n, :])

    active = []
    pend = [tile_gen(ti, t, n) for ti, (t, n) in enumerate(tiles)]
    while active or pend:
        if pend and len(active) < NSL:
            active.append(pend.pop(0))
        keep = []
        for g_ in active:
            try:
                next(g_)
                keep.append(g_)
            except StopIteration:
                pass
        active = keep
    P.barrier()
    A.reset(m_phase)
```
